# Optimizing a Trainium2 kernel written in Bass

```python
import math
import jax
import jax.numpy as jnp
from jax import lax
import numpy as np

D_MODEL = 2048
BATCH = 2
SEQ = 4096
DEPTH = 1

CTX_LEN = 256
GRID_W = 64
D_MIX = D_MODEL
D_SSD = D_MIX // 2
D_HYENA = D_MIX - D_SSD
SSD_HEAD_DIM = 64
SSD_HEADS = D_SSD // SSD_HEAD_DIM
SSD_GROUPS = 2
SSD_HPG = SSD_HEADS // SSD_GROUPS
SSD_STATE = 128
SSD_CONV = 3
SSD_CHUNK = 128
D_XBC = D_SSD + 2 * SSD_GROUPS * SSD_STATE
HY_SHORT = 3
HY_EMB = 33
HY_BANDS = (HY_EMB - 1) // 2
HY_ORDER = 64
HY_TARGET = 1e-2
HY_FAST_PCT = 0.3
HY_SLOW_PCT = 1.5
D_FF = -(-8 * D_MODEL // (3 * 256)) * 256
N_IN = D_SSD + D_XBC + 2 * SSD_HEADS + 3 * D_HYENA
RMS_EPS = 1e-6
POS_THETA = 10000.0

kernel_name = 'hymba_ssd_hyena_diffusion_block'


def rmsnorm(x, w):
    xf = x.astype(jnp.float32)
    r = lax.rsqrt(jnp.mean(xf * xf, axis=-1, keepdims=True) + RMS_EPS)
    return (xf * r).astype(x.dtype) * w


def modulate(x, w, shift, scale):
    return rmsnorm(x, w) * (1 + scale) + shift


def flip_seq(t):
    return t[:, ::-1]


def dwconv_centred(u, w, b):
    k_w = w.shape[0]
    pad = k_w // 2
    L = u.shape[1]
    up = jnp.pad(u, ((0, 0), (pad, pad), (0, 0)))
    out = up[:, 0:L] * w[0]
    for k in range(1, k_w):
        out = out + up[:, k:k + L] * w[k]
    return out + b


def sincos_pos_2d(rows, cols, dim):
    q = dim // 4
    omega = 1.0 / (POS_THETA ** (jnp.arange(q, dtype=jnp.float32) / q))
    r = jnp.arange(rows, dtype=jnp.float32)[:, None] * omega
    cc = jnp.arange(cols, dtype=jnp.float32)[:, None] * omega
    r_emb = jnp.concatenate([jnp.sin(r), jnp.cos(r)], -1)
    c_emb = jnp.concatenate([jnp.sin(cc), jnp.cos(cc)], -1)
    emb = jnp.concatenate([jnp.broadcast_to(r_emb[:, None], (rows, cols, 2 * q)),
                           jnp.broadcast_to(c_emb[None], (rows, cols, 2 * q))], -1)
    return emb.reshape(rows * cols, 4 * q)


def segsum(a):
    T = a.shape[-1]
    ar = jnp.broadcast_to(a[..., None], a.shape + (T,))
    strict = jnp.tril(jnp.ones((T, T), dtype=bool), -1)
    cs = jnp.cumsum(jnp.where(strict, ar, 0.0), axis=-2)
    return jnp.where(jnp.tril(jnp.ones((T, T), dtype=bool), 0), cs, -jnp.inf)


def ssd_chunked(x, dt, A, B, C, h0, return_y):
    nb, nl, ng, nr, hp = x.shape
    nc = nl // SSD_CHUNK
    xf = x.astype(jnp.float32).reshape(nb, nc, SSD_CHUNK, ng, nr, hp)
    dtc = dt.reshape(nb, nc, SSD_CHUNK, ng, nr)
    Bf = B.astype(jnp.float32).reshape(nb, nc, SSD_CHUNK, ng, -1)
    Cf = C.astype(jnp.float32).reshape(nb, nc, SSD_CHUNK, ng, -1)
    xdt = xf * dtc[..., None]
    a = jnp.transpose(dtc * A, (0, 3, 4, 1, 2))
    a_cs = jnp.cumsum(a, axis=-1)
    decay_states = jnp.exp(a_cs[..., -1:] - a_cs)
    states = jnp.einsum('bcsgn,bgrcs,bcsgrp->bcgrpn', Bf, decay_states, xdt)
    states = jnp.concatenate([h0[:, None].astype(jnp.float32), states], axis=1)
    chunk_a = jnp.pad(a_cs[..., -1], ((0, 0), (0, 0), (0, 0), (1, 0)))
    decay_chunk = jnp.exp(segsum(chunk_a))
    new_states = jnp.einsum('bgrzc,bcgrpn->bzgrpn', decay_chunk, states)
    final = new_states[:, -1]
    if not return_y:
        return None, final
    Lmat = jnp.exp(segsum(a))
    scores = jnp.einsum('bclgn,bcsgn->bgcls', Cf, Bf)
    y_diag = jnp.einsum('bgcls,bgrcls,bcsgrp->bclgrp', scores, Lmat, xdt)
    y_off = jnp.einsum('bclgn,bcgrpn,bgrcl->bclgrp', Cf, new_states[:, :-1], jnp.exp(a_cs))
    y = (y_diag + y_off).reshape(nb, nl, ng, nr, hp).astype(x.dtype)
    return y, final


def ssd_mixer(z, xbc, dt_raw, zc, xbc_c, dt_raw_c, p, need_ctx):
    def prep(u, dr):
        u = jax.nn.silu(dwconv_centred(u, p['ssd_conv_w'], p['ssd_conv_b']))
        nb, nl = u.shape[:2]
        xs, Bm, Cm = jnp.split(u, [D_SSD, D_SSD + SSD_GROUPS * SSD_STATE], axis=-1)
        xs = xs.reshape(nb, nl, SSD_GROUPS, SSD_HPG, SSD_HEAD_DIM)
        Bm = Bm.reshape(nb, nl, SSD_GROUPS, SSD_STATE)
        Cm = Cm.reshape(nb, nl, SSD_GROUPS, SSD_STATE)
        dt = jax.nn.softplus(dr.astype(jnp.float32).reshape(nb, nl, 2, SSD_GROUPS, SSD_HPG)
                             + p['ssd_dt_bias'].astype(jnp.float32).reshape(2, SSD_GROUPS, SSD_HPG))
        return xs, Bm, Cm, dt

    A = -jnp.exp(p['ssd_a_log'].astype(jnp.float32)).reshape(2, SSD_GROUPS, SSD_HPG)
    Dskip = p['ssd_d'].reshape(SSD_GROUPS, SSD_HPG)[..., None]
    xs, Bm, Cm, dt = prep(xbc, dt_raw)
    xc, Bc, Cc, dtc = prep(xbc_c, dt_raw_c)
    h0 = jnp.zeros((xs.shape[0], SSD_GROUPS, SSD_HPG, SSD_HEAD_DIM, SSD_STATE), jnp.float32)
    yc_f, s_f = ssd_chunked(xc, dtc[:, :, 0], A[0], Bc, Cc, h0, need_ctx)
    yc_b, s_b = ssd_chunked(flip_seq(xc), flip_seq(dtc[:, :, 1]), A[1], flip_seq(Bc), flip_seq(Cc), h0, need_ctx)
    y_f, _ = ssd_chunked(xs, dt[:, :, 0], A[0], Bm, Cm, s_f, True)
    y_b, _ = ssd_chunked(flip_seq(xs), flip_seq(dt[:, :, 1]), A[1], flip_seq(Bm), flip_seq(Cm), s_b, True)

    def finish(yf, yb, xs_, z_):
        y = (yf + flip_seq(yb) + Dskip * xs_).reshape(z_.shape)
        return rmsnorm(y * jax.nn.silu(z_), p['ssd_norm'])

    y = finish(y_f, y_b, xs, z)
    yc = finish(yc_f, yc_b, xc, zc) if need_ctx else None
    return y, yc


def hyena_filters(L, p):
    f32 = jnp.float32
    t = jnp.linspace(0.0, 1.0, L, dtype=f32)[:, None]
    w = 2.0 * math.pi * jnp.arange(L, dtype=f32)[:, None] / L
    fb = jnp.linspace(1e-4, HY_BANDS - 1, HY_BANDS, dtype=f32)[None]
    zpos = jnp.concatenate([t, jnp.cos(fb * w), -jnp.sin(fb * w)], -1)
    fr = p['hy_freq'].astype(f32)
    h = jnp.sin(fr * (zpos @ p['hy_w1'].astype(f32) + p['hy_b1'].astype(f32)))
    h = jnp.sin(fr * (h @ p['hy_w2'].astype(f32) + p['hy_b2'].astype(f32)))
    h = jnp.sin(fr * (h @ p['hy_w3'].astype(f32) + p['hy_b3'].astype(f32)))
    h = (h @ p['hy_w4'].astype(f32)).reshape(L, 2, D_HYENA)
    max_decay = math.log(HY_TARGET) / HY_FAST_PCT
    min_decay = math.log(HY_TARGET) / HY_SLOW_PCT
    deltas = jnp.abs(jnp.linspace(min_decay, max_decay, D_HYENA, dtype=f32))
    h = h * jnp.exp(-t[:, :, None] * deltas)
    return h / (jnp.sum(jnp.abs(h), axis=(0, 1), keepdims=True) + 1e-6)


def long_conv_bidir(u, h, bias):
    L, C = u.shape[1], u.shape[2]
    k = jnp.concatenate([h[:, 0], jnp.zeros((1, C), jnp.float32), h[:0:-1, 1]], axis=0)
    uf = u.astype(jnp.float32)
    U = jnp.fft.rfft(uf, n=2 * L, axis=1)
    Kf = jnp.fft.rfft(k, n=2 * L, axis=0)
    y = jnp.fft.irfft(U * Kf[None], n=2 * L, axis=1)[:, :L]
    return (y + uf * bias.astype(jnp.float32)).astype(u.dtype)


def hyena_mixer(u, p):
    L = u.shape[1]
    u = dwconv_centred(u, p['hy_conv_w'], p['hy_conv_b'])
    x0, x1, v = jnp.split(u, 3, axis=-1)
    v = long_conv_bidir(v * x1, hyena_filters(L, p), p['hy_bias'])
    return x0 * v


def token_mix(h, hc, p, need_ctx):
    cuts = [D_SSD, D_SSD + D_XBC, D_SSD + D_XBC + 2 * SSD_HEADS]
    z, xbc, dtr, hy = jnp.split(h @ p['w_in'], cuts, axis=-1)
    if need_ctx:
        zc, xbcc, dtrc, hyc = jnp.split(hc @ p['w_in'], cuts, axis=-1)
    else:
        xbcc, dtrc = jnp.split(hc @ p['w_in'][:, cuts[0]:cuts[2]], [D_XBC], axis=-1)
        zc, hyc = None, None
    y_ssd, yc_ssd = ssd_mixer(z, xbc, dtr, zc, xbcc, dtrc, p, need_ctx)
    y = jnp.concatenate([y_ssd, hyena_mixer(hy, p)], axis=-1) @ p['w_out']
    if not need_ctx:
        return y, None
    yc = jnp.concatenate([yc_ssd, hyena_mixer(hyc, p)], axis=-1) @ p['w_out']
    return y, yc


def swiglu(h, wg, wu, wd):
    return (jax.nn.silu(h @ wg) * (h @ wu)) @ wd


def setup_inputs(seed: int = 0) -> dict:
    key = jax.random.key(seed)
    ks = jax.random.split(key, 32)
    f32 = jnp.float32

    def nrm(k, shape, scale=1.0):
        return jax.random.normal(k, shape, f32) * scale

    def gain(k, shape):
        return 1.0 + 0.02 * jax.random.normal(k, shape, f32)

    dt0 = jnp.exp(jax.random.uniform(ks[13], (DEPTH, 2, SSD_HEADS), f32, math.log(1e-3), math.log(1e-1)))
    return {
        'x': nrm(ks[0], (BATCH, SEQ, D_MODEL)),
        'c': nrm(ks[1], (BATCH, D_MODEL)),
        'ctx': nrm(ks[2], (BATCH, CTX_LEN, D_MODEL)),
        'c_ctx': nrm(ks[3], (D_MODEL,)),
        'w_ada': nrm(ks[4], (DEPTH, D_MODEL, 6 * D_MODEL), D_MODEL ** -0.5),
        'b_ada': nrm(ks[5], (DEPTH, 6 * D_MODEL), 0.02),
        'norm_mix_pre': gain(ks[6], (DEPTH, D_MODEL)),
        'norm_mix_post': gain(ks[7], (DEPTH, D_MODEL)),
        'norm_ffn_pre': gain(ks[8], (DEPTH, D_MODEL)),
        'norm_ffn_post': gain(ks[9], (DEPTH, D_MODEL)),
        'w_in': nrm(ks[10], (DEPTH, D_MODEL, N_IN), D_MODEL ** -0.5),
        'ssd_conv_w': nrm(ks[11], (DEPTH, SSD_CONV, D_XBC), SSD_CONV ** -0.5),
        'ssd_conv_b': nrm(ks[12], (DEPTH, D_XBC), 0.02),
        'ssd_a_log': jnp.log(jax.random.uniform(ks[14], (DEPTH, 2, SSD_HEADS), f32, 1.0, 16.0)),
        'ssd_dt_bias': dt0 + jnp.log(-jnp.expm1(-dt0)),
        'ssd_d': gain(ks[15], (DEPTH, SSD_HEADS)),
        'ssd_norm': gain(ks[16], (DEPTH, D_SSD)),
        'hy_conv_w': nrm(ks[17], (DEPTH, HY_SHORT, 3 * D_HYENA), HY_SHORT ** -0.5),
        'hy_conv_b': nrm(ks[18], (DEPTH, 3 * D_HYENA), 0.02),
        'hy_w1': nrm(ks[19], (DEPTH, HY_EMB, HY_ORDER), HY_EMB ** -0.5),
        'hy_b1': nrm(ks[20], (DEPTH, HY_ORDER), 0.02),
        'hy_w2': nrm(ks[21], (DEPTH, HY_ORDER, HY_ORDER), HY_ORDER ** -0.5),
        'hy_b2': nrm(ks[22], (DEPTH, HY_ORDER), 0.02),
        'hy_w3': nrm(ks[23], (DEPTH, HY_ORDER, HY_ORDER), HY_ORDER ** -0.5),
        'hy_b3': nrm(ks[24], (DEPTH, HY_ORDER), 0.02),
        'hy_w4': nrm(ks[25], (DEPTH, HY_ORDER, 2 * D_HYENA), HY_ORDER ** -0.5),
        'hy_freq': gain(ks[26], (DEPTH, HY_ORDER)),
        'hy_bias': nrm(ks[27], (DEPTH, D_HYENA)),
        'w_out': nrm(ks[28], (DEPTH, D_MIX, D_MODEL), D_MIX ** -0.5),
        'w_gate': nrm(ks[29], (DEPTH, D_MODEL, D_FF), D_MODEL ** -0.5),
        'w_up': nrm(ks[30], (DEPTH, D_MODEL, D_FF), D_MODEL ** -0.5),
        'w_down': nrm(ks[31], (DEPTH, D_FF, D_MODEL), D_FF ** -0.5),
    }


def reference(x, c, ctx, c_ctx, w_ada, b_ada, norm_mix_pre, norm_mix_post, norm_ffn_pre, norm_ffn_post,
              w_in, ssd_conv_w, ssd_conv_b, ssd_a_log, ssd_dt_bias, ssd_d, ssd_norm,
              hy_conv_w, hy_conv_b, hy_w1, hy_b1, hy_w2, hy_b2, hy_w3, hy_b3, hy_w4, hy_freq, hy_bias,
              w_out, w_gate, w_up, w_down):
    L = x.shape[1]
    ROWS = L // GRID_W
    x = x + sincos_pos_2d(ROWS, GRID_W, D_MODEL).astype(x.dtype)[None]
    xc = ctx
    for layer in range(DEPTH):
        need_ctx = layer < DEPTH - 1
        p = {
            'w_in': w_in[layer], 'w_out': w_out[layer],
            'ssd_conv_w': ssd_conv_w[layer], 'ssd_conv_b': ssd_conv_b[layer],
            'ssd_a_log': ssd_a_log[layer], 'ssd_dt_bias': ssd_dt_bias[layer],
            'ssd_d': ssd_d[layer], 'ssd_norm': ssd_norm[layer],
            'hy_conv_w': hy_conv_w[layer], 'hy_conv_b': hy_conv_b[layer],
            'hy_w1': hy_w1[layer], 'hy_b1': hy_b1[layer], 'hy_w2': hy_w2[layer], 'hy_b2': hy_b2[layer],
            'hy_w3': hy_w3[layer], 'hy_b3': hy_b3[layer], 'hy_w4': hy_w4[layer],
            'hy_freq': hy_freq[layer], 'hy_bias': hy_bias[layer],
        }
        mod = jax.nn.silu(c) @ w_ada[layer] + b_ada[layer]
        sh1, sc1, g1, sh2, sc2, g2 = jnp.split(mod[:, None, :], 6, axis=-1)
        mod_c = jax.nn.silu(c_ctx) @ w_ada[layer] + b_ada[layer]
        csh1, csc1, cg1, csh2, csc2, cg2 = jnp.split(mod_c, 6, axis=-1)
        h = modulate(x, norm_mix_pre[layer], sh1, sc1)
        hc = modulate(xc, norm_mix_pre[layer], csh1, csc1)
        y, yc = token_mix(h, hc, p, need_ctx)
        x = x + g1 * rmsnorm(y, norm_mix_post[layer])
        h = modulate(x, norm_ffn_pre[layer], sh2, sc2)
        x = x + g2 * rmsnorm(swiglu(h, w_gate[layer], w_up[layer], w_down[layer]), norm_ffn_post[layer])
        if need_ctx:
            xc = xc + cg1 * rmsnorm(yc, norm_mix_post[layer])
            hc = modulate(xc, norm_ffn_pre[layer], csh2, csc2)
            xc = xc + cg2 * rmsnorm(swiglu(hc, w_gate[layer], w_up[layer], w_down[layer]), norm_ffn_post[layer])
    return x
```

```python
import math
import numpy as np
import ml_dtypes
import concourse.bass as bass
import concourse.mybir as mybir
from concourse.bass_utils import run_bass_kernel_spmd

F32 = mybir.dt.float32
BF16 = mybir.dt.bfloat16
I32 = mybir.dt.int32
AF = mybir.ActivationFunctionType
ALU = mybir.AluOpType
AX = mybir.AxisListType

D = 2048
L = 4096
LC = 256
NCH = 16
DFF = 5632
NFF = 44
NDFT = 8192
KT = 33
EPS = 1e-6
NEG = -30000.0
TWO_PI = 2.0 * math.pi

DEBUG = {}


class Prog:
    def __init__(self, nc):
        self.nc = nc
        self.engs = {"pe": nc.tensor, "act": nc.scalar, "dve": nc.vector, "pool": nc.gpsimd, "sp": nc.sync}
        self.q = {k: [] for k in self.engs}
        self.sems = {}
        self.cnt = {}
        self.seen = {k: {} for k in self.engs}
        self.W = {}
        self.R = {}
        self.free = []
        self.pcount = {}
        self.phys = {}
        self.live = []
        self.gen = 0
        for k in ("pe", "act", "dve", "pool"):
            self._sem("e_" + k)

    def _sem(self, key):
        if key not in self.sems:
            if key.startswith("e_") or key.startswith("c_") or key.startswith("q_") or not self.free:
                phys = self.nc.alloc_semaphore("s%d" % len(self.pcount))
                self.pcount[id(phys)] = 0
                self.phys[id(phys)] = phys
            else:
                phys = self.free.pop()
            self.sems[key] = phys
            self.cnt[key] = self.pcount[id(phys)]
            if not (key.startswith("e_") or key.startswith("c_") or key.startswith("q_")):
                self.live.append(key)
        return self.sems[key]

    def _bump(self, key, inc):
        self.cnt[key] += inc
        self.pcount[id(self.sems[key])] = self.cnt[key]
        return self.cnt[key]

    def _deps(self, eng, r, w):
        need = {}
        for b in r:
            for sk, v in self.W.get(b, {}).items():
                need[sk] = max(need.get(sk, 0), v)
        for b in w:
            for sk, v in self.W.get(b, {}).items():
                need[sk] = max(need.get(sk, 0), v)
            for sk, v in self.R.get(b, {}).items():
                need[sk] = max(need.get(sk, 0), v)
        waits = []
        for sk, v in need.items():
            if eng == "pe" and sk == "e_pe":
                continue
            if self.seen[eng].get(sk, 0) < v:
                self.seen[eng][sk] = v
                waits.append((sk, v))
        return waits

    def _mark(self, sk, v, r, w):
        for b in r:
            d = self.R.setdefault(b, {})
            d[sk] = max(d.get(sk, 0), v)
        for b in w:
            d = self.W.setdefault(b, {})
            d[sk] = max(d.get(sk, 0), v)

    def op(self, eng, fn, r=(), w=()):
        waits = self._deps(eng, r, w)
        sk = "e_" + eng
        v = self._bump(sk, 1)
        self._emit(eng, waits, fn, sk, 1)
        self._mark(sk, v, r, w)

    def _emit(self, eng, waits, fn, sk, inc):
        e = self.engs[eng]
        for wsk, v in waits:
            e.wait_ge(self.sems[wsk], v)
        if fn is not None:
            fn(e).then_inc(self.sems[sk], inc)

    def fence(self):
        for eng in self.engs:
            waits = []
            for sk, v in self.cnt.items():
                if sk.startswith("c_"):
                    continue
                if v > 0 and self.seen[eng].get(sk, 0) < v and not (sk == "e_" + eng):
                    self.seen[eng][sk] = v
                    waits.append((sk, v))
            self._emit(eng, waits, None, None, 0)
        for key in self.live:
            self.free.append(self.sems[key])
        self.live = []
        self.gen += 1

    def dma(self, eng, out, in_, r=(), w=(), sk=None, **kw):
        waits = self._deps(eng, r, w)
        base = sk if sk is not None else (w[0] if w else r[0])
        sk = ("q_" + base) if eng == "pool" else ("d_" + base + "@%d" % self.gen)
        self._sem(sk)
        v = self._bump(sk, 16)
        self._emit(eng, waits, (lambda e: e.dma_start(out=out, in_=in_, **kw)), sk, 16)
        self._mark(sk, v, r, w)

    def gather(self, out, in_, idx, r=(), w=(), sk=None):
        waits = self._deps("pool", r, w)
        sk = "q_" + (sk if sk is not None else w[0])
        self._sem(sk)
        v = self._bump(sk, 16)
        self._emit("pool", waits, (lambda e: e.indirect_dma_start(
            out=out, out_offset=None, in_=in_, in_offset=bass.IndirectOffsetOnAxis(ap=idx, axis=0))), sk, 16)
        self._mark(sk, v, r, w)

    def coll(self, kind, groups, in_, out, r=(), w=(), sk="coll"):
        waits = self._deps("pool", r, w)
        sk = "c_" + sk + "@%d" % self.gen
        self._sem(sk)
        v = self._bump(sk, 1)
        self._emit("pool", waits, (lambda e: e.collective_compute(kind, ALU.bypass, replica_groups=groups,
                                                                   ins=[in_], outs=[out])), sk, 1)
        self._mark(sk, v, r, w)

    def final_wait(self, eng, bufs):
        waits = self._deps(eng, bufs, bufs)
        self._emit(eng, waits, None, None, 0)

    def emit(self):
        pass


def bc_last(ap, n):
    return ap.unsqueeze(2).to_broadcast([ap.shape[0], ap.shape[1], n])


def build(stop="all"):
    nc = bass.Bass("TRN2", target_bir_lowering=False)
    P = Prog(nc)
    GRP = [[0, 1, 2, 3], [4, 5, 6, 7]]

    in_names = []

    def din(name, shape, dt=F32):
        in_names.append(name)
        return nc.dram_tensor(name, list(shape), dt, kind="ExternalInput").ap()

    def dout(name, shape, dt=F32):
        return nc.dram_tensor(name, list(shape), dt, kind="ExternalOutput").ap()

    def dscr(name, shape, dt=F32):
        return nc.dram_tensor(name, list(shape), dt).ap()

    x_b = din("x_b", [L, D])
    ctx_b = din("ctx_b", [LC, D])
    cv = din("cv", [128, NCH, 2])
    w_ada_q = din("w_ada_q", [D, 3072])
    b_ada_q = din("b_ada_q", [1, 3072])
    nrm_pre1 = din("nrm_pre1", [128, NCH])
    nrm_pre2 = din("nrm_pre2", [128, NCH])
    nrm_post1 = din("nrm_post1", [1, D])
    nrm_post2 = din("nrm_post2", [1, D])
    c_ident = din("c_ident", [128, 128])
    c_jrow = din("c_jrow", [1, 512])
    c_kcol = din("c_kcol", [128, 1])
    w_in_q = din("w_in_q", [D, 1288])
    ssd_cw = din("ssd_cw", [128, 4, 3])
    ssd_cb = din("ssd_cb", [128, 4])
    ssd_alog = din("ssd_alog", [1, 8])
    ssd_dtb = din("ssd_dtb", [1, 8])
    ssd_Dq = din("ssd_Dq", [1, 4])
    gidx_in = din("gidx", [128, 56], I32)
    w_z = din("w_z", [D, 1024])
    ssd_nrm = din("ssd_nrm", [1, 1024])
    w_out = din("w_out", [D, D])
    w_gate = din("w_gate", [D, DFF])
    w_up = din("w_up", [D, DFF])
    w_down = din("w_down", [DFF, D])
    hy_cw = din("hy_cw", [128, 6, 3])
    hy_cb = din("hy_cb", [128, 6])
    hy_w1 = din("hy_w1", [33, 64]); hy_w2 = din("hy_w2", [64, 64]); hy_w3 = din("hy_w3", [64, 64])
    hy_b123 = din("hy_b123", [64, 3]); hy_fr = din("hy_fr", [64, 1])
    hy_w4q = din("hy_w4q", [64, 512])
    hy_biasq = din("hy_biasq", [128, 2])
    c_delta = din("c_delta", [128, 2])
    c_irow = din("c_irow", [1, L])
    c_fbph = din("c_fbph", [33, 2])
    c_wk = din("c_wk", [128, KT])
    t2cos = din("t2cos", [KT, 128, KT, 128], BF16)
    t2sin = din("t2sin", [KT, 128, KT, 128], BF16)
    c_trif = din("c_trif", [128, 128])
    c_trib = din("c_trib", [128, 128])
    c_negf = din("c_negf", [128, 128])
    c_negb = din("c_negb", [128, 128])
    out_own = dout("out_own", [1024, D])

    dbg = {}

    def dbg_out(name, shape, dt=F32):
        dbg[name] = dout("dbg_" + name, shape, dt)
        return dbg[name]

    POS = dscr("POS", [64, 1024])
    MODIN = dscr("MODIN", [2, 3072])
    MODG = dscr("MODG", [8, 3072])
    XP = dscr("XP", [L, D])
    U = dscr("U", [1288, L])
    UC = dscr("UC", [520, LC])
    YS = dscr("YS", [L, 256])
    GS = dscr("GS", [4 * L, 256])
    X1S = dscr("X1S", [1024, D])
    FSD = dscr("FSD", [L, 256], BF16)
    FDD = dscr("FDD", [L, 256], BF16)
    X0S = dscr("X0S", [256, L])
    VXS = dscr("VXS", [256, L])
    YH = dscr("YH", [4 * 256, 1024], BF16)
    GH = dscr("GH", [16 * 256, 1024], BF16)

    import contextlib
    es = contextlib.ExitStack()
    scopes = [es]

    class Scope:
        def __enter__(self_):
            st = contextlib.ExitStack()
            st.__enter__()
            scopes.append(st)
            return st

        def __exit__(self_, *a):
            P.fence()
            st = scopes.pop()
            st.__exit__(None, None, None)
            return False

    with es:
        def sb(name, shape, dt=F32):
            return scopes[-1].enter_context(nc.sbuf_tensor(name, list(shape), dt))

        def ps(name, shape, dt=F32):
            return scopes[-1].enter_context(nc.psum_tensor(name, list(shape), dt))

        def range_reduce_sin(eng_out, argt, kft, kit, npart, ncol, keys):
            ka, kf_, ki_ = keys
            P.op("dve", lambda e: e.tensor_scalar(out=kft[0:npart, 0:ncol], in0=argt[0:npart, 0:ncol], scalar1=1.0 / TWO_PI,
                                                  scalar2=None, op0=ALU.mult), r=[ka], w=[kf_])
            P.op("dve", lambda e: e.tensor_copy(out=kit[0:npart, 0:ncol], in_=kft[0:npart, 0:ncol]), r=[kf_], w=[ki_])
            P.op("dve", lambda e: e.tensor_copy(out=kft[0:npart, 0:ncol], in_=kit[0:npart, 0:ncol]), r=[ki_], w=[kf_])
            P.op("dve", lambda e: e.scalar_tensor_tensor(out=argt[0:npart, 0:ncol], in0=kft[0:npart, 0:ncol], scalar=-TWO_PI,
                                                         in1=argt[0:npart, 0:ncol], op0=ALU.mult, op1=ALU.add),
                 r=[kf_, ka], w=[ka])
            P.op("dve", lambda e: e.tensor_scalar(out=argt[0:npart, 0:ncol], in0=argt[0:npart, 0:ncol], scalar1=-3.141592,
                                                  scalar2=3.141592, op0=ALU.max, op1=ALU.min), r=[ka], w=[ka])
            P.op("act", lambda e: e.activation(out=eng_out, in_=argt[0:npart, 0:ncol], func=AF.Sin), r=[ka], w=["sinout"])

        tm_state = {"ptr": None, "trc": {"n": 0}}

        def to_token_major(src_bf, skey, dsts):
            for g8 in range(4):
                k = tm_state["trc"]["n"] % 2
                tm_state["trc"]["n"] += 1
                pk = "ptr%d" % k
                for i8 in range(8):
                    tt = g8 * 8 + i8
                    P.op("pe", lambda e: e.transpose(out=tm_state["ptr"][k][:, i8, :], in_=src_bf[:, tt * 128:(tt + 1) * 128],
                                                     identity=identb[:]), r=[skey, "identb"], w=[pk])
                dt0, dk0, c00 = dsts[0]
                P.op("act", lambda e: e.copy(out=dt0[:, g8 * 8:(g8 + 1) * 8, c00:c00 + 128], in_=tm_state["ptr"][k][:]),
                     r=[pk], w=[dk0])
                for (dt_, dk_, c0) in dsts[1:]:
                    P.op("pool", lambda e: e.tensor_copy(out=dt_[:, g8 * 8:(g8 + 1) * 8, c0:c0 + 128],
                                                         in_=dt0[:, g8 * 8:(g8 + 1) * 8, c00:c00 + 128]),
                         r=[dk0], w=[dk_])


        def filter_gen():
            FSt = sb("FSt", [128, 32, 128], BF16)
            with Scope():
                h3T = sb("h3T", [64, L]); irow = sb("irow", [128, L])
                P.dma("sp", irow[:], c_irow.partition_broadcast(128), w=["irow"])
                w1 = sb("hw1", [33, 64]); w2 = sb("hw2", [64, 64]); w3 = sb("hw3", [64, 64]); w4 = sb("hw4", [64, 512])
                b123 = sb("hb123", [64, 3]); fr = sb("hfr", [64, 1]); frb = sb("hfrb", [64, 3]); fbph = sb("fbph", [33, 2])
                for t_, src in ((w1, hy_w1), (w2, hy_w2), (w3, hy_w3), (w4, hy_w4q), (b123, hy_b123), (fr, hy_fr), (fbph, c_fbph)):
                    P.dma("sp", t_[:], src[:, :], w=[t_.name])
                P.op("dve", lambda e: e.tensor_scalar(out=frb[:], in0=b123[:], scalar1=fr[:, 0:1], scalar2=None, op0=ALU.mult),
                     r=["hb123", "hfr"], w=["hfrb"])
                with Scope():
                    wk_arg = sb("wk_arg", [64, L]); wk_kf = sb("wk_kf", [64, L]); wk_ki = sb("wk_ki", [64, L], I32)
                    hA = sb("hA", [64, L])
                    keys = ("wk_arg", "wk_kf", "wk_ki")
                    P.op("dve", lambda e: e.tensor_scalar(out=wk_arg[0:33, :], in0=irow[0:33, :], scalar1=TWO_PI / L, scalar2=None,
                                                          op0=ALU.mult), r=["irow"], w=["wk_arg"])
                    P.op("dve", lambda e: e.tensor_scalar(out=wk_arg[0:33, :], in0=wk_arg[0:33, :], scalar1=fbph[:, 0:1],
                                                          scalar2=fbph[:, 1:2], op0=ALU.mult, op1=ALU.add),
                         r=["wk_arg", "fbph"], w=["wk_arg"])
                    range_reduce_sin(hA[0:33, :], wk_arg, wk_kf, wk_ki, 33, L, keys)
                    yield
                    P.op("dve", lambda e: e.tensor_scalar(out=hA[0:1, :], in0=irow[0:1, :], scalar1=1.0 / (L - 1), scalar2=None,
                                                          op0=ALU.mult), r=["irow", "sinout"], w=["sinout"])
                    lay = [(w1, 33, hA, h3T, 0), (w2, 64, h3T, hA, 1), (w3, 64, hA, h3T, 2)]
                    for (wt, kin, hin, hout, li) in lay:
                        for blk in range(8):
                            k = blk % 2
                            P.op("pe", lambda e: e.matmul(pf[k][0:64, :], lhsT=wt[0:kin, :], rhs=hin[0:kin, blk * 512:(blk + 1) * 512],
                                                          start=True, stop=True), r=[wt.name, "sinout"], w=["pf%d" % k])
                            P.op("dve", lambda e: e.tensor_scalar(out=wk_arg[:, blk * 512:(blk + 1) * 512], in0=pf[k][0:64, :],
                                                                  scalar1=fr[:, 0:1], scalar2=frb[:, li:li + 1],
                                                                  op0=ALU.mult, op1=ALU.add), r=["pf%d" % k, "hfr", "hfrb"], w=["wk_arg"])
                        range_reduce_sin(hout[:, :], wk_arg, wk_kf, wk_ki, 64, L, keys)
                        yield
                with Scope():
                    hf = sb("hf", [128, L]); hbk = sb("hbk", [128, L]); wexp = sb("wexp", [128, 512])
                    fsb = sb("fsb", [128, L], BF16); fdb = sb("fdb", [128, L], BF16)
                    dsc = sb("dsc", [128, 2]); nrm = sb("nrm", [128, 2]); rinv = sb("rinv", [128, 2]); junkf = sb("junkf", [128, L], BF16)
                    P.dma("sp", dsc[:], c_delta[:, :], w=["dsc"])
                    P.op("dve", lambda e: e.tensor_scalar(out=dsc[:], in0=dsc[:], scalar1=-1.0 / (L - 1), scalar2=None, op0=ALU.mult),
                         r=["dsc"], w=["dsc"])
                    for j in range(2):
                        for blk in range(8):
                            P.op("act", lambda e: e.activation(out=wexp[:], in_=irow[:, blk * 512:(blk + 1) * 512], func=AF.Exp,
                                                               scale=dsc[:, j:j + 1]), r=["irow", "dsc"], w=["wexp"])
                            for d_, dstt in ((0, hf), (1, hbk)):
                                k = d_
                                c0 = d_ * 256 + j * 128
                                P.op("pe", lambda e: e.matmul(pf[k][:], lhsT=w4[:, c0:c0 + 128], rhs=h3T[:, blk * 512:(blk + 1) * 512],
                                                              start=True, stop=True), r=["hw4", "sinout"], w=["pf%d" % k])
                                P.op("dve", lambda e: e.tensor_tensor(out=dstt[:, blk * 512:(blk + 1) * 512], in0=pf[k][:], in1=wexp[:],
                                                                      op=ALU.mult), r=["pf%d" % k, "wexp"], w=[dstt.name])
                        P.op("dve", lambda e: e.memset(nrm[:], 0.0), w=["nrm"])
                        P.op("act", lambda e: e.activation(out=junkf[:], in_=hf[:], func=AF.Abs, accum_out=nrm[:, 0:1]),
                             r=["hf"], w=["nrm", "junkf"])
                        P.op("act", lambda e: e.activation(out=junkf[:], in_=hbk[:], func=AF.Abs, accum_out=nrm[:, 1:2]),
                             r=["hbk"], w=["nrm", "junkf"])
                        P.op("dve", lambda e: e.scalar_tensor_tensor(out=rinv[:, 0:1], in0=nrm[:, 0:1], scalar=1e-6, in1=nrm[:, 1:2],
                                                                     op0=ALU.add, op1=ALU.add), r=["nrm"], w=["rinv"])
                        P.op("dve", lambda e: e.reciprocal(out=rinv[:, 0:1], in_=rinv[:, 0:1]), r=["rinv"], w=["rinv"])
                        P.op("dve", lambda e: e.tensor_scalar(out=rinv[:, 1:2], in0=rinv[:, 0:1], scalar1=-1.0, scalar2=None,
                                                              op0=ALU.mult), r=["rinv"], w=["rinv"])
                        P.op("pool", lambda e: e.memset(hbk[:, 0:1], 0.0), r=["nrm"], w=["hbk"])
                        P.op("act", lambda e: e.activation(out=hbk[:], in_=hbk[:], func=AF.Identity, scale=rinv[:, 0:1]),
                             r=["rinv"], w=["hbk"])
                        P.op("dve", lambda e: e.scalar_tensor_tensor(out=fsb[:], in0=hf[:], scalar=rinv[:, 0:1], in1=hbk[:],
                                                                     op0=ALU.mult, op1=ALU.add), r=["hf", "hbk", "rinv"], w=["fsb"])
                        P.op("dve", lambda e: e.scalar_tensor_tensor(out=fdb[:], in0=hf[:], scalar=rinv[:, 1:2], in1=hbk[:],
                                                                     op0=ALU.mult, op1=ALU.add), r=["hf", "hbk", "rinv"], w=["fdb"])
                        to_token_major(fsb, "fsb", [(FSt, "FSt", 0)])
                        P.dma("sp", FSD[:, 128 * j:128 * (j + 1)].rearrange("(t p) c -> p t c", p=128), FSt[:], r=["FSt"], w=["FSD"])
                        to_token_major(fdb, "fdb", [(FSt, "FSt", 0)])
                        P.dma("sp", FDD[:, 128 * j:128 * (j + 1)].rearrange("(t p) c -> p t c", p=128), FSt[:], r=["FSt"], w=["FDD"])
            yield

        ident = sb("ident", [128, 128])
        identb = sb("identb", [128, 128], BF16)
        P.dma("sp", ident[:], c_ident[:, :], w=["ident"])
        P.op("dve", lambda e: e.tensor_copy(out=identb[:], in_=ident[:]), r=["ident"], w=["identb"])

        kcol = sb("kcol", [128, 1])
        poscol = sb("poscol", [128, 1024])
        sh1 = sb("sh1", [128, NCH]); G1 = sb("G1", [128, NCH])
        csh1 = sb("csh1", [128, NCH]); cG1 = sb("cG1", [128, NCH])
        sh2 = sb("sh2", [128, NCH]); G2 = sb("G2", [128, NCH])
        np1 = sb("np1", [128, NCH]); np2 = sb("np2", [128, NCH])
        psm = ps("psm", [128, 512])
        epsc = sb("epsc", [128, 1])
        P.op("dve", lambda e: e.memset(epsc[:], EPS), w=["epsc"])
        with Scope():
            jrow = sb("jrow", [64, 512])
            om = sb("om", [64, 512])
            arg = sb("parg", [64, 1024])
            kk_i = sb("pki", [64, 1024], I32)
            kk_f = sb("pkf", [64, 1024])
            Ttab = sb("Ttab", [64, 1024])
            P.dma("sp", jrow[:], c_jrow.partition_broadcast(64), w=["jrow"])
            P.dma("sp", kcol[:], c_kcol[:, :], w=["kcol"])
            P.op("act", lambda e: e.activation(out=om[:], in_=jrow[:], func=AF.Exp, scale=-math.log(10000.0) / 512.0),
                 r=["jrow"], w=["om"])
            P.op("dve", lambda e: e.tensor_scalar(out=arg[:, 0:512], in0=om[:], scalar1=kcol[0:64, 0:1], scalar2=None,
                                                  op0=ALU.mult), r=["om", "kcol"], w=["parg"])
            P.op("dve", lambda e: e.tensor_scalar(out=arg[:, 512:1024], in0=arg[:, 0:512], scalar1=math.pi / 2.0,
                                                  scalar2=None, op0=ALU.add), r=["parg"], w=["parg"])
            P.op("dve", lambda e: e.tensor_scalar(out=kk_f[:], in0=arg[:], scalar1=1.0 / TWO_PI, scalar2=None,
                                                  op0=ALU.mult), r=["parg"], w=["pkf"])
            P.op("dve", lambda e: e.tensor_copy(out=kk_i[:], in_=kk_f[:]), r=["pkf"], w=["pki"])
            P.op("dve", lambda e: e.tensor_copy(out=kk_f[:], in_=kk_i[:]), r=["pki"], w=["pkf"])
            P.op("dve", lambda e: e.scalar_tensor_tensor(out=arg[:], in0=kk_f[:], scalar=-TWO_PI, in1=arg[:],
                                                         op0=ALU.mult, op1=ALU.add), r=["pkf", "parg"], w=["parg"])
            P.op("dve", lambda e: e.tensor_scalar(out=arg[:], in0=arg[:], scalar1=-3.141592, scalar2=3.141592,
                                                  op0=ALU.max, op1=ALU.min), r=["parg"], w=["parg"])
            P.op("act", lambda e: e.activation(out=Ttab[:], in_=arg[:], func=AF.Sin), r=["parg"], w=["Ttab"])
            P.dma("sp", POS[:, :], Ttab[:], r=["Ttab"], w=["POS"])
            P.dma("sp", poscol[0:64, :], POS[:, :], r=["POS"], w=["poscol"])
            P.dma("sp", poscol[64:128, :], POS[:, :], r=["POS"], w=["poscol"])

            cvs = sb("cvs", [128, NCH, 2])
            sig = sb("sig", [128, NCH, 2])
            cvb = sb("cvb", [128, NCH, 2])
            bada2 = sb("bada2", [2, 3072])
            modrow = sb("modrow", [2, 3072])
            wa = [sb("wa0", [128, NCH, 256]), sb("wa1", [128, NCH, 256])]
            P.dma("sp", cvs[:], cv[:, :, :], w=["cvs"])
            P.dma("sp", bada2[0:1, :], b_ada_q[:, :], w=["bada2"])
            P.dma("sp", bada2[1:2, :], b_ada_q[:, :], w=["bada2"])
            P.op("act", lambda e: e.activation(out=sig[:], in_=cvs[:], func=AF.Silu), r=["cvs"], w=["sig"])
            P.op("dve", lambda e: e.tensor_copy(out=cvb[:], in_=sig[:]), r=["sig"], w=["cvb"])
            w_ada_v = w_ada_q.rearrange("(p c) n -> p c n", c=NCH)
            fptr = [ps("fptr0", [128, 8, 128], BF16), ps("fptr1", [128, 8, 128], BF16)]
            pf = [ps("fpf0", [128, 512]), ps("fpf1", [128, 512])]
            tm_state["ptr"] = fptr
            fgen = filter_gen()
            for nb in range(12):
                if nb % 2 == 0:
                    next(fgen, None)
                wb = wa[nb % 2]
                wk = "wa%d" % (nb % 2)
                P.dma("sp", wb[:], w_ada_v[:, :, nb * 256:(nb + 1) * 256], w=[wk])
                for c in range(NCH):
                    P.op("pe", (lambda e, c=c, wb=wb: e.matmul(psm[0:2, 0:256], lhsT=cvb[:, c, :], rhs=wb[:, c, :],
                                                               start=(c == 0), stop=(c == NCH - 1))),
                         r=["cvb", wk], w=["psm"])
                P.op("dve", (lambda e, nb=nb: e.tensor_tensor(out=modrow[:, nb * 256:(nb + 1) * 256], in0=psm[0:2, 0:256],
                                                              in1=bada2[:, nb * 256:(nb + 1) * 256], op=ALU.add)),
                     r=["psm", "bada2"], w=["modrow"])
            for _ in fgen:
                pass
            right0 = contextlib.ExitStack()
            win = right0.enter_context(nc.sbuf_tensor("win", [128, NCH, 1288], BF16, side="right"))
            w_in_v = w_in_q.rearrange("(p c) n -> p c n", c=NCH)
            for j in range(4):
                P.dma("pool", win[:, :, j * 322:(j + 1) * 322], w_in_v[:, :, j * 322:(j + 1) * 322], w=["win"])
            P.dma("sp", MODIN[:, :], modrow[:], r=["modrow"], w=["MODIN"])
            P.coll("AllGather", GRP, MODIN[:, :], MODG[:, :], r=["MODIN"], w=["MODG"])

            def mod_fm(dst, r, seg):
                for half in range(2):
                    j0 = seg * 2048 + half * 1024
                    qs, col = j0 // 3072, j0 % 3072
                    src = MODG[2 * qs + r:2 * qs + r + 1, col:col + 1024].rearrange("o (p c) -> (o p) c", c=NCH)
                    P.dma("sp", dst[half * 64:(half + 1) * 64, :], src, r=["MODG"], w=[dst.name])

            def mod_bc(dst, r, seg):
                for half in range(2):
                    j0 = seg * 2048 + half * 1024
                    qs, col = j0 // 3072, j0 % 3072
                    src = MODG[2 * qs + r:2 * qs + r + 1, col:col + 1024].partition_broadcast(128)
                    P.dma("sp", dst[:, half * 1024:(half + 1) * 1024], src, r=["MODG"], w=[dst.name])

            P.dma("sp", np1[:], nrm_pre1[:, :], w=["np1"])
            P.dma("sp", np2[:], nrm_pre2[:, :], w=["np2"])
            mod_fm(sh1, 0, 0); mod_fm(G1, 0, 1)
            mod_fm(csh1, 1, 0); mod_fm(cG1, 1, 1)
            mod_fm(sh2, 0, 3); mod_fm(G2, 0, 4)
            for g, npx in ((G1, np1), (cG1, np1), (G2, np2)):
                nm = g.name
                P.op("dve", (lambda e, g=g, npx=npx: e.scalar_tensor_tensor(out=g[:], in0=g[:], scalar=1.0, in1=npx[:],
                                                                             op0=ALU.add, op1=ALU.mult)),
                     r=[nm, npx.name], w=[nm])
            if stop == "p0":
                d1 = dbg_out("pos", [64, 1024]); d2 = dbg_out("modg", [8, 3072]); d3 = dbg_out("G1", [128, NCH])
                d5 = dbg_out("sh2", [128, NCH])
                P.dma("sp", d1[:, :], Ttab[:], r=["Ttab"], w=["dbgo"])
                P.dma("sp", d2[:, :], MODG[:, :], r=["MODG"], w=["dbgo"])
                P.dma("sp", d3[:, :], G1[:], r=["G1"], w=["dbgo"])
                P.dma("sp", d5[:, :], sh2[:], r=["sh2"], w=["dbgo"])
                P.final_wait("sp", ["dbgo"])
                P.emit()
                return nc, dbg, in_names


        with Scope():
            NB_ = 4
            xt = [sb("xt%d" % j_, [128, D]) for j_ in range(NB_)]
            posrow = [sb("posrow%d" % j_, [128, 1024]) for j_ in range(NB_)]
            junk = sb("junk", [128, D], BF16)
            ssq = [sb("ssq%d" % j_, [128, 1]) for j_ in range(NB_)]
            rstd = [sb("rstd%d" % j_, [128, 1]) for j_ in range(NB_)]
            xn = [sb("xn%d" % j_, [128, D], BF16) for j_ in range(NB_)]
            mtmp = [sb("mtmp0", [128, NCH, 128]), sb("mtmp1", [128, NCH, 128])]
            hT = [sb("hT0", [128, NCH, 512], BF16), sb("hT1", [128, NCH, 512], BF16)]
            ust = [sb("ust%d" % j_, [128, 512]) for j_ in range(4)]
            tp = [ps("tp0", [128, NCH, 128], BF16), ps("tp1", [128, NCH, 128], BF16)]
            pmm = [psm, ps("pmm1", [128, 512]), ps("pmm2", [128, 512])]
            cnt = {"tile": 0, "mm": 0}

            def norm_tile(src_rows, Gm, Sm, use_pos, pos_i, xp_rows, hT_dst, hkey):
                s_ = cnt["tile"] % NB_
                s2_ = cnt["tile"] % 2
                cnt["tile"] += 1
                xk = "xt%d" % s_
                P.dma("sp", xt[s_][:], src_rows, w=[xk + "a", xk + "b"], sk=xk)
                if use_pos:
                    pk = "posrow%d" % s_
                    P.dma("sp", posrow[s_][0:64, :], POS[2 * pos_i:2 * pos_i + 1, :].partition_broadcast(64),
                          r=["POS"], w=[pk])
                    P.dma("sp", posrow[s_][64:128, :], POS[2 * pos_i + 1:2 * pos_i + 2, :].partition_broadcast(64),
                          r=["POS"], w=[pk])
                    P.op("pool", lambda e: e.tensor_tensor(out=xt[s_][:, 0:1024], in0=xt[s_][:, 0:1024],
                                                           in1=posrow[s_][:], op=ALU.add), r=[pk], w=[xk + "a"])
                    P.op("dve", lambda e: e.tensor_tensor(out=xt[s_][:, 1024:D], in0=xt[s_][:, 1024:D],
                                                          in1=poscol[:], op=ALU.add), r=["poscol"], w=[xk + "b"])
                sk_ = "ssq%d" % s_
                P.op("dve", lambda e: e.memset(ssq[s_][:], 0.0), w=[sk_])
                P.op("act", lambda e: e.activation(out=junk[:], in_=xt[s_][:], func=AF.Square, accum_out=ssq[s_][:]),
                     r=[xk + "a", xk + "b"], w=[sk_, "junk"])
                rk = "rstd%d" % s_
                P.op("act", lambda e: e.activation(out=rstd[s_][:], in_=ssq[s_][:], func=AF.Sqrt, scale=1.0 / D, bias=epsc[:, 0:1]),
                     r=[sk_, "epsc"], w=[rk])
                P.op("dve", lambda e: e.reciprocal(out=rstd[s_][:], in_=rstd[s_][:]), r=[rk], w=[rk])
                nk = "xn%d" % s_
                P.op("dve", lambda e: e.tensor_scalar(out=xn[s_][:], in0=xt[s_][:], scalar1=rstd[s_][:, 0:1], scalar2=None,
                                                      op0=ALU.mult), r=[xk + "a", xk + "b", rk], w=[nk])
                return (s_, s2_, Gm, Sm, hT_dst, hkey)

            def norm_post(st):
                s_, s2_, Gm, Sm, hT_dst, hkey = st
                nk = "xn%d" % s_
                tk = "tp%d" % s2_
                xnv = xn[s_][:].rearrange("p (j c) -> p c j", c=NCH)
                for c in range(NCH):
                    P.op("pe", (lambda e, c=c: e.transpose(out=tp[s2_][:, c, :], in_=xnv[:, c, :], identity=identb[:])),
                         r=[nk, "identb"], w=[tk])
                mk = "mtmp%d" % s2_
                P.op("dve", lambda e: e.tensor_tensor(out=mtmp[s2_][:], in0=tp[s2_][:], in1=bc_last(Gm[:], 128), op=ALU.mult),
                     r=[tk, Gm.name], w=[mk])
                P.op("pool", lambda e: e.tensor_tensor(out=hT_dst, in0=mtmp[s2_][:], in1=bc_last(Sm[:], 128), op=ALU.add),
                     r=[mk, Sm.name], w=[hkey])

            def project(hbuf, hkey, ntok, ctiles, dst_fn):
                for (c0, ncol, r0) in ctiles:
                    m_ = cnt["mm"] % 3
                    cnt["mm"] += 1
                    pk = "pmm%d" % m_
                    for c in range(NCH):
                        P.op("pe", (lambda e, c=c, c0=c0, ncol=ncol, m_=m_: e.matmul(
                            pmm[m_][0:ncol, 0:ntok], lhsT=win[:, c, c0:c0 + ncol], rhs=hbuf[:, c, 0:ntok],
                            start=(c == 0), stop=(c == NCH - 1))), r=["win", hkey], w=[pk])
                    u_ = cnt["mm"] % 4
                    uk = "ust%d" % u_
                    P.op("act", (lambda e, m_=m_, u_=u_, ncol=ncol: e.copy(out=ust[u_][0:ncol, 0:ntok],
                                                                           in_=pmm[m_][0:ncol, 0:ntok])),
                         r=[pk], w=[uk])
                    P.dma("act", dst_fn(r0, ncol), ust[u_][0:ncol, 0:ntok], r=[uk], w=["U"], sk=uk + "st")

            main_ct = [(j * 128, 128, j * 128) for j in range(10)] + [(1280, 8, 1280)]
            ctx_ct = [(j * 128, 128, j * 128) for j in range(4)] + [(1280, 8, 512)]
            nblk = 8 if stop != "pA1" else 1
            hb = 0
            sts = [norm_tile(ctx_b[128 * t:128 * (t + 1), :], cG1, csh1, False, 0, None,
                             hT[hb][:, :, 128 * t:128 * (t + 1)], "hT%d" % hb) for t in range(2)]
            for st in sts:
                norm_post(st)
            project(hT[hb], "hT%d" % hb, 256, ctx_ct, lambda r0, n: UC[r0:r0 + n, :])
            def norm_block(blk):
                hb = (blk + 1) % 2
                sts = []
                for t in range(4):
                    i = 4 * blk + t
                    sts.append(norm_tile(x_b[128 * i:128 * (i + 1), :], G1, sh1, True, i, None,
                                         hT[hb][:, :, 128 * t:128 * (t + 1)], "hT%d" % hb))
                for st in sts:
                    norm_post(st)

            def project_block(blk):
                hb = (blk + 1) % 2
                project(hT[hb], "hT%d" % hb, 512, main_ct,
                        (lambda r0, n, blk=blk: U[r0:r0 + n, 512 * blk:512 * (blk + 1)]))

            norm_block(0)
            for blk in range(nblk):
                if blk + 1 < nblk:
                    norm_block(blk + 1)
                project_block(blk)

            if stop in ("pA", "pA1"):
                d1 = dbg_out("U", [1288, L]); d2 = dbg_out("UC", [520, LC])
                P.dma("sp", d1[:, :], U[:, :], r=["U"], w=["dbgo"])
                P.dma("sp", d2[:, :], UC[:, :], r=["U"], w=["dbgo"])
                P.final_wait("sp", ["dbgo"])
                P.emit()
                return nc, dbg, in_names

        right0.close()
        NCK = 34
        with Scope():
            trif = sb("trif", [128, 128]); trib = sb("trib", [128, 128])
            negf4 = sb("negf4", [128, 4, 128]); negb4 = sb("negb4", [128, 4, 128]); ones = sb("ones", [128, 128])
            onec = sb("onec", [128, 1])
            P.dma("sp", trif[:], c_trif[:, :], w=["trif"]); P.dma("sp", trib[:], c_trib[:, :], w=["trib"])
            for h in range(4):
                P.dma("sp", negf4[:, h, :], c_negf[:, :], w=["negf4"])
                P.dma("sp", negb4[:, h, :], c_negb[:, :], w=["negb4"])
            P.op("pool", lambda e: e.memset(ones[:], 1.0), w=["ones"])
            P.op("pool", lambda e: e.memset(onec[:], 1.0), w=["onec"])
            cw = sb("cw", [128, 4, 3]); cb = sb("cb", [128, 4])
            P.dma("sp", cw[:], ssd_cw[:, :, :], w=["cw"]); P.dma("sp", cb[:], ssd_cb[:, :], w=["cb"])
            Aneg = sb("Aneg", [128, 8]); dtb = sb("dtb", [128, 8]); Dq = sb("Dq", [128, 4])
            P.dma("sp", Aneg[:], ssd_alog.partition_broadcast(128), w=["Aneg"])
            P.dma("sp", dtb[:], ssd_dtb.partition_broadcast(128), w=["dtb"])
            P.dma("sp", Dq[:], ssd_Dq.partition_broadcast(128), w=["Dq"])
            P.op("act", lambda e: e.activation(out=Aneg[:], in_=Aneg[:], func=AF.Exp), r=["Aneg"], w=["Aneg"])
            P.op("dve", lambda e: e.tensor_scalar(out=Aneg[:], in0=Aneg[:], scalar1=-1.0, scalar2=None, op0=ALU.mult),
                 r=["Aneg"], w=["Aneg"])

            BT = sb("BT", [128, L + LC], BF16); CT = sb("CT", [128, L + LC], BF16)
            xs_tok = sb("xs_tok", [128, NCK, 256]); B_tok = sb("B_tok", [128, NCK, 128], BF16)
            scor = sb("scor", [128, 32, 128]); Yacc = sb("Yacc", [128, 32, 256])
            dt_t = sb("dt_t", [128, NCK, 8]); a_t = sb("a_t", [128, NCK, 8]); cs_t = sb("cs_t", [128, NCK, 8])
            ncs_t = sb("ncs_t", [128, NCK, 8]); ecs_t = sb("ecs_t", [128, NCK, 8]); wst_t = sb("wst_t", [128, NCK, 8])
            etot_t = sb("etot_t", [128, NCK, 8]); tot_t = sb("tot_t", [128, NCK, 8]); tmp_t = sb("tmp_t", [128, NCK, 8])
            pA_ = ps("pB1a", [128, 512]); pB_ = ps("pB1b", [128, 512]); pC_ = ps("pB1c", [128, 512])
            pD_ = ps("pB1d", [128, 512]); pE_ = ps("pB1e", [128, 512])
            pT_full = ps("pB1t", [128, 1024], BF16)
            pT_ = pT_full[:, 0:128]

            with Scope():
                dtraw = sb("dtraw", [8, L + LC])
                P.dma("sp", dtraw[:, 0:L], U[1280:1288, :], r=["U"], w=["dtraw"])
                P.dma("sp", dtraw[:, L:L + LC], UC[512:520, :], r=["U"], w=["dtraw"])
                for ck_ in range(NCK):
                    P.op("pe", (lambda e: e.transpose(out=pA_[:, ck_ * 8:(ck_ + 1) * 8],
                                                      in_=dtraw[0:8, ck_ * 128:(ck_ + 1) * 128],
                                                      identity=ident[0:8, 0:8])),
                         r=["dtraw", "ident"], w=["pB1a"])
                pAv = pA_[:, 0:NCK * 8].rearrange("p (c j) -> p c j", j=8)
                dtb_b = dtb[:].unsqueeze(1).to_broadcast([128, NCK, 8])
                Aneg_b = Aneg[:].unsqueeze(1).to_broadcast([128, NCK, 8])
                P.op("dve", lambda e: e.tensor_tensor(out=dt_t[:], in0=pAv, in1=dtb_b, op=ALU.add),
                     r=["pB1a", "dtb"], w=["dt_t"])
                P.op("act", lambda e: e.activation(out=tmp_t[:], in_=dt_t[:], func=AF.Abs),
                     r=["dt_t"], w=["tmp_t"])
                P.op("act", lambda e: e.activation(out=tmp_t[:], in_=tmp_t[:], func=AF.Exp, scale=-1.0),
                     r=["tmp_t"], w=["tmp_t"])
                P.op("act", lambda e: e.activation(out=tmp_t[:], in_=tmp_t[:], func=AF.Ln, bias=onec[:, 0:1]),
                     r=["tmp_t", "onec"], w=["tmp_t"])
                P.op("dve", lambda e: e.scalar_tensor_tensor(out=dt_t[:], in0=dt_t[:], scalar=0.0, in1=tmp_t[:],
                                                             op0=ALU.max, op1=ALU.add), r=["dt_t", "tmp_t"], w=["dt_t"])
                P.op("dve", lambda e: e.tensor_tensor(out=a_t[:], in0=dt_t[:], in1=Aneg_b, op=ALU.mult),
                     r=["dt_t", "Aneg"], w=["a_t"])
                a_flat = a_t[:].rearrange("p c j -> p (c j)")
                P.op("pe", lambda e: e.matmul(pB_[:, 0:NCK * 8], lhsT=trif[:], rhs=a_flat, start=True, stop=True),
                     r=["trif", "a_t"], w=["pB1b"])
                P.op("pe", lambda e: e.matmul(pC_[:, 0:NCK * 8], lhsT=trib[:], rhs=a_flat, start=True, stop=True),
                     r=["trib", "a_t"], w=["pB1c"])
                P.op("pe", lambda e: e.matmul(pD_[:, 0:NCK * 8], lhsT=ones[:], rhs=a_flat, start=True, stop=True),
                     r=["ones", "a_t"], w=["pB1d"])
                pBv = pB_[:, 0:NCK * 8].rearrange("p (c j) -> p c j", j=8)
                pCv = pC_[:, 0:NCK * 8].rearrange("p (c j) -> p c j", j=8)
                pDv = pD_[:, 0:NCK * 8].rearrange("p (c j) -> p c j", j=8)
                P.op("dve", lambda e: e.tensor_copy(out=cs_t[:, :, 0:4], in_=pBv[:, :, 0:4]), r=["pB1b"], w=["cs_t"])
                P.op("dve", lambda e: e.tensor_copy(out=cs_t[:, :, 4:8], in_=pCv[:, :, 4:8]), r=["pB1c"], w=["cs_t"])
                P.op("dve", lambda e: e.tensor_copy(out=tot_t[:], in_=pDv), r=["pB1d"], w=["tot_t"])
                P.op("dve", lambda e: e.tensor_scalar(out=ncs_t[:], in0=cs_t[:], scalar1=-1.0, scalar2=None, op0=ALU.mult),
                     r=["cs_t"], w=["ncs_t"])
                P.op("act", lambda e: e.activation(out=ecs_t[:], in_=cs_t[:], func=AF.Exp), r=["cs_t"], w=["ecs_t"])
                P.op("act", lambda e: e.activation(out=etot_t[:], in_=tot_t[:], func=AF.Exp), r=["tot_t"], w=["etot_t"])
                P.op("dve", lambda e: e.tensor_tensor(out=wst_t[:], in0=tot_t[:], in1=cs_t[:], op=ALU.subtract),
                     r=["tot_t", "cs_t"], w=["wst_t"])
                P.op("act", lambda e: e.activation(out=wst_t[:], in_=wst_t[:], func=AF.Exp), r=["wst_t"], w=["wst_t"])
                P.op("dve", lambda e: e.tensor_tensor(out=wst_t[:], in0=wst_t[:], in1=dt_t[:], op=ALU.mult),
                     r=["wst_t", "dt_t"], w=["wst_t"])
            with Scope():
                cin = [sb("cin0", [128, L + 2]), sb("cin1", [128, L + 2])]
                cacc = sb("cacc", [128, L])
                xsTj = sb("xsTj", [128, L + LC])
                k_ = 0
                for ct in range(4):
                    for (src, ntok, off) in ((U, L, 0), (UC, LC, L)):
                        s_ = k_ % 2
                        k_ += 1
                        ck = "cin%d" % s_; ak = "cacc"
                        P.op("pool", (lambda e: e.memset(cin[s_][:, 0:1], 0.0)), w=[ck])
                        P.op("pool", (lambda e: e.memset(cin[s_][:, ntok + 1:ntok + 2], 0.0)), w=[ck])
                        P.dma("sp", cin[s_][:, 1:ntok + 1], src[128 * ct:128 * (ct + 1), :], r=["U"], w=[ck])
                        eng = "dve"
                        P.op("act", (lambda e: e.activation(
                            out=cacc[:, 0:ntok], in_=cin[s_][:, 0:ntok], func=AF.Identity, scale=cw[:, ct, 0:1])),
                             r=[ck, "cw"], w=[ak])
                        for tap in (1, 2):
                            P.op(eng, (lambda e: e.scalar_tensor_tensor(
                                out=cacc[:, 0:ntok], in0=cin[s_][:, tap:tap + ntok], scalar=cw[:, ct, tap:tap + 1],
                                in1=cacc[:, 0:ntok], op0=ALU.mult, op1=ALU.add)), r=[ck, "cw"], w=[ak])
                        if ct < 2:
                            dst = xsTj[:, off:off + ntok]; dk = "xsTj"
                        elif ct == 2:
                            dst = BT[:, off:off + ntok]; dk = "BT"
                        else:
                            dst = CT[:, off:off + ntok]; dk = "CT"
                        P.op("act", (lambda e: e.activation(
                            out=dst, in_=cacc[:, 0:ntok], func=AF.Silu, bias=cb[:, ct:ct + 1])),
                             r=[ak, "cb"], w=[dk])
                    if ct < 2:
                        for ck_ in range(NCK):
                            col = ck_ * 128
                            pk = ("pB1e", "pB1c")[ck_ % 2]
                            pe_o = (pE_, pC_)[ck_ % 2][:, 0:128]
                            P.op("pe", (lambda e: e.transpose(out=pe_o, in_=xsTj[:, col:col + 128], identity=ident[:])),
                                 r=["xsTj", "ident"], w=[pk])
                            P.op("act", (lambda e: e.copy(out=xs_tok[:, ck_, ct * 128:(ct + 1) * 128], in_=pe_o)),
                                 r=[pk], w=["xs_tok"])
                Dq_b = Dq[:].unsqueeze(2).to_broadcast([128, 4, 64])
                for ck_ in range(NCK):
                    col = ck_ * 128
                    P.op("pe", (lambda e: e.transpose(out=pT_, in_=BT[:, col:col + 128], identity=identb[:])),
                         r=["BT", "identb"], w=["pB1t"])
                    P.op("dve", (lambda e: e.tensor_copy(out=B_tok[:, ck_, :], in_=pT_)),
                         r=["pB1t"], w=["B_tok"])
                    if ck_ < 32:
                        pk = ("pB1d", "pB1b")[ck_ % 2]
                        pd_o = (pD_, pB_)[ck_ % 2][:, 0:128]
                        P.op("pe", (lambda e: e.matmul(pd_o, lhsT=BT[:, col:col + 128],
                                                       rhs=CT[:, col:col + 128], start=True, stop=True)),
                             r=["BT", "CT"], w=[pk])
                        P.op("act", (lambda e: e.copy(out=scor[:, ck_, :], in_=pd_o)),
                             r=[pk], w=["scor"])
                        P.op("pool", (lambda e: e.tensor_tensor(
                            out=Yacc[:, ck_, :].rearrange("p (h d) -> p h d", h=4),
                            in0=xs_tok[:, ck_, :].rearrange("p (h d) -> p h d", h=4), in1=Dq_b, op=ALU.mult)),
                             r=["xs_tok", "Dq"], w=["Yacc"])

            S = [sb("S0", [128, 256]), sb("S1", [128, 256])]
            Sb2 = [[sb("Sb%d_%d" % (d, j_), [128, 256], BF16) for j_ in range(2)] for d in range(2)]
            xdt = [sb("xdt%d" % d, [128, 256], BF16) for d in range(2)]; xw = [sb("xw%d" % d, [128, 256], BF16) for d in range(2)]
            R4 = [sb("R4%d" % d, [128, 4, 128]) for d in range(2)]; arg4 = [sb("arg4%d" % d, [128, 4, 128]) for d in range(2)]
            Lm4 = [sb("Lm4%d" % d, [128, 4, 128]) for d in range(2)]; M4 = [sb("M4%d" % d, [128, 4, 128], BF16) for d in range(2)]
            t1 = [sb("t1%d" % d, [128, 256]) for d in range(2)]; t2 = [sb("t2%d" % d, [128, 256]) for d in range(2)]
            stmp = [sb("stmp%d" % d, [128, 256]) for d in range(2)]
            PA = [(pA_, "pB1a"), (pC_, "pB1c")]
            PBC = [(pB_, "pB1b"), (pE_, "pB1e")]
            PD = [(pD_, "pB1d"), (psm, "psm")]
            for d_ in range(2):
                P.op("pool", (lambda e, d_=d_: e.memset(S[d_][:], 0.0)), w=["S%d" % d_])
            tri4 = [trif[:].unsqueeze(1).to_broadcast([128, 4, 128]), trib[:].unsqueeze(1).to_broadcast([128, 4, 128])]
            neg4 = [negf4, negb4]

            def h4(ap):
                return ap.rearrange("p (h d) -> p h d", h=4)

            def scan_step(d_, ck_, with_y, par):
                sk_ = "S%d" % d_
                sl = slice(4 * d_, 4 * d_ + 4)
                sfx = "%d" % d_
                pa, pak = PA[d_]; pbc, pbck = PBC[d_]; pd, pdk = PD[d_]
                P.op("pool", lambda e: e.tensor_tensor(out=h4(xw[d_][:]), in0=h4(xs_tok[:, ck_, :]),
                                                       in1=bc_last(wst_t[:, ck_, sl], 64), op=ALU.mult),
                     r=["xs_tok", "wst_t"], w=["xw" + sfx])
                yield
                P.op("pe", lambda e: e.matmul(pd[:, 0:256], lhsT=B_tok[:, ck_, :], rhs=xw[d_][:], start=True, stop=True),
                     r=["B_tok", "xw" + sfx], w=[pdk])
                yield
                P.op("dve", lambda e: e.tensor_tensor(out=h4(stmp[d_][:]), in0=h4(S[d_][:]),
                                                      in1=bc_last(etot_t[:, ck_, sl], 64), op=ALU.mult),
                     r=[sk_, "etot_t"], w=["stmp" + sfx])
                yield
                P.op("dve", lambda e: e.tensor_tensor(out=S[d_][:], in0=stmp[d_][:], in1=pd[:, 0:256], op=ALU.add),
                     r=["stmp" + sfx, pdk], w=[sk_])
                yield
                P.op("act", lambda e: e.copy(out=Sb2[d_][1 - par][:], in_=S[d_][:]), r=[sk_], w=["Sb%d_%d" % (d_, 1 - par)])
                yield

                if with_y:
                    P.op("dve", lambda e: e.tensor_tensor(out=R4[d_][:], in0=tri4[d_],
                                                          in1=bc_last(a_t[:, ck_, sl], 128), op=ALU.mult),
                         r=["trif", "trib", "a_t"], w=["R4" + sfx])
                    yield
                    P.op("pe", lambda e: e.matmul(pa[:], lhsT=ones[:], rhs=R4[d_][:].rearrange("p h l -> p (h l)"),
                                                  start=True, stop=False), r=["ones", "R4" + sfx], w=[pak])
                    yield
                    P.op("pe", lambda e: e.matmul(pa[:], lhsT=ident[:], rhs=neg4[d_][:].rearrange("p h l -> p (h l)"),
                                                  start=False, stop=True), r=["ident", "negf4", "negb4"], w=[pak])
                    yield
                    P.op("dve", lambda e: e.tensor_tensor(out=arg4[d_][:], in0=h4(pa[:]),
                                                          in1=bc_last(ncs_t[:, ck_, sl], 128), op=ALU.add),
                         r=[pak, "ncs_t"], w=["arg4" + sfx])
                    yield
                    P.op("act", lambda e: e.activation(out=Lm4[d_][:], in_=arg4[d_][:], func=AF.Exp), r=["arg4" + sfx], w=["Lm4" + sfx])
                    yield
                    P.op("pool", lambda e: e.tensor_tensor(out=M4[d_][:], in0=Lm4[d_][:],
                                                           in1=scor[:, ck_, :].unsqueeze(1).to_broadcast([128, 4, 128]),
                                                           op=ALU.mult), r=["Lm4" + sfx, "scor"], w=["M4" + sfx])
                    yield
                    P.op("dve", lambda e: e.tensor_tensor(out=h4(xdt[d_][:]), in0=h4(xs_tok[:, ck_, :]),
                                                          in1=bc_last(dt_t[:, ck_, sl], 64), op=ALU.mult),
                         r=["xs_tok", "dt_t"], w=["xdt" + sfx])
                    yield
                    for h in range(4):
                        P.op("pe", (lambda e, h=h: e.matmul(pbc[:, h * 64:(h + 1) * 64], lhsT=M4[d_][:, h, :],
                                                            rhs=xdt[d_][:, h * 64:(h + 1) * 64], start=True, stop=True)),
                             r=["M4" + sfx, "xdt" + sfx], w=[pbck])
                        yield
                    col = ck_ * 128
                    P.op("pe", lambda e: e.matmul(pbc[:, 256:512], lhsT=CT[:, col:col + 128], rhs=Sb2[d_][par][:],
                                                  start=True, stop=True), r=["CT", "Sb%d_%d" % (d_, par)], w=[pbck])
                    yield
                    P.op("dve", lambda e: e.tensor_tensor(out=h4(t1[d_][:]), in0=h4(pbc[:, 256:512]),
                                                          in1=bc_last(ecs_t[:, ck_, sl], 64), op=ALU.mult),
                         r=[pbck, "ecs_t"], w=["t1" + sfx])
                    yield
                    P.op("dve", lambda e: e.tensor_tensor(out=t2[d_][:], in0=t1[d_][:], in1=pbc[:, 0:256], op=ALU.add),
                         r=["t1" + sfx, pbck], w=["t2" + sfx])
                    yield
                    P.op("pool", lambda e: e.tensor_tensor(out=Yacc[:, ck_, :], in0=Yacc[:, ck_, :], in1=t2[d_][:], op=ALU.add),
                         r=["t2" + sfx], w=["Yacc"])
                    yield

            nmain = 32 if stop != "pB1s" else 2
            def run2(g0, g1):
                done0 = done1 = False
                while not (done0 and done1):
                    if not done0:
                        try:
                            next(g0)
                        except StopIteration:
                            done0 = True
                    if not done1:
                        try:
                            next(g1)
                        except StopIteration:
                            done1 = True

            run2(scan_step(0, 32, False, 0), scan_step(1, 33, False, 0))
            run2(scan_step(0, 33, False, 1), scan_step(1, 32, False, 1))
            if stop == "pB1d":
                dd = {}
                for nm_, t_, shp in (("xs_tok", xs_tok, [128, NCK, 256]), ("dt_t", dt_t, [128, NCK, 8]), ("cs_t", cs_t, [128, NCK, 8]),
                                     ("wst_t", wst_t, [128, NCK, 8]), ("etot_t", etot_t, [128, NCK, 8]), ("S0", S[0], [128, 256]),
                                     ("S1", S[1], [128, 256]), ("scor", scor, [128, 32, 128])):
                    dd[nm_] = dbg_out(nm_, shp)
                    P.dma("sp", dd[nm_], t_[:], r=[nm_], w=["dbgo"])
                P.final_wait("sp", ["dbgo"])
            if stop == "pB1d":
                nmain = 0
            for ck_ in range(nmain):
                run2(scan_step(0, ck_, True, ck_ % 2), scan_step(1, 31 - ck_, True, ck_ % 2))
            P.dma("sp", YS.rearrange("(c p) n -> p c n", p=128), Yacc[:], r=["Yacc"], w=["YS"])
            if stop in ("pB1", "pB1s"):
                d1 = dbg_out("YS", [L, 256]); d2 = dbg_out("S", [2, 128, 256])
                P.dma("sp", d1[:, :], YS[:, :], r=["YS"], w=["dbgo"])
                P.dma("sp", d2[0], S[0][:], r=["S0"], w=["dbgo"])
                P.dma("sp", d2[1], S[1][:], r=["S1"], w=["dbgo"])
                P.final_wait("sp", ["dbgo"])
        if stop in ("pB1", "pB1s", "pB1d"):
            return nc, dbg, in_names

        with Scope():
            Ybuf = sb("Ybuf", [128, KT, 2, 256], BF16)
            wkc = sb("wkc", [128, KT]); hbq = sb("hbq", [128, 2])
            P.dma("sp", wkc[:], c_wk[:, :], w=["wkc"]); P.dma("sp", hbq[:], hy_biasq[:, :], w=["hbq"])
            with Scope():
                DC = sb("DC", [128, 32, 512], BF16); DS = sb("DS", [128, 32, 512], BF16)
                ptr = [ps("ptr0", [128, 8, 128], BF16), ps("ptr1", [128, 8, 128], BF16)]
                pf = [ps("pf0", [128, 512]), ps("pf1", [128, 512])]
                tm_state["ptr"] = ptr

                with Scope():
                    hcw = sb("hcw", [128, 6, 3]); hcb = sb("hcb", [128, 6])
                    P.dma("sp", hcw[:], hy_cw[:, :, :], w=["hcw"]); P.dma("sp", hcb[:], hy_cb[:, :], w=["hcb"])
                    cin = [sb("hcin0", [128, L + 2]), sb("hcin1", [128, L + 2])]
                    cacc = sb("hcacc", [128, L])
                    x1c = sb("x1c", [128, L]); cvo = sb("cvo", [128, L]); vxb = sb("vxb", [128, L], BF16)
                    k_ = 0
                    for ct in (0, 1, 2, 4, 3, 5):
                        s_ = k_ % 2
                        k_ += 1
                        ck = "hcin%d" % s_
                        P.op("pool", lambda e: e.memset(cin[s_][:, 0:1], 0.0), w=[ck])
                        P.op("pool", lambda e: e.memset(cin[s_][:, L + 1:L + 2], 0.0), w=[ck])
                        P.dma("sp", cin[s_][:, 1:L + 1], U[512 + 128 * ct:512 + 128 * (ct + 1), :], r=["U"], w=[ck])
                        P.op("act", lambda e: e.activation(out=cacc[:], in_=cin[s_][:, 0:L], func=AF.Identity,
                                                           scale=hcw[:, ct, 0:1]), r=[ck, "hcw"], w=["hcacc"])
                        for tap in (1, 2):
                            P.op("dve", lambda e: e.scalar_tensor_tensor(out=cacc[:], in0=cin[s_][:, tap:tap + L],
                                                                         scalar=hcw[:, ct, tap:tap + 1], in1=cacc[:],
                                                                         op0=ALU.mult, op1=ALU.add), r=[ck, "hcw"], w=["hcacc"])
                        j = ct % 2
                        if ct < 2:
                            P.op("act", lambda e: e.activation(out=cvo[:], in_=cacc[:], func=AF.Identity, bias=hcb[:, ct:ct + 1]),
                                 r=["hcacc", "hcb"], w=["cvo"])
                            P.dma("sp", X0S[128 * j:128 * (j + 1), :], cvo[:], r=["cvo"], w=["X0S"])
                        elif ct < 4:
                            P.op("act", lambda e: e.activation(out=x1c[:], in_=cacc[:], func=AF.Identity, bias=hcb[:, ct:ct + 1]),
                                 r=["hcacc", "hcb"], w=["x1c"])
                        else:
                            P.op("act", lambda e: e.activation(out=cvo[:], in_=cacc[:], func=AF.Identity, bias=hcb[:, ct:ct + 1]),
                                 r=["hcacc", "hcb"], w=["cvo"])
                            P.op("dve", lambda e: e.tensor_tensor(out=cvo[:], in0=cvo[:], in1=x1c[:], op=ALU.mult),
                                 r=["cvo", "x1c"], w=["cvo"])
                            P.op("act", lambda e: e.copy(out=vxb[:], in_=cvo[:]), r=["cvo"], w=["vxb"])
                            P.dma("sp", VXS[128 * j:128 * (j + 1), :], cvo[:], r=["cvo"], w=["VXS"])
                            to_token_major(vxb, "vxb", [(DC, "DC", 128 * j), (DS, "DS", 128 * j)])

                P.dma("act", DC[:, :, 256:512], FSD.rearrange("(t p) c -> p t c", p=128), r=["FSD"], w=["DC"])
                P.dma("act", DS[:, :, 256:512], FDD.rearrange("(t p) c -> p t c", p=128), r=["FDD"], w=["DS"])

                right_stack = contextlib.ExitStack()
                wo = right_stack.enter_context(nc.sbuf_tensor("wo", [128, NCH, D], BF16, side="right"))
                w_out_v = w_out.rearrange("(c p) n -> p c n", p=128)
                for c in range(NCH):
                    P.dma("pool", wo[:, c, :], w_out_v[:, c, :], w=["wo"])
                if stop not in ("pB2", "pB2f"):
                    for tq_ in range(4):
                        P.coll("AllGather", GRP, YS[1024 * tq_:1024 * (tq_ + 1), :], GS[4096 * tq_:4096 * (tq_ + 1), :],
                               r=["YS"], w=["GS"], sk="gs")
                with Scope():
                    tcs = [sb("tcs0", [128, KT, 128], BF16), sb("tcs1", [128, KT, 128], BF16)]
                    tsn = [sb("tsn0", [128, KT, 128], BF16), sb("tsn1", [128, KT, 128], BF16)]
                    pcs = pf
                    psn = [ps("psn0", [128, 512]), ps("psn1", [128, 512])]
                    ev = [sb("evA", [128, 256]), sb("evKr", [128, 256]), sb("evB", [128, 256]), sb("evKi", [128, 256])]
                    tq = [sb("tq1", [128, 256]), sb("tq2", [128, 256]), sb("tq3", [128, 256]), sb("tq4", [128, 256])]
                    nkt = KT if stop != "pB2s" else 2
                    for kt in range(KT):
                        k = kt % 2
                        P.dma("sp", tcs[k][:], t2cos[kt], w=["tcs%d" % k])
                        P.dma("sp", tsn[k][:], t2sin[kt], w=["tsn%d" % k])
                        for tt in range(32):
                            P.op("pe", lambda e: e.matmul(pcs[k][:], lhsT=tcs[k][:, tt, :], rhs=DC[:, tt, :], start=(tt == 0), stop=(tt == 31)),
                                 r=["tcs%d" % k, "DC"], w=["pf%d" % k])
                        for tt in range(32):
                            P.op("pe", lambda e: e.matmul(psn[k][:], lhsT=tsn[k][:, tt, :], rhs=DS[:, tt, :], start=(tt == 0), stop=(tt == 31)),
                                 r=["tsn%d" % k, "DS"], w=["psn%d" % k])
                        P.op("act", lambda e: e.activation(out=ev[0][:], in_=pcs[k][:, 0:256], func=AF.Identity, scale=wkc[:, kt:kt + 1]),
                             r=["pf%d" % k, "wkc"], w=["evA"])
                        P.op("act", lambda e: e.copy(out=ev[1][:], in_=pcs[k][:, 256:512]), r=["pf%d" % k], w=["evKr"])
                        P.op("dve", lambda e: e.tensor_scalar(out=ev[2][:], in0=psn[k][:, 0:256], scalar1=wkc[:, kt:kt + 1], scalar2=None,
                                                              op0=ALU.mult), r=["psn%d" % k, "wkc"], w=["evB"])
                        P.op("dve", lambda e: e.tensor_copy(out=ev[3][:], in_=psn[k][:, 256:512]), r=["psn%d" % k], w=["evKi"])
                        P.op("pool", lambda e: e.tensor_tensor(out=tq[0][:], in0=ev[0][:], in1=ev[1][:], op=ALU.mult), r=["evA", "evKr"], w=["tq1"])
                        P.op("pool", lambda e: e.tensor_tensor(out=tq[1][:], in0=ev[2][:], in1=ev[3][:], op=ALU.mult), r=["evB", "evKi"], w=["tq2"])
                        P.op("pool", lambda e: e.tensor_tensor(out=Ybuf[:, kt, 0, :], in0=tq[0][:], in1=tq[1][:], op=ALU.add),
                             r=["tq1", "tq2"], w=["Ybuf"])
                        P.op("dve", lambda e: e.tensor_tensor(out=tq[2][:], in0=ev[0][:], in1=ev[3][:], op=ALU.mult), r=["evA", "evKi"], w=["tq3"])
                        P.op("dve", lambda e: e.tensor_tensor(out=tq[3][:], in0=ev[2][:], in1=ev[1][:], op=ALU.mult), r=["evB", "evKr"], w=["tq4"])
                        P.op("dve", lambda e: e.tensor_tensor(out=Ybuf[:, kt, 1, :], in0=tq[3][:], in1=tq[2][:], op=ALU.subtract),
                             r=["tq3", "tq4"], w=["Ybuf"])
            with Scope():
                tic = [sb("tic0", [128, 2, KT, 128], BF16), sb("tic1", [128, 2, KT, 128], BF16)]
                tis = [sb("tis0", [128, 2, KT, 128], BF16), sb("tis1", [128, 2, KT, 128], BF16)]
                x0b = [sb("x0b0", [128, 2, 256]), sb("x0b1", [128, 2, 256])]
                vxs = [sb("vxs0", [128, 2, 256]), sb("vxs1", [128, 2, 256])]
                otmp = sb("otmp", [128, 256])
                YHo = sb("YHo", [128, 2, L], BF16)
                po = [ps("po0", [128, 512]), ps("po1", [128, 512]), ps("po2", [128, 512]), ps("po3", [128, 512])]
                for g in range(16):
                    k = g % 2
                    for i2 in range(2):
                        P.dma("sp", tic[k][:, i2], t2cos[2 * g + i2], w=["tic%d" % k])
                        P.dma("sp", tis[k][:, i2], t2sin[2 * g + i2], w=["tis%d" % k])
                    P.dma("sp", x0b[k][:], X0S[:, 256 * g:256 * (g + 1)].rearrange("(j p) t -> p j t", p=128), r=["X0S"], w=["x0b%d" % k])
                    P.dma("sp", vxs[k][:], VXS[:, 256 * g:256 * (g + 1)].rearrange("(j p) t -> p j t", p=128), r=["VXS"], w=["vxs%d" % k])
                    for j in range(2):
                        pq = po[2 * k + j]
                        pk = "po%d" % (2 * k + j)
                        for kt in range(KT):
                            P.op("pe", lambda e: e.matmul(pq[:, 0:256], lhsT=Ybuf[:, kt, 0, 128 * j:128 * (j + 1)], rhs=tic[k][:, :, kt, :],
                                                          start=(kt == 0), stop=False), r=["Ybuf", "tic%d" % k], w=[pk])
                            P.op("pe", lambda e: e.matmul(pq[:, 0:256], lhsT=Ybuf[:, kt, 1, 128 * j:128 * (j + 1)], rhs=tis[k][:, :, kt, :],
                                                          start=False, stop=(kt == KT - 1)), r=["Ybuf", "tis%d" % k], w=[pk])
                        P.op("dve", lambda e: e.scalar_tensor_tensor(out=otmp[:], in0=vxs[k][:, j, :], scalar=hbq[:, j:j + 1],
                                                                     in1=pq[:, 0:256], op0=ALU.mult, op1=ALU.add),
                             r=["vxs%d" % k, "hbq", pk], w=["otmp"])
                        P.op("pool", lambda e: e.tensor_tensor(out=YHo[:, j, 256 * g:256 * (g + 1)], in0=otmp[:], in1=x0b[k][:, j, :],
                                                               op=ALU.mult), r=["otmp", "x0b%d" % k], w=["YHo"])
                    if g in (7, 15) and stop != "pB2":
                        c2 = g // 8
                        for tq_ in (2 * c2, 2 * c2 + 1):
                            P.dma("sp", YH[256 * tq_:256 * (tq_ + 1), :].rearrange("(j p) t -> p j t", p=128),
                                  YHo[:, :, 1024 * tq_:1024 * (tq_ + 1)], r=["YHo"], w=["YH%d" % c2])
                        P.coll("AllGather", GRP, YH[512 * c2:512 * (c2 + 1), :], GH[2048 * c2:2048 * (c2 + 1), :],
                               r=["YH%d" % c2], w=["GH"], sk="gh")
                if stop == "pB2":
                    yf = sb("yf", [128, 2, L])
                    P.op("dve", lambda e: e.tensor_copy(out=yf[:], in_=YHo[:]), r=["YHo"], w=["yf"])
                    d1 = dbg_out("yhy", [256, L])
                    P.dma("sp", d1.rearrange("(j p) t -> p j t", p=128), yf[:], r=["yf"], w=["dbgo"])
                    P.final_wait("sp", ["dbgo"])
        if stop in ("pB2", "pB2f"):
            return nc, dbg, in_names

        gidx = sb("gidx_sb", [128, 56], I32)
        P.dma("sp", gidx[:], gidx_in[:, :], w=["gidx"])

        def make_norm(tag, nslot=1):
            xts_ = [sb(tag + "xt%d" % j_, [128, D]) for j_ in range(nslot)]
            posr_ = sb(tag + "posr", [128, 1024]) if nslot > 1 else None
            xt_ = xts_[0]; junk_ = sb(tag + "junk", [128, D], BF16); ss_ = sb(tag + "ss", [128, 1])
            rs_ = sb(tag + "rs", [128, 1]); xn_ = sb(tag + "xn", [128, D], BF16); mt_ = sb(tag + "mt", [128, NCH, 128])
            tp_ = ps(tag + "tp", [128, NCH, 128], BF16)

            def fn(src_dram, Gm, Sm, hT_dst, hkey, gcol=None, slot=0):
                xt_ = xts_[slot]
                xkey = tag + "xt" + ("%d" % slot if nslot > 1 else "")
                if gcol is not None:
                    P.gather(xt_[:], src_dram, gidx[:, gcol:gcol + 1], r=["XP", "gidx"], w=[xkey])
                    P.gather(posr_[:], POS[:, :], gidx[:, 48 + gcol:49 + gcol], r=["POS", "gidx"], w=[tag + "posr"])
                    P.op("pool", lambda e: e.tensor_tensor(out=xt_[:, 0:1024], in0=xt_[:, 0:1024], in1=posr_[:], op=ALU.add),
                         r=[tag + "posr"], w=[xkey])
                    P.op("dve", lambda e: e.tensor_tensor(out=xt_[:, 1024:D], in0=xt_[:, 1024:D], in1=poscol[:], op=ALU.add),
                         r=["poscol"], w=[xkey])
                else:
                    P.dma("sp", xt_[:], src_dram, r=["XP", "X1S"], w=[xkey])
                P.op("dve", lambda e: e.memset(ss_[:], 0.0), w=[tag + "ss"])
                P.op("act", lambda e: e.activation(out=junk_[:], in_=xt_[:], func=AF.Square, accum_out=ss_[:]),
                     r=[xkey], w=[tag + "ss", tag + "junk"])
                P.op("act", lambda e: e.activation(out=rs_[:], in_=ss_[:], func=AF.Sqrt, scale=1.0 / D, bias=epsc[:, 0:1]),
                     r=[tag + "ss", "epsc"], w=[tag + "rs"])
                P.op("dve", lambda e: e.reciprocal(out=rs_[:], in_=rs_[:]), r=[tag + "rs"], w=[tag + "rs"])
                P.op("dve", lambda e: e.tensor_scalar(out=xn_[:], in0=xt_[:], scalar1=rs_[:, 0:1], scalar2=None, op0=ALU.mult),
                     r=[xkey, tag + "rs"], w=[tag + "xn"])
                xnv = xn_[:].rearrange("p (j c) -> p c j", c=NCH)
                for c in range(NCH):
                    P.op("pe", lambda e: e.transpose(out=tp_[:, c, :], in_=xnv[:, c, :], identity=identb[:]),
                         r=[tag + "xn", "identb"], w=[tag + "tp"])
                P.op("dve", lambda e: e.tensor_tensor(out=mt_[:], in0=tp_[:], in1=bc_last(Gm[:], 128), op=ALU.mult),
                     r=[tag + "tp", Gm.name], w=[tag + "mt"])
                P.op("pool", lambda e: e.tensor_tensor(out=hT_dst, in0=mt_[:], in1=bc_last(Sm[:], 128), op=ALU.add),
                     r=[tag + "mt", Sm.name], w=[hkey])
            return fn, (xts_ if nslot > 1 else xt_), junk_

        with Scope():
            gp1 = sb("gp1", [128, D]); gtmp = sb("gtmp", [128, D])
            mod_bc(gp1, 0, 2)
            P.dma("sp", gtmp[:], nrm_post1.partition_broadcast(128), w=["gtmp"])
            P.op("pool", lambda e: e.tensor_tensor(out=gp1[:], in0=gp1[:], in1=gtmp[:], op=ALU.mult), r=["gp1", "gtmp"], w=["gp1"])
            snrm = sb("snrm", [128, 1024])
            P.dma("sp", snrm[:], ssd_nrm.partition_broadcast(128), w=["snrm"])
            wz = sb("wz", [128, NCH, 1024], BF16)
            w_z_v = w_z.rearrange("(p c) n -> p c n", c=NCH)
            for j in range(4):
                P.dma("pool", wz[:, :, 256 * j:256 * (j + 1)], w_z_v[:, :, 256 * j:256 * (j + 1)], w=["wz"])
            yhT = sb("yhT", [128, 8, 1024], BF16)
            for src_ in range(4):
                for jj in range(2):
                    P.gather(yhT[:, 2 * src_ + jj, :], GH[:, :], gidx[:, 40 + 2 * src_ + jj:41 + 2 * src_ + jj],
                             r=["GH", "gidx"], w=["yhT"])
            normC, xtC, junkC = make_norm("c1", nslot=2)
            hTc = sb("hTc", [128, NCH, 128], BF16)
            zs = sb("zs", [128, 1024]); ysb = [sb("ysb%d" % j_, [128, 256]) for j_ in range(4)]; gg = sb("gg", [128, 1024]); gnb = sb("gnb", [128, 1024], BF16)
            ymT = [sb("ymT0", [128, 8, 128], BF16), sb("ymT1", [128, 8, 128], BF16)]
            ss2 = sb("ss2", [128, 4]); rs2 = sb("rs2", [128, 1]); rsA = sb("rsA", [128, 1])
            x1t = sb("x1t", [128, D]); junk2 = sb("junk2", [128, 1024], BF16)
            pz = [ps("pz0", [128, 512]), ps("pz1", [128, 512])]
            pym = ps("pym", [128, 8, 128], BF16)
            pwo = [ps("pwo0", [128, 512]), ps("pwo1", [128, 512])]

            def stageA(i):
                sl_ = i % 2
                normC(x_b[:, :], G1, sh1, hTc[:], "hTc", gcol=i, slot=sl_)
                yield
                for nb in range(2):
                    for c in range(NCH):
                        P.op("pe", lambda e: e.matmul(pz[nb][:], lhsT=hTc[:, c, :], rhs=wz[:, c, 512 * nb:512 * (nb + 1)],
                                                      start=(c == 0), stop=(c == NCH - 1)), r=["hTc", "wz"], w=["pz%d" % nb])
                    P.op("act", lambda e: e.activation(out=zs[:, 512 * nb:512 * (nb + 1)], in_=pz[nb][:], func=AF.Silu),
                         r=["pz%d" % nb], w=["zs"])
                for src_ in range(4):
                    P.gather(ysb[src_][:], GS[:, :], gidx[:, 8 + 4 * i + src_:9 + 4 * i + src_],
                             r=["GS", "gidx"], w=["ysb%d" % src_])
                    P.op("dve", lambda e: e.tensor_tensor(out=gg[:, 256 * src_:256 * (src_ + 1)], in0=ysb[src_][:],
                                                          in1=zs[:, 256 * src_:256 * (src_ + 1)], op=ALU.mult),
                         r=["ysb%d" % src_, "zs"], w=["gg"])
                P.op("dve", lambda e: e.memset(rsA[:], 0.0), w=["rsA"])
                P.op("act", lambda e: e.activation(out=junkC[:, 0:1024], in_=gg[:], func=AF.Square, accum_out=rsA[:]),
                     r=["gg"], w=["rsA", "c1junk"])
                P.op("act", lambda e: e.activation(out=rsA[:], in_=rsA[:], func=AF.Sqrt, scale=1.0 / 1024, bias=epsc[:, 0:1]),
                     r=["rsA", "epsc"], w=["rsA"])
                P.op("dve", lambda e: e.reciprocal(out=rsA[:], in_=rsA[:]), r=["rsA"], w=["rsA"])
                P.op("dve", lambda e: e.scalar_tensor_tensor(out=gnb[:], in0=gg[:], scalar=rsA[:, 0:1], in1=snrm[:],
                                                             op0=ALU.mult, op1=ALU.mult), r=["gg", "rsA", "snrm"], w=["gnb"])
                yield
                for c in range(8):
                    P.op("pe", lambda e: e.transpose(out=pym[:, c, :], in_=gnb[:, 128 * c:128 * (c + 1)], identity=identb[:]),
                         r=["gnb", "identb"], w=["pym"])
                P.op("act", lambda e: e.copy(out=ymT[sl_][:], in_=pym[:]), r=["pym"], w=["ymT%d" % sl_])
                yield

            def stageB(i):
                sl_ = i % 2
                P.op("dve", lambda e: e.memset(ss2[:], 0.0), w=["ss2"])
                for nb in range(4):
                    k = nb % 2
                    for c in range(NCH):
                        lhs = ymT[sl_][:, c, :] if c < 8 else yhT[:, c - 8, 128 * i:128 * (i + 1)]
                        P.op("pe", lambda e: e.matmul(pwo[k][:], lhsT=lhs, rhs=wo[:, c, 512 * nb:512 * (nb + 1)],
                                                      start=(c == 0), stop=(c == NCH - 1)), r=["ymT%d" % sl_, "yhT", "wo"], w=["pwo%d" % k])
                    P.op("act", lambda e: e.activation(out=junk2[:, 0:512], in_=pwo[k][:], func=AF.Square, accum_out=ss2[:, nb:nb + 1]),
                         r=["pwo%d" % k], w=["ss2", "junk2"])
                    P.op("dve", lambda e: e.tensor_tensor(out=x1t[:, 512 * nb:512 * (nb + 1)], in0=pwo[k][:],
                                                          in1=gp1[:, 512 * nb:512 * (nb + 1)], op=ALU.mult),
                         r=["pwo%d" % k, "gp1", "junk2"], w=["x1t"])
                    yield
                P.op("dve", lambda e: e.tensor_reduce(out=rs2[:], in_=ss2[:], axis=AX.X, op=ALU.add), r=["ss2"], w=["rs2"])
                P.op("act", lambda e: e.activation(out=rs2[:], in_=rs2[:], func=AF.Sqrt, scale=1.0 / D, bias=epsc[:, 0:1]),
                     r=["rs2", "epsc"], w=["rs2"])
                P.op("dve", lambda e: e.reciprocal(out=rs2[:], in_=rs2[:]), r=["rs2"], w=["rs2"])
                P.op("dve", lambda e: e.scalar_tensor_tensor(out=x1t[:], in0=x1t[:], scalar=rs2[:, 0:1], in1=xtC[sl_][:],
                                                             op0=ALU.mult, op1=ALU.add), r=["x1t", "rs2", "c1xt%d" % sl_], w=["x1t"])
                P.dma("sp", X1S[128 * i:128 * (i + 1), :], x1t[:], r=["x1t"], w=["X1S"])

            for _ in stageA(0):
                pass
            for i in range(8):
                ga = stageA(i + 1) if i + 1 < 8 else iter(())
                gb = stageB(i)
                next(ga, None)
                next(gb, None)
                next(ga, None)
                next(gb, None)
                next(gb, None)
                next(ga, None)
                for _ in gb:
                    pass
                for _ in ga:
                    pass
            if stop == "pC1":
                d1 = dbg_out("x1", [1024, D])
                P.dma("sp", d1, X1S[:, :], r=["X1S"], w=["dbgo"])
                P.final_wait("sp", ["dbgo"])
        if stop == "pC1":
            return nc, dbg, in_names
        right_stack.close()

        with Scope():
            gp2 = sb("gp2", [128, D])
            with Scope():
                gtmp2 = sb("gtmp2", [128, D])
                mod_bc(gp2, 0, 5)
                P.dma("sp", gtmp2[:], nrm_post2.partition_broadcast(128), w=["gtmp2"])
                P.op("pool", lambda e: e.tensor_tensor(out=gp2[:], in0=gp2[:], in1=gtmp2[:], op=ALU.mult), r=["gp2", "gtmp2"], w=["gp2"])
            normF, xtF, junkF = make_norm("c2")
            h2T = sb("h2T", [128, NCH, 512], BF16)
            AT = sb("AT", [128, NFF, 512], BF16)
            wg = [sb("wg0", [128, NCH, 256], BF16), sb("wg1", [128, NCH, 256], BF16)]
            wu = [sb("wu0", [128, NCH, 256], BF16), sb("wu1", [128, NCH, 256], BF16)]
            wd = [sb("wd0", [128, NFF, 256], BF16), sb("wd1", [128, NFF, 256], BF16)]
            fo = sb("fo", [128, 4, D]); act_ = sb("act_", [128, 512])
            ss3 = sb("ss3", [128, 1]); rs3 = sb("rs3", [128, 1])
            pg = [ps("pg0", [128, 512]), ps("pg1", [128, 512])]
            pu = [ps("pu0", [128, 512]), ps("pu1", [128, 512])]
            pdn = pg
            w_gate_v = w_gate.rearrange("(p c) n -> p c n", c=NCH)
            w_up_v = w_up.rearrange("(p c) n -> p c n", c=NCH)
            w_down_v = w_down.rearrange("(c p) n -> p c n", p=128)
            for half in range(2):
                if half == 0:
                    for t in range(4):
                        normF(X1S[128 * t:128 * (t + 1), :], G2, sh2, h2T[:, :, 128 * t:128 * (t + 1)], "h2T")
                for gi in range(NFF // 2):
                    k = gi % 2
                    P.dma("pool", wg[k][:], w_gate_v[:, :, 256 * gi:256 * (gi + 1)], w=["wg%d" % k])
                    P.dma("pool", wu[k][:], w_up_v[:, :, 256 * gi:256 * (gi + 1)], w=["wu%d" % k])
                    for sub in range(2):
                        fc = 2 * gi + sub
                        pk = fc % 2
                        for c in range(NCH):
                            P.op("pe", lambda e: e.matmul(pg[pk][:], lhsT=wg[k][:, c, 128 * sub:128 * (sub + 1)], rhs=h2T[:, c, :],
                                                          start=(c == 0), stop=(c == NCH - 1)), r=["wg%d" % k, "h2T"], w=["pg%d" % pk])
                        for c in range(NCH):
                            P.op("pe", lambda e: e.matmul(pu[pk][:], lhsT=wu[k][:, c, 128 * sub:128 * (sub + 1)], rhs=h2T[:, c, :],
                                                          start=(c == 0), stop=(c == NCH - 1)), r=["wu%d" % k, "h2T"], w=["pu%d" % pk])
                        P.op("act", lambda e: e.activation(out=act_[:], in_=pg[pk][:], func=AF.Silu), r=["pg%d" % pk], w=["act_"])
                        P.op("dve", lambda e: e.tensor_tensor(out=AT[:, fc, :], in0=act_[:], in1=pu[pk][:], op=ALU.mult),
                             r=["act_", "pu%d" % pk], w=["AT"])
                for nbh in range(8):
                    k = nbh % 2
                    P.dma("pool", wd[k][:], w_down_v[:, :, 256 * nbh:256 * (nbh + 1)], w=["wd%d" % k])
                    if half == 0 and nbh % 2 == 1:
                        t_ = nbh // 2
                        normF(X1S[128 * (4 + t_):128 * (5 + t_), :], G2, sh2, h2T[:, :, 128 * t_:128 * (t_ + 1)], "h2T")
                    for t in range(4):
                        pk = (nbh * 4 + t) % 2
                        for fc in range(NFF):
                            P.op("pe", lambda e: e.matmul(pdn[pk][:, 0:256], lhsT=AT[:, fc, 128 * t:128 * (t + 1)], rhs=wd[k][:, fc, :],
                                                          start=(fc == 0), stop=(fc == NFF - 1)), r=["AT", "wd%d" % k], w=["pg%d" % pk])
                        P.op("act", lambda e: e.copy(out=fo[:, t, 256 * nbh:256 * (nbh + 1)], in_=pdn[pk][:, 0:256]),
                             r=["pg%d" % pk], w=["fo"])
                for t in range(4):
                    i = 4 * half + t
                    P.op("dve", lambda e: e.memset(ss3[:], 0.0), w=["ss3"])
                    P.op("act", lambda e: e.activation(out=junkF[:], in_=fo[:, t, :], func=AF.Square, accum_out=ss3[:]),
                         r=["fo"], w=["ss3", "c2junk"])
                    P.op("act", lambda e: e.activation(out=rs3[:], in_=ss3[:], func=AF.Sqrt, scale=1.0 / D, bias=epsc[:, 0:1]),
                         r=["ss3", "epsc"], w=["rs3"])
                    P.op("dve", lambda e: e.reciprocal(out=rs3[:], in_=rs3[:]), r=["rs3"], w=["rs3"])
                    P.dma("sp", xtF[:], X1S[128 * i:128 * (i + 1), :], r=["X1S"], w=["c2xt"])
                    P.op("pool", lambda e: e.tensor_tensor(out=fo[:, t, :], in0=fo[:, t, :], in1=gp2[:], op=ALU.mult),
                         r=["fo", "gp2"], w=["fo"])
                    P.op("dve", lambda e: e.scalar_tensor_tensor(out=xtF[:], in0=fo[:, t, :], scalar=rs3[:, 0:1], in1=xtF[:],
                                                                 op0=ALU.mult, op1=ALU.add), r=["fo", "rs3", "c2xt"], w=["c2xt"])
                    P.dma("sp", out_own[128 * i:128 * (i + 1), :], xtF[:], r=["c2xt"], w=["OUT"])
            P.final_wait("sp", ["OUT"])

        P.emit()
    return nc, dbg, in_names


CONST = {}


def make_consts():
    if CONST:
        return
    f = np.float32
    a = np.arange(KT * 128, dtype=np.int64)
    prod = (a[:, None] * a[None, :]) % NDFT
    ang = prod.astype(np.float64) * (2.0 * np.pi / NDFT)
    def lay(mat):
        m4 = mat.reshape(KT, 128, KT, 128)
        return np.ascontiguousarray(m4.transpose(2, 1, 0, 3)).astype(ml_dtypes.bfloat16)
    CONST["t2cos"] = lay(np.cos(ang)); CONST["t2sin"] = lay(np.sin(ang))
    CONST["irow"] = np.arange(L, dtype=f)[None, :]
    fb = np.linspace(1e-4, 15.0, 16, dtype=f)
    fbph = np.zeros((33, 2), f)
    fbph[1:17, 0] = fb; fbph[17:33, 0] = fb
    fbph[1:17, 1] = np.pi / 2.0; fbph[17:33, 1] = np.pi
    CONST["fbph"] = fbph
    k = np.arange(KT * 128)
    wk = np.where(k <= 4096, 2.0, 0.0); wk[0] = 1.0; wk[4096] = 1.0
    CONST["wk"] = np.ascontiguousarray((wk / NDFT).astype(f).reshape(KT, 128).T)
    max_decay = math.log(1e-2) / 0.3; min_decay = math.log(1e-2) / 1.5
    CONST["delta"] = np.abs(np.linspace(min_decay, max_decay, 1024, dtype=f))


def host_inputs(inp):
    make_consts()
    f = np.float32
    x = np.asarray(inp["x"], f); c = np.asarray(inp["c"], f); ctx = np.asarray(inp["ctx"], f)
    c_ctx = np.asarray(inp["c_ctx"], f)
    w_ada = np.asarray(inp["w_ada"], f)[0]; b_ada = np.asarray(inp["b_ada"], f)[0]
    maps = []
    ident = np.eye(128, dtype=f)
    jrow = np.arange(512, dtype=f)[None, :]
    kcol = np.arange(128, dtype=f)[:, None]
    for core in range(8):
        b, q = core // 4, core % 4
        cvv = np.stack([c[b], c_ctx], axis=-1).reshape(128, NCH, 2)
        m = {
            "x_b": x[b], "ctx_b": ctx[b], "cv": np.ascontiguousarray(cvv),
            "w_ada_q": np.ascontiguousarray(w_ada[:, 3072 * q:3072 * (q + 1)]),
            "b_ada_q": np.ascontiguousarray(b_ada[None, 3072 * q:3072 * (q + 1)]),
            "nrm_pre1": np.asarray(inp["norm_mix_pre"], f)[0].reshape(128, NCH),
            "nrm_pre2": np.asarray(inp["norm_ffn_pre"], f)[0].reshape(128, NCH),
            "nrm_post1": np.asarray(inp["norm_mix_post"], f)[0][None, :],
            "nrm_post2": np.asarray(inp["norm_ffn_post"], f)[0][None, :],
            "c_ident": ident, "c_jrow": jrow, "c_kcol": kcol,
        }
        g = q // 2
        w_in = np.asarray(inp["w_in"], f)[0]
        cols = np.concatenate([
            1024 + 256 * q + np.arange(256), 2048 + 128 * g + np.arange(128), 2304 + 128 * g + np.arange(128),
            2592 + 256 * q + np.arange(256), 3616 + 256 * q + np.arange(256), 4640 + 256 * q + np.arange(256),
            2560 + 4 * q + np.arange(4), 2576 + 4 * q + np.arange(4)])
        m["w_in_q"] = np.ascontiguousarray(w_in[:, cols])
        scw = np.asarray(inp["ssd_conv_w"], f)[0]; scb = np.asarray(inp["ssd_conv_b"], f)[0]
        ccols = np.concatenate([256 * q + np.arange(256), 1024 + 128 * g + np.arange(128), 1280 + 128 * g + np.arange(128)])
        m["ssd_cw"] = np.ascontiguousarray(scw[:, ccols].T.reshape(4, 128, 3).transpose(1, 0, 2))
        m["ssd_cb"] = np.ascontiguousarray(scb[ccols].reshape(4, 128).T)
        hs = 4 * q + np.arange(4)
        m["ssd_alog"] = np.asarray(inp["ssd_a_log"], f)[0][:, hs].reshape(1, 8)
        m["ssd_dtb"] = np.asarray(inp["ssd_dt_bias"], f)[0][:, hs].reshape(1, 8)
        m["ssd_Dq"] = np.asarray(inp["ssd_d"], f)[0][hs].reshape(1, 4)
        gi = np.zeros((128, 56), np.int32)
        pp = np.arange(128)
        for i_ in range(8):
            gi[:, i_] = q * 1024 + 128 * i_ + pp
            for s_ in range(4):
                gi[:, 8 + 4 * i_ + s_] = q * 4096 + s_ * 1024 + 128 * i_ + pp
        for s_ in range(4):
            for j_ in range(2):
                gi[:, 40 + 2 * s_ + j_] = (q // 2) * 2048 + (q % 2) * 256 + s_ * 512 + j_ * 128 + pp
        for i_ in range(8):
            gi[:, 48 + i_] = 16 * q + 2 * i_ + pp // 64
        m["gidx"] = gi
        m["w_z"] = np.ascontiguousarray(w_in[:, 0:1024])
        m["ssd_nrm"] = np.asarray(inp["ssd_norm"], f)[0][None, :]
        m["w_out"] = np.asarray(inp["w_out"], f)[0]
        m["w_gate"] = np.asarray(inp["w_gate"], f)[0]; m["w_up"] = np.asarray(inp["w_up"], f)[0]
        m["w_down"] = np.asarray(inp["w_down"], f)[0]
        hcw = np.asarray(inp["hy_conv_w"], f)[0]; hcb = np.asarray(inp["hy_conv_b"], f)[0]
        hcols = np.concatenate([1024 * a + 256 * q + np.arange(256) for a in range(3)])
        m["hy_cw"] = np.ascontiguousarray(hcw[:, hcols].T.reshape(6, 128, 3).transpose(1, 0, 2))
        m["hy_cb"] = np.ascontiguousarray(hcb[hcols].reshape(6, 128).T)
        m["hy_w1"] = np.asarray(inp["hy_w1"], f)[0]; m["hy_w2"] = np.asarray(inp["hy_w2"], f)[0]; m["hy_w3"] = np.asarray(inp["hy_w3"], f)[0]
        m["hy_b123"] = np.stack([np.asarray(inp["hy_b1"], f)[0], np.asarray(inp["hy_b2"], f)[0], np.asarray(inp["hy_b3"], f)[0]], axis=1)
        m["hy_fr"] = np.asarray(inp["hy_freq"], f)[0][:, None]
        w4 = np.asarray(inp["hy_w4"], f)[0]
        m["hy_w4q"] = np.ascontiguousarray(np.concatenate([w4[:, 256 * q:256 * (q + 1)], w4[:, 1024 + 256 * q:1024 + 256 * (q + 1)]], axis=1))
        m["hy_biasq"] = np.ascontiguousarray(np.asarray(inp["hy_bias"], f)[0][256 * q:256 * (q + 1)].reshape(2, 128).T)
        m["c_delta"] = np.ascontiguousarray(CONST["delta"][256 * q:256 * (q + 1)].reshape(2, 128).T)
        m["c_irow"] = CONST["irow"]; m["c_fbph"] = CONST["fbph"]; m["c_wk"] = CONST["wk"]
        m["t2cos"] = CONST["t2cos"]; m["t2sin"] = CONST["t2sin"]
        ii = np.arange(128)
        m["c_trif"] = (ii[:, None] <= ii[None, :]).astype(f)
        m["c_trib"] = (ii[:, None] >= ii[None, :]).astype(f)
        m["c_negf"] = np.where(ii[None, :] >= ii[:, None], 0.0, NEG).astype(f)
        m["c_negb"] = np.where(ii[None, :] <= ii[:, None], 0.0, NEG).astype(f)
        maps.append(m)
    return maps


def run(inp, stop="all"):
    nc, dbg, in_names = build(stop)
    maps = [{k: np.ascontiguousarray(m[k]) for k in in_names} for m in host_inputs(inp)]
    res = run_bass_kernel_spmd(nc, maps, core_ids=list(range(8)))
    return res


def kernel(**inputs):
    res = run(inputs, "all")
    out = np.zeros((2, L, D), np.float32)
    for core in range(8):
        b, q = core // 4, core % 4
        out[b, 1024 * q:1024 * (q + 1)] = res.results[core]["out_own"]
    return out
```

```python
import math
import numpy as np
import ml_dtypes
import concourse.bass as bass
import concourse.mybir as mybir
from concourse.bass_utils import run_bass_kernel_spmd

F32 = mybir.dt.float32
BF16 = mybir.dt.bfloat16
I32 = mybir.dt.int32
AF = mybir.ActivationFunctionType
ALU = mybir.AluOpType
AX = mybir.AxisListType

D = 2048
L = 4096
LC = 256
NCH = 16
DFF = 5632
NFF = 44
NDFT = 8192
KT = 33
EPS = 1e-6
NEG = -30000.0
TWO_PI = 2.0 * math.pi

DEBUG = {}


class Prog:
    def __init__(self, nc):
        self.nc = nc
        self.engs = {"pe": nc.tensor, "act": nc.scalar, "dve": nc.vector, "pool": nc.gpsimd, "sp": nc.sync}
        self.q = {k: [] for k in self.engs}
        self.sems = {}
        self.cnt = {}
        self.seen = {k: {} for k in self.engs}
        self.W = {}
        self.R = {}
        self.free = []
        self.pcount = {}
        self.phys = {}
        self.live = []
        self.gen = 0
        for k in ("pe", "act", "dve", "pool"):
            self._sem("e_" + k)

    def _sem(self, key):
        if key not in self.sems:
            if key.startswith("e_") or key.startswith("c_") or key.startswith("q_") or not self.free:
                phys = self.nc.alloc_semaphore("s%d" % len(self.pcount))
                self.pcount[id(phys)] = 0
                self.phys[id(phys)] = phys
            else:
                phys = self.free.pop()
            self.sems[key] = phys
            self.cnt[key] = self.pcount[id(phys)]
            if not (key.startswith("e_") or key.startswith("c_") or key.startswith("q_")):
                self.live.append(key)
        return self.sems[key]

    def _bump(self, key, inc):
        self.cnt[key] += inc
        self.pcount[id(self.sems[key])] = self.cnt[key]
        return self.cnt[key]

    def _deps(self, eng, r, w):
        need = {}
        for b in r:
            for sk, v in self.W.get(b, {}).items():
                need[sk] = max(need.get(sk, 0), v)
        for b in w:
            for sk, v in self.W.get(b, {}).items():
                need[sk] = max(need.get(sk, 0), v)
            for sk, v in self.R.get(b, {}).items():
                need[sk] = max(need.get(sk, 0), v)
        waits = []
        for sk, v in need.items():
            if eng == "pe" and sk == "e_pe":
                continue
            if self.seen[eng].get(sk, 0) < v:
                self.seen[eng][sk] = v
                waits.append((sk, v))
        return waits

    def _mark(self, sk, v, r, w):
        for b in r:
            d = self.R.setdefault(b, {})
            d[sk] = max(d.get(sk, 0), v)
        for b in w:
            d = self.W.setdefault(b, {})
            d[sk] = max(d.get(sk, 0), v)

    def op(self, eng, fn, r=(), w=()):
        waits = self._deps(eng, r, w)
        sk = "e_" + eng
        v = self._bump(sk, 1)
        self._emit(eng, waits, fn, sk, 1)
        self._mark(sk, v, r, w)

    def _emit(self, eng, waits, fn, sk, inc):
        e = self.engs[eng]
        for wsk, v in waits:
            e.wait_ge(self.sems[wsk], v)
        if fn is not None:
            fn(e).then_inc(self.sems[sk], inc)

    def fence(self):
        for eng in self.engs:
            waits = []
            for sk, v in self.cnt.items():
                if sk.startswith("c_"):
                    continue
                if v > 0 and self.seen[eng].get(sk, 0) < v and not (sk == "e_" + eng):
                    self.seen[eng][sk] = v
                    waits.append((sk, v))
            self._emit(eng, waits, None, None, 0)
        for key in self.live:
            self.free.append(self.sems[key])
        self.live = []
        self.gen += 1

    def dma(self, eng, out, in_, r=(), w=(), sk=None, **kw):
        waits = self._deps(eng, r, w)
        base = sk if sk is not None else (w[0] if w else r[0])
        sk = ("q_" + base) if eng == "pool" else ("d_" + base + "@%d" % self.gen)
        self._sem(sk)
        v = self._bump(sk, 16)
        self._emit(eng, waits, (lambda e: e.dma_start(out=out, in_=in_, **kw)), sk, 16)
        self._mark(sk, v, r, w)

    def gather(self, out, in_, idx, r=(), w=(), sk=None):
        waits = self._deps("pool", r, w)
        sk = "q_" + (sk if sk is not None else w[0])
        self._sem(sk)
        v = self._bump(sk, 16)
        self._emit("pool", waits, (lambda e: e.indirect_dma_start(
            out=out, out_offset=None, in_=in_, in_offset=bass.IndirectOffsetOnAxis(ap=idx, axis=0))), sk, 16)
        self._mark(sk, v, r, w)

    def coll(self, kind, groups, in_, out, r=(), w=(), sk="coll"):
        waits = self._deps("pool", r, w)
        sk = "c_" + sk + "@%d" % self.gen
        self._sem(sk)
        v = self._bump(sk, 1)
        self._emit("pool", waits, (lambda e: e.collective_compute(kind, ALU.bypass, replica_groups=groups,
                                                                   ins=[in_], outs=[out])), sk, 1)
        self._mark(sk, v, r, w)

    def final_wait(self, eng, bufs):
        waits = self._deps(eng, bufs, bufs)
        self._emit(eng, waits, None, None, 0)

    def emit(self):
        pass


def bc_last(ap, n):
    return ap.unsqueeze(2).to_broadcast([ap.shape[0], ap.shape[1], n])


def build(stop="all"):
    nc = bass.Bass("TRN2", target_bir_lowering=False)
    P = Prog(nc)
    GRP = [[0, 1, 2, 3], [4, 5, 6, 7]]

    in_names = []

    def din(name, shape, dt=F32):
        in_names.append(name)
        return nc.dram_tensor(name, list(shape), dt, kind="ExternalInput").ap()

    def dout(name, shape, dt=F32):
        return nc.dram_tensor(name, list(shape), dt, kind="ExternalOutput").ap()

    def dscr(name, shape, dt=F32):
        return nc.dram_tensor(name, list(shape), dt).ap()

    x_b = din("x_b", [L, D])
    ctx_b = din("ctx_b", [LC, D])
    cv = din("cv", [128, NCH, 2])
    w_ada_q = din("w_ada_q", [D, 3072])
    b_ada_q = din("b_ada_q", [1, 3072])
    nrm_pre1 = din("nrm_pre1", [128, NCH])
    nrm_pre2 = din("nrm_pre2", [128, NCH])
    nrm_post1 = din("nrm_post1", [1, D])
    nrm_post2 = din("nrm_post2", [1, D])
    c_ident = din("c_ident", [128, 128])
    c_jrow = din("c_jrow", [1, 512])
    c_kcol = din("c_kcol", [128, 1])
    w_in_q = din("w_in_q", [D, 1288])
    ssd_cw = din("ssd_cw", [128, 4, 3])
    ssd_cb = din("ssd_cb", [128, 4])
    ssd_alog = din("ssd_alog", [1, 8])
    ssd_dtb = din("ssd_dtb", [1, 8])
    ssd_Dq = din("ssd_Dq", [1, 4])
    gidx_in = din("gidx", [128, 56], I32)
    w_z = din("w_z", [D, 1024])
    ssd_nrm = din("ssd_nrm", [1, 1024])
    w_out = din("w_out", [D, D])
    w_gate = din("w_gate", [D, DFF])
    w_up = din("w_up", [D, DFF])
    w_down = din("w_down", [DFF, D])
    hy_cw = din("hy_cw", [128, 6, 3])
    hy_cb = din("hy_cb", [128, 6])
    hy_w1 = din("hy_w1", [33, 64]); hy_w2 = din("hy_w2", [64, 64]); hy_w3 = din("hy_w3", [64, 64])
    hy_b123 = din("hy_b123", [64, 3]); hy_fr = din("hy_fr", [64, 1])
    hy_w4q = din("hy_w4q", [64, 512])
    hy_biasq = din("hy_biasq", [128, 2])
    c_delta = din("c_delta", [128, 2])
    c_irow = din("c_irow", [1, L])
    c_fbph = din("c_fbph", [33, 2])
    c_wk = din("c_wk", [128, KT])
    t2cos = din("t2cos", [KT, 128, KT, 128], BF16)
    t2sin = din("t2sin", [KT, 128, KT, 128], BF16)
    c_trif = din("c_trif", [128, 128])
    c_trib = din("c_trib", [128, 128])
    c_negf = din("c_negf", [128, 128])
    c_negb = din("c_negb", [128, 128])
    out_own = dout("out_own", [1024, D])

    dbg = {}

    def dbg_out(name, shape, dt=F32):
        dbg[name] = dout("dbg_" + name, shape, dt)
        return dbg[name]

    POS = dscr("POS", [64, 1024])
    MODIN = dscr("MODIN", [2, 3072])
    MODG = dscr("MODG", [8, 3072])
    XP = dscr("XP", [L, D])
    U = dscr("U", [1288, L])
    UC = dscr("UC", [520, LC])
    YS = dscr("YS", [L, 256])
    GS = dscr("GS", [4 * L, 256])
    X1S = dscr("X1S", [1024, D])
    FSD = dscr("FSD", [L, 256], BF16)
    FDD = dscr("FDD", [L, 256], BF16)
    X0S = dscr("X0S", [256, L])
    VXS = dscr("VXS", [256, L])
    YH = dscr("YH", [4 * 256, 1024], BF16)
    GH = dscr("GH", [16 * 256, 1024], BF16)

    import contextlib
    es = contextlib.ExitStack()
    scopes = [es]

    class Scope:
        def __enter__(self_):
            st = contextlib.ExitStack()
            st.__enter__()
            scopes.append(st)
            return st

        def __exit__(self_, *a):
            P.fence()
            st = scopes.pop()
            st.__exit__(None, None, None)
            return False

    with es:
        def sb(name, shape, dt=F32):
            return scopes[-1].enter_context(nc.sbuf_tensor(name, list(shape), dt))

        def ps(name, shape, dt=F32):
            return scopes[-1].enter_context(nc.psum_tensor(name, list(shape), dt))

        def range_reduce_sin(eng_out, argt, kft, kit, npart, ncol, keys):
            ka, kf_, ki_ = keys
            P.op("dve", lambda e: e.tensor_scalar(out=kft[0:npart, 0:ncol], in0=argt[0:npart, 0:ncol], scalar1=1.0 / TWO_PI,
                                                  scalar2=None, op0=ALU.mult), r=[ka], w=[kf_])
            P.op("dve", lambda e: e.tensor_copy(out=kit[0:npart, 0:ncol], in_=kft[0:npart, 0:ncol]), r=[kf_], w=[ki_])
            P.op("dve", lambda e: e.tensor_copy(out=kft[0:npart, 0:ncol], in_=kit[0:npart, 0:ncol]), r=[ki_], w=[kf_])
            P.op("dve", lambda e: e.scalar_tensor_tensor(out=argt[0:npart, 0:ncol], in0=kft[0:npart, 0:ncol], scalar=-TWO_PI,
                                                         in1=argt[0:npart, 0:ncol], op0=ALU.mult, op1=ALU.add),
                 r=[kf_, ka], w=[ka])
            P.op("dve", lambda e: e.tensor_scalar(out=argt[0:npart, 0:ncol], in0=argt[0:npart, 0:ncol], scalar1=-3.141592,
                                                  scalar2=3.141592, op0=ALU.max, op1=ALU.min), r=[ka], w=[ka])
            P.op("act", lambda e: e.activation(out=eng_out, in_=argt[0:npart, 0:ncol], func=AF.Sin), r=[ka], w=["sinout"])

        tm_state = {"ptr": None, "trc": {"n": 0}}

        def to_token_major(src_bf, skey, dsts):
            for g8 in range(4):
                k = tm_state["trc"]["n"] % 2
                tm_state["trc"]["n"] += 1
                pk = "ptr%d" % k
                for i8 in range(8):
                    tt = g8 * 8 + i8
                    P.op("pe", lambda e: e.transpose(out=tm_state["ptr"][k][:, i8, :], in_=src_bf[:, tt * 128:(tt + 1) * 128],
                                                     identity=identb[:]), r=[skey, "identb"], w=[pk])
                dt0, dk0, c00 = dsts[0]
                P.op("act", lambda e: e.copy(out=dt0[:, g8 * 8:(g8 + 1) * 8, c00:c00 + 128], in_=tm_state["ptr"][k][:]),
                     r=[pk], w=[dk0])
                for (dt_, dk_, c0) in dsts[1:]:
                    P.op("pool", lambda e: e.tensor_copy(out=dt_[:, g8 * 8:(g8 + 1) * 8, c0:c0 + 128],
                                                         in_=dt0[:, g8 * 8:(g8 + 1) * 8, c00:c00 + 128]),
                         r=[dk0], w=[dk_])


        def filter_gen():
            FSt = sb("FSt", [128, 32, 128], BF16)
            with Scope():
                h3T = sb("h3T", [64, L]); irow = sb("irow", [128, L])
                P.dma("sp", irow[:], c_irow.partition_broadcast(128), w=["irow"])
                w1 = sb("hw1", [33, 64]); w2 = sb("hw2", [64, 64]); w3 = sb("hw3", [64, 64]); w4 = sb("hw4", [64, 512])
                b123 = sb("hb123", [64, 3]); fr = sb("hfr", [64, 1]); frb = sb("hfrb", [64, 3]); fbph = sb("fbph", [33, 2])
                for t_, src in ((w1, hy_w1), (w2, hy_w2), (w3, hy_w3), (w4, hy_w4q), (b123, hy_b123), (fr, hy_fr), (fbph, c_fbph)):
                    P.dma("sp", t_[:], src[:, :], w=[t_.name])
                P.op("dve", lambda e: e.tensor_scalar(out=frb[:], in0=b123[:], scalar1=fr[:, 0:1], scalar2=None, op0=ALU.mult),
                     r=["hb123", "hfr"], w=["hfrb"])
                with Scope():
                    wk_arg = sb("wk_arg", [64, L]); wk_kf = sb("wk_kf", [64, L]); wk_ki = sb("wk_ki", [64, L], I32)
                    hA = sb("hA", [64, L])
                    keys = ("wk_arg", "wk_kf", "wk_ki")
                    P.op("dve", lambda e: e.tensor_scalar(out=wk_arg[0:33, :], in0=irow[0:33, :], scalar1=TWO_PI / L, scalar2=None,
                                                          op0=ALU.mult), r=["irow"], w=["wk_arg"])
                    P.op("dve", lambda e: e.tensor_scalar(out=wk_arg[0:33, :], in0=wk_arg[0:33, :], scalar1=fbph[:, 0:1],
                                                          scalar2=fbph[:, 1:2], op0=ALU.mult, op1=ALU.add),
                         r=["wk_arg", "fbph"], w=["wk_arg"])
                    range_reduce_sin(hA[0:33, :], wk_arg, wk_kf, wk_ki, 33, L, keys)
                    yield
                    P.op("dve", lambda e: e.tensor_scalar(out=hA[0:1, :], in0=irow[0:1, :], scalar1=1.0 / (L - 1), scalar2=None,
                                                          op0=ALU.mult), r=["irow", "sinout"], w=["sinout"])
                    lay = [(w1, 33, hA, h3T, 0), (w2, 64, h3T, hA, 1), (w3, 64, hA, h3T, 2)]
                    for (wt, kin, hin, hout, li) in lay:
                        for blk in range(8):
                            k = blk % 2
                            P.op("pe", lambda e: e.matmul(pf[k][0:64, :], lhsT=wt[0:kin, :], rhs=hin[0:kin, blk * 512:(blk + 1) * 512],
                                                          start=True, stop=True), r=[wt.name, "sinout"], w=["pf%d" % k])
                            P.op("dve", lambda e: e.tensor_scalar(out=wk_arg[:, blk * 512:(blk + 1) * 512], in0=pf[k][0:64, :],
                                                                  scalar1=fr[:, 0:1], scalar2=frb[:, li:li + 1],
                                                                  op0=ALU.mult, op1=ALU.add), r=["pf%d" % k, "hfr", "hfrb"], w=["wk_arg"])
                        range_reduce_sin(hout[:, :], wk_arg, wk_kf, wk_ki, 64, L, keys)
                        yield
                with Scope():
                    hf = sb("hf", [128, L]); hbk = sb("hbk", [128, L]); wexp = sb("wexp", [128, 512])
                    fsb = sb("fsb", [128, L], BF16); fdb = sb("fdb", [128, L], BF16)
                    dsc = sb("dsc", [128, 2]); nrm = sb("nrm", [128, 2]); rinv = sb("rinv", [128, 2]); junkf = sb("junkf", [128, L], BF16)
                    P.dma("sp", dsc[:], c_delta[:, :], w=["dsc"])
                    P.op("dve", lambda e: e.tensor_scalar(out=dsc[:], in0=dsc[:], scalar1=-1.0 / (L - 1), scalar2=None, op0=ALU.mult),
                         r=["dsc"], w=["dsc"])
                    for j in range(2):
                        for blk in range(8):
                            P.op("act", lambda e: e.activation(out=wexp[:], in_=irow[:, blk * 512:(blk + 1) * 512], func=AF.Exp,
                                                               scale=dsc[:, j:j + 1]), r=["irow", "dsc"], w=["wexp"])
                            for d_, dstt in ((0, hf), (1, hbk)):
                                k = d_
                                c0 = d_ * 256 + j * 128
                                P.op("pe", lambda e: e.matmul(pf[k][:], lhsT=w4[:, c0:c0 + 128], rhs=h3T[:, blk * 512:(blk + 1) * 512],
                                                              start=True, stop=True), r=["hw4", "sinout"], w=["pf%d" % k])
                                P.op("dve", lambda e: e.tensor_tensor(out=dstt[:, blk * 512:(blk + 1) * 512], in0=pf[k][:], in1=wexp[:],
                                                                      op=ALU.mult), r=["pf%d" % k, "wexp"], w=[dstt.name])
                        P.op("dve", lambda e: e.memset(nrm[:], 0.0), w=["nrm"])
                        P.op("act", lambda e: e.activation(out=junkf[:], in_=hf[:], func=AF.Abs, accum_out=nrm[:, 0:1]),
                             r=["hf"], w=["nrm", "junkf"])
                        P.op("act", lambda e: e.activation(out=junkf[:], in_=hbk[:], func=AF.Abs, accum_out=nrm[:, 1:2]),
                             r=["hbk"], w=["nrm", "junkf"])
                        P.op("dve", lambda e: e.scalar_tensor_tensor(out=rinv[:, 0:1], in0=nrm[:, 0:1], scalar=1e-6, in1=nrm[:, 1:2],
                                                                     op0=ALU.add, op1=ALU.add), r=["nrm"], w=["rinv"])
                        P.op("dve", lambda e: e.reciprocal(out=rinv[:, 0:1], in_=rinv[:, 0:1]), r=["rinv"], w=["rinv"])
                        P.op("dve", lambda e: e.tensor_scalar(out=rinv[:, 1:2], in0=rinv[:, 0:1], scalar1=-1.0, scalar2=None,
                                                              op0=ALU.mult), r=["rinv"], w=["rinv"])
                        P.op("pool", lambda e: e.memset(hbk[:, 0:1], 0.0), r=["nrm"], w=["hbk"])
                        P.op("act", lambda e: e.activation(out=hbk[:], in_=hbk[:], func=AF.Identity, scale=rinv[:, 0:1]),
                             r=["rinv"], w=["hbk"])
                        P.op("dve", lambda e: e.scalar_tensor_tensor(out=fsb[:], in0=hf[:], scalar=rinv[:, 0:1], in1=hbk[:],
                                                                     op0=ALU.mult, op1=ALU.add), r=["hf", "hbk", "rinv"], w=["fsb"])
                        P.op("dve", lambda e: e.scalar_tensor_tensor(out=fdb[:], in0=hf[:], scalar=rinv[:, 1:2], in1=hbk[:],
                                                                     op0=ALU.mult, op1=ALU.add), r=["hf", "hbk", "rinv"], w=["fdb"])
                        to_token_major(fsb, "fsb", [(FSt, "FSt", 0)])
                        P.dma("sp", FSD[:, 128 * j:128 * (j + 1)].rearrange("(t p) c -> p t c", p=128), FSt[:], r=["FSt"], w=["FSD"])
                        to_token_major(fdb, "fdb", [(FSt, "FSt", 0)])
                        P.dma("sp", FDD[:, 128 * j:128 * (j + 1)].rearrange("(t p) c -> p t c", p=128), FSt[:], r=["FSt"], w=["FDD"])
            yield

        ident = sb("ident", [128, 128])
        identb = sb("identb", [128, 128], BF16)
        P.dma("sp", ident[:], c_ident[:, :], w=["ident"])
        P.op("dve", lambda e: e.tensor_copy(out=identb[:], in_=ident[:]), r=["ident"], w=["identb"])

        kcol = sb("kcol", [128, 1])
        poscol = sb("poscol", [128, 1024])
        sh1 = sb("sh1", [128, NCH]); G1 = sb("G1", [128, NCH])
        csh1 = sb("csh1", [128, NCH]); cG1 = sb("cG1", [128, NCH])
        sh2 = sb("sh2", [128, NCH]); G2 = sb("G2", [128, NCH])
        np1 = sb("np1", [128, NCH]); np2 = sb("np2", [128, NCH])
        psm = ps("psm", [128, 512])
        epsc = sb("epsc", [128, 1])
        P.op("dve", lambda e: e.memset(epsc[:], EPS), w=["epsc"])
        with Scope():
            jrow = sb("jrow", [64, 512])
            om = sb("om", [64, 512])
            arg = sb("parg", [64, 1024])
            kk_i = sb("pki", [64, 1024], I32)
            kk_f = sb("pkf", [64, 1024])
            Ttab = sb("Ttab", [64, 1024])
            P.dma("sp", jrow[:], c_jrow.partition_broadcast(64), w=["jrow"])
            P.dma("sp", kcol[:], c_kcol[:, :], w=["kcol"])
            P.op("act", lambda e: e.activation(out=om[:], in_=jrow[:], func=AF.Exp, scale=-math.log(10000.0) / 512.0),
                 r=["jrow"], w=["om"])
            P.op("dve", lambda e: e.tensor_scalar(out=arg[:, 0:512], in0=om[:], scalar1=kcol[0:64, 0:1], scalar2=None,
                                                  op0=ALU.mult), r=["om", "kcol"], w=["parg"])
            P.op("dve", lambda e: e.tensor_scalar(out=arg[:, 512:1024], in0=arg[:, 0:512], scalar1=math.pi / 2.0,
                                                  scalar2=None, op0=ALU.add), r=["parg"], w=["parg"])
            P.op("dve", lambda e: e.tensor_scalar(out=kk_f[:], in0=arg[:], scalar1=1.0 / TWO_PI, scalar2=None,
                                                  op0=ALU.mult), r=["parg"], w=["pkf"])
            P.op("dve", lambda e: e.tensor_copy(out=kk_i[:], in_=kk_f[:]), r=["pkf"], w=["pki"])
            P.op("dve", lambda e: e.tensor_copy(out=kk_f[:], in_=kk_i[:]), r=["pki"], w=["pkf"])
            P.op("dve", lambda e: e.scalar_tensor_tensor(out=arg[:], in0=kk_f[:], scalar=-TWO_PI, in1=arg[:],
                                                         op0=ALU.mult, op1=ALU.add), r=["pkf", "parg"], w=["parg"])
            P.op("dve", lambda e: e.tensor_scalar(out=arg[:], in0=arg[:], scalar1=-3.141592, scalar2=3.141592,
                                                  op0=ALU.max, op1=ALU.min), r=["parg"], w=["parg"])
            P.op("act", lambda e: e.activation(out=Ttab[:], in_=arg[:], func=AF.Sin), r=["parg"], w=["Ttab"])
            P.dma("sp", POS[:, :], Ttab[:], r=["Ttab"], w=["POS"])
            P.dma("sp", poscol[0:64, :], POS[:, :], r=["POS"], w=["poscol"])
            P.dma("sp", poscol[64:128, :], POS[:, :], r=["POS"], w=["poscol"])

            cvs = sb("cvs", [128, NCH, 2])
            sig = sb("sig", [128, NCH, 2])
            cvb = sb("cvb", [128, NCH, 2])
            bada2 = sb("bada2", [2, 3072])
            modrow = sb("modrow", [2, 3072])
            wa = [sb("wa0", [128, NCH, 256]), sb("wa1", [128, NCH, 256])]
            P.dma("sp", cvs[:], cv[:, :, :], w=["cvs"])
            P.dma("sp", bada2[0:1, :], b_ada_q[:, :], w=["bada2"])
            P.dma("sp", bada2[1:2, :], b_ada_q[:, :], w=["bada2"])
            P.op("act", lambda e: e.activation(out=sig[:], in_=cvs[:], func=AF.Silu), r=["cvs"], w=["sig"])
            P.op("dve", lambda e: e.tensor_copy(out=cvb[:], in_=sig[:]), r=["sig"], w=["cvb"])
            w_ada_v = w_ada_q.rearrange("(p c) n -> p c n", c=NCH)
            fptr = [ps("fptr0", [128, 8, 128], BF16), ps("fptr1", [128, 8, 128], BF16)]
            pf = [ps("fpf0", [128, 512]), ps("fpf1", [128, 512])]
            tm_state["ptr"] = fptr
            fgen = filter_gen()
            for nb in range(12):
                if nb % 2 == 0:
                    next(fgen, None)
                wb = wa[nb % 2]
                wk = "wa%d" % (nb % 2)
                P.dma("sp", wb[:], w_ada_v[:, :, nb * 256:(nb + 1) * 256], w=[wk])
                for c in range(NCH):
                    P.op("pe", (lambda e, c=c, wb=wb: e.matmul(psm[0:2, 0:256], lhsT=cvb[:, c, :], rhs=wb[:, c, :],
                                                               start=(c == 0), stop=(c == NCH - 1))),
                         r=["cvb", wk], w=["psm"])
                P.op("dve", (lambda e, nb=nb: e.tensor_tensor(out=modrow[:, nb * 256:(nb + 1) * 256], in0=psm[0:2, 0:256],
                                                              in1=bada2[:, nb * 256:(nb + 1) * 256], op=ALU.add)),
                     r=["psm", "bada2"], w=["modrow"])
            for _ in fgen:
                pass
            P.dma("sp", MODIN[:, :], modrow[:], r=["modrow"], w=["MODIN"])
            P.coll("AllGather", GRP, MODIN[:, :], MODG[:, :], r=["MODIN"], w=["MODG"])

            def mod_fm(dst, r, seg):
                for half in range(2):
                    j0 = seg * 2048 + half * 1024
                    qs, col = j0 // 3072, j0 % 3072
                    src = MODG[2 * qs + r:2 * qs + r + 1, col:col + 1024].rearrange("o (p c) -> (o p) c", c=NCH)
                    P.dma("sp", dst[half * 64:(half + 1) * 64, :], src, r=["MODG"], w=[dst.name])

            def mod_bc(dst, r, seg):
                for half in range(2):
                    j0 = seg * 2048 + half * 1024
                    qs, col = j0 // 3072, j0 % 3072
                    src = MODG[2 * qs + r:2 * qs + r + 1, col:col + 1024].partition_broadcast(128)
                    P.dma("sp", dst[:, half * 1024:(half + 1) * 1024], src, r=["MODG"], w=[dst.name])

            P.dma("sp", np1[:], nrm_pre1[:, :], w=["np1"])
            P.dma("sp", np2[:], nrm_pre2[:, :], w=["np2"])
            mod_fm(sh1, 0, 0); mod_fm(G1, 0, 1)
            mod_fm(csh1, 1, 0); mod_fm(cG1, 1, 1)
            mod_fm(sh2, 0, 3); mod_fm(G2, 0, 4)
            for g, npx in ((G1, np1), (cG1, np1), (G2, np2)):
                nm = g.name
                P.op("dve", (lambda e, g=g, npx=npx: e.scalar_tensor_tensor(out=g[:], in0=g[:], scalar=1.0, in1=npx[:],
                                                                             op0=ALU.add, op1=ALU.mult)),
                     r=[nm, npx.name], w=[nm])
            if stop == "p0":
                d1 = dbg_out("pos", [64, 1024]); d2 = dbg_out("modg", [8, 3072]); d3 = dbg_out("G1", [128, NCH])
                d5 = dbg_out("sh2", [128, NCH])
                P.dma("sp", d1[:, :], Ttab[:], r=["Ttab"], w=["dbgo"])
                P.dma("sp", d2[:, :], MODG[:, :], r=["MODG"], w=["dbgo"])
                P.dma("sp", d3[:, :], G1[:], r=["G1"], w=["dbgo"])
                P.dma("sp", d5[:, :], sh2[:], r=["sh2"], w=["dbgo"])
                P.final_wait("sp", ["dbgo"])
                P.emit()
                return nc, dbg, in_names


        with Scope():
            win = sb("win", [128, NCH, 1288], BF16)
            w_in_v = w_in_q.rearrange("(p c) n -> p c n", c=NCH)
            for j in range(4):
                P.dma("pool", win[:, :, j * 322:(j + 1) * 322], w_in_v[:, :, j * 322:(j + 1) * 322], w=["win"])
            NB_ = 4
            xt = [sb("xt%d" % j_, [128, D]) for j_ in range(NB_)]
            posrow = [sb("posrow%d" % j_, [128, 1024]) for j_ in range(NB_)]
            junk = sb("junk", [128, D], BF16)
            ssq = [sb("ssq%d" % j_, [128, 1]) for j_ in range(NB_)]
            rstd = [sb("rstd%d" % j_, [128, 1]) for j_ in range(NB_)]
            xn = [sb("xn%d" % j_, [128, D], BF16) for j_ in range(NB_)]
            mtmp = [sb("mtmp0", [128, NCH, 128]), sb("mtmp1", [128, NCH, 128])]
            hT = [sb("hT0", [128, NCH, 512], BF16), sb("hT1", [128, NCH, 512], BF16)]
            ust = [sb("ust%d" % j_, [128, 512]) for j_ in range(4)]
            tp = [ps("tp0", [128, NCH, 128], BF16), ps("tp1", [128, NCH, 128], BF16)]
            pmm = [psm, ps("pmm1", [128, 512]), ps("pmm2", [128, 512])]
            cnt = {"tile": 0, "mm": 0}

            def norm_tile(src_rows, Gm, Sm, use_pos, pos_i, xp_rows, hT_dst, hkey):
                s_ = cnt["tile"] % NB_
                s2_ = cnt["tile"] % 2
                cnt["tile"] += 1
                xk = "xt%d" % s_
                P.dma("sp", xt[s_][:], src_rows, w=[xk + "a", xk + "b"], sk=xk)
                if use_pos:
                    pk = "posrow%d" % s_
                    P.dma("sp", posrow[s_][0:64, :], POS[2 * pos_i:2 * pos_i + 1, :].partition_broadcast(64),
                          r=["POS"], w=[pk])
                    P.dma("sp", posrow[s_][64:128, :], POS[2 * pos_i + 1:2 * pos_i + 2, :].partition_broadcast(64),
                          r=["POS"], w=[pk])
                    P.op("pool", lambda e: e.tensor_tensor(out=xt[s_][:, 0:1024], in0=xt[s_][:, 0:1024],
                                                           in1=posrow[s_][:], op=ALU.add), r=[pk], w=[xk + "a"])
                    P.op("dve", lambda e: e.tensor_tensor(out=xt[s_][:, 1024:D], in0=xt[s_][:, 1024:D],
                                                          in1=poscol[:], op=ALU.add), r=["poscol"], w=[xk + "b"])
                sk_ = "ssq%d" % s_
                P.op("dve", lambda e: e.memset(ssq[s_][:], 0.0), w=[sk_])
                P.op("act", lambda e: e.activation(out=junk[:], in_=xt[s_][:], func=AF.Square, accum_out=ssq[s_][:]),
                     r=[xk + "a", xk + "b"], w=[sk_, "junk"])
                rk = "rstd%d" % s_
                P.op("act", lambda e: e.activation(out=rstd[s_][:], in_=ssq[s_][:], func=AF.Sqrt, scale=1.0 / D, bias=epsc[:, 0:1]),
                     r=[sk_, "epsc"], w=[rk])
                P.op("dve", lambda e: e.reciprocal(out=rstd[s_][:], in_=rstd[s_][:]), r=[rk], w=[rk])
                nk = "xn%d" % s_
                P.op("dve", lambda e: e.tensor_scalar(out=xn[s_][:], in0=xt[s_][:], scalar1=rstd[s_][:, 0:1], scalar2=None,
                                                      op0=ALU.mult), r=[xk + "a", xk + "b", rk], w=[nk])
                return (s_, s2_, Gm, Sm, hT_dst, hkey)

            def norm_post(st):
                s_, s2_, Gm, Sm, hT_dst, hkey = st
                nk = "xn%d" % s_
                tk = "tp%d" % s2_
                xnv = xn[s_][:].rearrange("p (j c) -> p c j", c=NCH)
                for c in range(NCH):
                    P.op("pe", (lambda e, c=c: e.transpose(out=tp[s2_][:, c, :], in_=xnv[:, c, :], identity=identb[:])),
                         r=[nk, "identb"], w=[tk])
                mk = "mtmp%d" % s2_
                P.op("dve", lambda e: e.tensor_tensor(out=mtmp[s2_][:], in0=tp[s2_][:], in1=bc_last(Gm[:], 128), op=ALU.mult),
                     r=[tk, Gm.name], w=[mk])
                P.op("pool", lambda e: e.tensor_tensor(out=hT_dst, in0=mtmp[s2_][:], in1=bc_last(Sm[:], 128), op=ALU.add),
                     r=[mk, Sm.name], w=[hkey])

            def project(hbuf, hkey, ntok, ctiles, dst_fn):
                for (c0, ncol, r0) in ctiles:
                    m_ = cnt["mm"] % 3
                    cnt["mm"] += 1
                    pk = "pmm%d" % m_
                    for c in range(NCH):
                        P.op("pe", (lambda e, c=c, c0=c0, ncol=ncol, m_=m_: e.matmul(
                            pmm[m_][0:ncol, 0:ntok], lhsT=win[:, c, c0:c0 + ncol], rhs=hbuf[:, c, 0:ntok],
                            start=(c == 0), stop=(c == NCH - 1))), r=["win", hkey], w=[pk])
                    u_ = cnt["mm"] % 4
                    uk = "ust%d" % u_
                    P.op("act", (lambda e, m_=m_, u_=u_, ncol=ncol: e.copy(out=ust[u_][0:ncol, 0:ntok],
                                                                           in_=pmm[m_][0:ncol, 0:ntok])),
                         r=[pk], w=[uk])
                    P.dma("act", dst_fn(r0, ncol), ust[u_][0:ncol, 0:ntok], r=[uk], w=["U"], sk=uk + "st")

            main_ct = [(j * 128, 128, j * 128) for j in range(10)] + [(1280, 8, 1280)]
            ctx_ct = [(j * 128, 128, j * 128) for j in range(4)] + [(1280, 8, 512)]
            nblk = 8 if stop != "pA1" else 1
            hb = 0
            sts = [norm_tile(ctx_b[128 * t:128 * (t + 1), :], cG1, csh1, False, 0, None,
                             hT[hb][:, :, 128 * t:128 * (t + 1)], "hT%d" % hb) for t in range(2)]
            for st in sts:
                norm_post(st)
            project(hT[hb], "hT%d" % hb, 256, ctx_ct, lambda r0, n: UC[r0:r0 + n, :])
            def norm_block(blk):
                hb = (blk + 1) % 2
                sts = []
                for t in range(4):
                    i = 4 * blk + t
                    sts.append(norm_tile(x_b[128 * i:128 * (i + 1), :], G1, sh1, True, i, None,
                                         hT[hb][:, :, 128 * t:128 * (t + 1)], "hT%d" % hb))
                for st in sts:
                    norm_post(st)

            def project_block(blk):
                hb = (blk + 1) % 2
                project(hT[hb], "hT%d" % hb, 512, main_ct,
                        (lambda r0, n, blk=blk: U[r0:r0 + n, 512 * blk:512 * (blk + 1)]))

            norm_block(0)
            for blk in range(nblk):
                if blk + 1 < nblk:
                    norm_block(blk + 1)
                project_block(blk)

            if stop in ("pA", "pA1"):
                d1 = dbg_out("U", [1288, L]); d2 = dbg_out("UC", [520, LC])
                P.dma("sp", d1[:, :], U[:, :], r=["U"], w=["dbgo"])
                P.dma("sp", d2[:, :], UC[:, :], r=["U"], w=["dbgo"])
                P.final_wait("sp", ["dbgo"])
                P.emit()
                return nc, dbg, in_names

        NCK = 34
        with Scope():
            trif = sb("trif", [128, 128]); trib = sb("trib", [128, 128])
            negf4 = sb("negf4", [128, 4, 128]); negb4 = sb("negb4", [128, 4, 128]); ones = sb("ones", [128, 128])
            onec = sb("onec", [128, 1])
            P.dma("sp", trif[:], c_trif[:, :], w=["trif"]); P.dma("sp", trib[:], c_trib[:, :], w=["trib"])
            for h in range(4):
                P.dma("sp", negf4[:, h, :], c_negf[:, :], w=["negf4"])
                P.dma("sp", negb4[:, h, :], c_negb[:, :], w=["negb4"])
            P.op("pool", lambda e: e.memset(ones[:], 1.0), w=["ones"])
            P.op("pool", lambda e: e.memset(onec[:], 1.0), w=["onec"])
            cw = sb("cw", [128, 4, 3]); cb = sb("cb", [128, 4])
            P.dma("sp", cw[:], ssd_cw[:, :, :], w=["cw"]); P.dma("sp", cb[:], ssd_cb[:, :], w=["cb"])
            Aneg = sb("Aneg", [128, 8]); dtb = sb("dtb", [128, 8]); Dq = sb("Dq", [128, 4])
            P.dma("sp", Aneg[:], ssd_alog.partition_broadcast(128), w=["Aneg"])
            P.dma("sp", dtb[:], ssd_dtb.partition_broadcast(128), w=["dtb"])
            P.dma("sp", Dq[:], ssd_Dq.partition_broadcast(128), w=["Dq"])
            P.op("act", lambda e: e.activation(out=Aneg[:], in_=Aneg[:], func=AF.Exp), r=["Aneg"], w=["Aneg"])
            P.op("dve", lambda e: e.tensor_scalar(out=Aneg[:], in0=Aneg[:], scalar1=-1.0, scalar2=None, op0=ALU.mult),
                 r=["Aneg"], w=["Aneg"])

            BT = sb("BT", [128, L + LC], BF16); CT = sb("CT", [128, L + LC], BF16)
            xs_tok = sb("xs_tok", [128, NCK, 256]); B_tok = sb("B_tok", [128, NCK, 128], BF16)
            scor = sb("scor", [128, 32, 128]); Yacc = sb("Yacc", [128, 32, 256])
            dt_t = sb("dt_t", [128, NCK, 8]); a_t = sb("a_t", [128, NCK, 8]); cs_t = sb("cs_t", [128, NCK, 8])
            ncs_t = sb("ncs_t", [128, NCK, 8]); ecs_t = sb("ecs_t", [128, NCK, 8]); wst_t = sb("wst_t", [128, NCK, 8])
            etot_t = sb("etot_t", [128, NCK, 8]); tot_t = sb("tot_t", [128, NCK, 8]); tmp_t = sb("tmp_t", [128, NCK, 8])
            pA_ = ps("pB1a", [128, 512]); pB_ = ps("pB1b", [128, 512]); pC_ = ps("pB1c", [128, 512])
            pD_ = ps("pB1d", [128, 512]); pE_ = ps("pB1e", [128, 512])
            pT_full = ps("pB1t", [128, 1024], BF16)
            pT_ = pT_full[:, 0:128]

            with Scope():
                dtraw = sb("dtraw", [8, L + LC])
                P.dma("sp", dtraw[:, 0:L], U[1280:1288, :], r=["U"], w=["dtraw"])
                P.dma("sp", dtraw[:, L:L + LC], UC[512:520, :], r=["U"], w=["dtraw"])
                for ck_ in range(NCK):
                    P.op("pe", (lambda e: e.transpose(out=pA_[:, ck_ * 8:(ck_ + 1) * 8],
                                                      in_=dtraw[0:8, ck_ * 128:(ck_ + 1) * 128],
                                                      identity=ident[0:8, 0:8])),
                         r=["dtraw", "ident"], w=["pB1a"])
                pAv = pA_[:, 0:NCK * 8].rearrange("p (c j) -> p c j", j=8)
                dtb_b = dtb[:].unsqueeze(1).to_broadcast([128, NCK, 8])
                Aneg_b = Aneg[:].unsqueeze(1).to_broadcast([128, NCK, 8])
                P.op("dve", lambda e: e.tensor_tensor(out=dt_t[:], in0=pAv, in1=dtb_b, op=ALU.add),
                     r=["pB1a", "dtb"], w=["dt_t"])
                P.op("act", lambda e: e.activation(out=tmp_t[:], in_=dt_t[:], func=AF.Abs),
                     r=["dt_t"], w=["tmp_t"])
                P.op("act", lambda e: e.activation(out=tmp_t[:], in_=tmp_t[:], func=AF.Exp, scale=-1.0),
                     r=["tmp_t"], w=["tmp_t"])
                P.op("act", lambda e: e.activation(out=tmp_t[:], in_=tmp_t[:], func=AF.Ln, bias=onec[:, 0:1]),
                     r=["tmp_t", "onec"], w=["tmp_t"])
                P.op("dve", lambda e: e.scalar_tensor_tensor(out=dt_t[:], in0=dt_t[:], scalar=0.0, in1=tmp_t[:],
                                                             op0=ALU.max, op1=ALU.add), r=["dt_t", "tmp_t"], w=["dt_t"])
                P.op("dve", lambda e: e.tensor_tensor(out=a_t[:], in0=dt_t[:], in1=Aneg_b, op=ALU.mult),
                     r=["dt_t", "Aneg"], w=["a_t"])
                a_flat = a_t[:].rearrange("p c j -> p (c j)")
                P.op("pe", lambda e: e.matmul(pB_[:, 0:NCK * 8], lhsT=trif[:], rhs=a_flat, start=True, stop=True),
                     r=["trif", "a_t"], w=["pB1b"])
                P.op("pe", lambda e: e.matmul(pC_[:, 0:NCK * 8], lhsT=trib[:], rhs=a_flat, start=True, stop=True),
                     r=["trib", "a_t"], w=["pB1c"])
                P.op("pe", lambda e: e.matmul(pD_[:, 0:NCK * 8], lhsT=ones[:], rhs=a_flat, start=True, stop=True),
                     r=["ones", "a_t"], w=["pB1d"])
                pBv = pB_[:, 0:NCK * 8].rearrange("p (c j) -> p c j", j=8)
                pCv = pC_[:, 0:NCK * 8].rearrange("p (c j) -> p c j", j=8)
                pDv = pD_[:, 0:NCK * 8].rearrange("p (c j) -> p c j", j=8)
                P.op("dve", lambda e: e.tensor_copy(out=cs_t[:, :, 0:4], in_=pBv[:, :, 0:4]), r=["pB1b"], w=["cs_t"])
                P.op("dve", lambda e: e.tensor_copy(out=cs_t[:, :, 4:8], in_=pCv[:, :, 4:8]), r=["pB1c"], w=["cs_t"])
                P.op("dve", lambda e: e.tensor_copy(out=tot_t[:], in_=pDv), r=["pB1d"], w=["tot_t"])
                P.op("dve", lambda e: e.tensor_scalar(out=ncs_t[:], in0=cs_t[:], scalar1=-1.0, scalar2=None, op0=ALU.mult),
                     r=["cs_t"], w=["ncs_t"])
                P.op("act", lambda e: e.activation(out=ecs_t[:], in_=cs_t[:], func=AF.Exp), r=["cs_t"], w=["ecs_t"])
                P.op("act", lambda e: e.activation(out=etot_t[:], in_=tot_t[:], func=AF.Exp), r=["tot_t"], w=["etot_t"])
                P.op("dve", lambda e: e.tensor_tensor(out=wst_t[:], in0=tot_t[:], in1=cs_t[:], op=ALU.subtract),
                     r=["tot_t", "cs_t"], w=["wst_t"])
                P.op("act", lambda e: e.activation(out=wst_t[:], in_=wst_t[:], func=AF.Exp), r=["wst_t"], w=["wst_t"])
                P.op("dve", lambda e: e.tensor_tensor(out=wst_t[:], in0=wst_t[:], in1=dt_t[:], op=ALU.mult),
                     r=["wst_t", "dt_t"], w=["wst_t"])
            with Scope():
                cin = [sb("cin0", [128, L + 2]), sb("cin1", [128, L + 2])]
                cacc = sb("cacc", [128, L])
                xsTj = sb("xsTj", [128, L + LC])
                k_ = 0
                for ct in range(4):
                    for (src, ntok, off) in ((U, L, 0), (UC, LC, L)):
                        s_ = k_ % 2
                        k_ += 1
                        ck = "cin%d" % s_; ak = "cacc"
                        P.op("pool", (lambda e: e.memset(cin[s_][:, 0:1], 0.0)), w=[ck])
                        P.op("pool", (lambda e: e.memset(cin[s_][:, ntok + 1:ntok + 2], 0.0)), w=[ck])
                        P.dma("sp", cin[s_][:, 1:ntok + 1], src[128 * ct:128 * (ct + 1), :], r=["U"], w=[ck])
                        eng = "dve"
                        P.op("act", (lambda e: e.activation(
                            out=cacc[:, 0:ntok], in_=cin[s_][:, 0:ntok], func=AF.Identity, scale=cw[:, ct, 0:1])),
                             r=[ck, "cw"], w=[ak])
                        for tap in (1, 2):
                            P.op(eng, (lambda e: e.scalar_tensor_tensor(
                                out=cacc[:, 0:ntok], in0=cin[s_][:, tap:tap + ntok], scalar=cw[:, ct, tap:tap + 1],
                                in1=cacc[:, 0:ntok], op0=ALU.mult, op1=ALU.add)), r=[ck, "cw"], w=[ak])
                        if ct < 2:
                            dst = xsTj[:, off:off + ntok]; dk = "xsTj"
                        elif ct == 2:
                            dst = BT[:, off:off + ntok]; dk = "BT"
                        else:
                            dst = CT[:, off:off + ntok]; dk = "CT"
                        P.op("act", (lambda e: e.activation(
                            out=dst, in_=cacc[:, 0:ntok], func=AF.Silu, bias=cb[:, ct:ct + 1])),
                             r=[ak, "cb"], w=[dk])
                    if ct < 2:
                        for ck_ in range(NCK):
                            col = ck_ * 128
                            pk = ("pB1e", "pB1c")[ck_ % 2]
                            pe_o = (pE_, pC_)[ck_ % 2][:, 0:128]
                            P.op("pe", (lambda e: e.transpose(out=pe_o, in_=xsTj[:, col:col + 128], identity=ident[:])),
                                 r=["xsTj", "ident"], w=[pk])
                            P.op("act", (lambda e: e.copy(out=xs_tok[:, ck_, ct * 128:(ct + 1) * 128], in_=pe_o)),
                                 r=[pk], w=["xs_tok"])
                Dq_b = Dq[:].unsqueeze(2).to_broadcast([128, 4, 64])
                for ck_ in range(NCK):
                    col = ck_ * 128
                    P.op("pe", (lambda e: e.transpose(out=pT_, in_=BT[:, col:col + 128], identity=identb[:])),
                         r=["BT", "identb"], w=["pB1t"])
                    P.op("dve", (lambda e: e.tensor_copy(out=B_tok[:, ck_, :], in_=pT_)),
                         r=["pB1t"], w=["B_tok"])
                    if ck_ < 32:
                        pk = ("pB1d", "pB1b")[ck_ % 2]
                        pd_o = (pD_, pB_)[ck_ % 2][:, 0:128]
                        P.op("pe", (lambda e: e.matmul(pd_o, lhsT=BT[:, col:col + 128],
                                                       rhs=CT[:, col:col + 128], start=True, stop=True)),
                             r=["BT", "CT"], w=[pk])
                        P.op("act", (lambda e: e.copy(out=scor[:, ck_, :], in_=pd_o)),
                             r=[pk], w=["scor"])
                        P.op("pool", (lambda e: e.tensor_tensor(
                            out=Yacc[:, ck_, :].rearrange("p (h d) -> p h d", h=4),
                            in0=xs_tok[:, ck_, :].rearrange("p (h d) -> p h d", h=4), in1=Dq_b, op=ALU.mult)),
                             r=["xs_tok", "Dq"], w=["Yacc"])

            S = [sb("S0", [128, 256]), sb("S1", [128, 256])]
            Sb2 = [[sb("Sb%d_%d" % (d, j_), [128, 256], BF16) for j_ in range(2)] for d in range(2)]
            xdt = [sb("xdt%d" % d, [128, 256], BF16) for d in range(2)]; xw = [sb("xw%d" % d, [128, 256], BF16) for d in range(2)]
            R4 = [sb("R4%d" % d, [128, 4, 128]) for d in range(2)]; arg4 = [sb("arg4%d" % d, [128, 4, 128]) for d in range(2)]
            Lm4 = [sb("Lm4%d" % d, [128, 4, 128]) for d in range(2)]; M4 = [sb("M4%d" % d, [128, 4, 128], BF16) for d in range(2)]
            t1 = [sb("t1%d" % d, [128, 256]) for d in range(2)]; t2 = [sb("t2%d" % d, [128, 256]) for d in range(2)]
            stmp = [sb("stmp%d" % d, [128, 256]) for d in range(2)]
            PA = [(pA_, "pB1a"), (pC_, "pB1c")]
            PBC = [(pB_, "pB1b"), (pE_, "pB1e")]
            PD = [(pD_, "pB1d"), (psm, "psm")]
            for d_ in range(2):
                P.op("pool", (lambda e, d_=d_: e.memset(S[d_][:], 0.0)), w=["S%d" % d_])
            tri4 = [trif[:].unsqueeze(1).to_broadcast([128, 4, 128]), trib[:].unsqueeze(1).to_broadcast([128, 4, 128])]
            neg4 = [negf4, negb4]

            def h4(ap):
                return ap.rearrange("p (h d) -> p h d", h=4)

            def scan_step(d_, ck_, with_y, par):
                sk_ = "S%d" % d_
                sl = slice(4 * d_, 4 * d_ + 4)
                sfx = "%d" % d_
                pa, pak = PA[d_]; pbc, pbck = PBC[d_]; pd, pdk = PD[d_]
                P.op("pool", lambda e: e.tensor_tensor(out=h4(xw[d_][:]), in0=h4(xs_tok[:, ck_, :]),
                                                       in1=bc_last(wst_t[:, ck_, sl], 64), op=ALU.mult),
                     r=["xs_tok", "wst_t"], w=["xw" + sfx])
                yield
                P.op("pe", lambda e: e.matmul(pd[:, 0:256], lhsT=B_tok[:, ck_, :], rhs=xw[d_][:], start=True, stop=True),
                     r=["B_tok", "xw" + sfx], w=[pdk])
                yield
                P.op("dve", lambda e: e.tensor_tensor(out=h4(stmp[d_][:]), in0=h4(S[d_][:]),
                                                      in1=bc_last(etot_t[:, ck_, sl], 64), op=ALU.mult),
                     r=[sk_, "etot_t"], w=["stmp" + sfx])
                yield
                P.op("dve", lambda e: e.tensor_tensor(out=S[d_][:], in0=stmp[d_][:], in1=pd[:, 0:256], op=ALU.add),
                     r=["stmp" + sfx, pdk], w=[sk_])
                yield
                P.op("act", lambda e: e.copy(out=Sb2[d_][1 - par][:], in_=S[d_][:]), r=[sk_], w=["Sb%d_%d" % (d_, 1 - par)])
                yield

                if with_y:
                    P.op("dve", lambda e: e.tensor_tensor(out=R4[d_][:], in0=tri4[d_],
                                                          in1=bc_last(a_t[:, ck_, sl], 128), op=ALU.mult),
                         r=["trif", "trib", "a_t"], w=["R4" + sfx])
                    yield
                    P.op("pe", lambda e: e.matmul(pa[:], lhsT=ones[:], rhs=R4[d_][:].rearrange("p h l -> p (h l)"),
                                                  start=True, stop=False), r=["ones", "R4" + sfx], w=[pak])
                    yield
                    P.op("pe", lambda e: e.matmul(pa[:], lhsT=ident[:], rhs=neg4[d_][:].rearrange("p h l -> p (h l)"),
                                                  start=False, stop=True), r=["ident", "negf4", "negb4"], w=[pak])
                    yield
                    P.op("dve", lambda e: e.tensor_tensor(out=arg4[d_][:], in0=h4(pa[:]),
                                                          in1=bc_last(ncs_t[:, ck_, sl], 128), op=ALU.add),
                         r=[pak, "ncs_t"], w=["arg4" + sfx])
                    yield
                    P.op("act", lambda e: e.activation(out=Lm4[d_][:], in_=arg4[d_][:], func=AF.Exp), r=["arg4" + sfx], w=["Lm4" + sfx])
                    yield
                    P.op("pool", lambda e: e.tensor_tensor(out=M4[d_][:], in0=Lm4[d_][:],
                                                           in1=scor[:, ck_, :].unsqueeze(1).to_broadcast([128, 4, 128]),
                                                           op=ALU.mult), r=["Lm4" + sfx, "scor"], w=["M4" + sfx])
                    yield
                    P.op("dve", lambda e: e.tensor_tensor(out=h4(xdt[d_][:]), in0=h4(xs_tok[:, ck_, :]),
                                                          in1=bc_last(dt_t[:, ck_, sl], 64), op=ALU.mult),
                         r=["xs_tok", "dt_t"], w=["xdt" + sfx])
                    yield
                    for h in range(4):
                        P.op("pe", (lambda e, h=h: e.matmul(pbc[:, h * 64:(h + 1) * 64], lhsT=M4[d_][:, h, :],
                                                            rhs=xdt[d_][:, h * 64:(h + 1) * 64], start=True, stop=True)),
                             r=["M4" + sfx, "xdt" + sfx], w=[pbck])
                        yield
                    col = ck_ * 128
                    P.op("pe", lambda e: e.matmul(pbc[:, 256:512], lhsT=CT[:, col:col + 128], rhs=Sb2[d_][par][:],
                                                  start=True, stop=True), r=["CT", "Sb%d_%d" % (d_, par)], w=[pbck])
                    yield
                    P.op("dve", lambda e: e.tensor_tensor(out=h4(t1[d_][:]), in0=h4(pbc[:, 256:512]),
                                                          in1=bc_last(ecs_t[:, ck_, sl], 64), op=ALU.mult),
                         r=[pbck, "ecs_t"], w=["t1" + sfx])
                    yield
                    P.op("dve", lambda e: e.tensor_tensor(out=t2[d_][:], in0=t1[d_][:], in1=pbc[:, 0:256], op=ALU.add),
                         r=["t1" + sfx, pbck], w=["t2" + sfx])
                    yield
                    P.op("pool", lambda e: e.tensor_tensor(out=Yacc[:, ck_, :], in0=Yacc[:, ck_, :], in1=t2[d_][:], op=ALU.add),
                         r=["t2" + sfx], w=["Yacc"])
                    yield

            nmain = 32 if stop != "pB1s" else 2
            def run2(g0, g1):
                done0 = done1 = False
                while not (done0 and done1):
                    if not done0:
                        try:
                            next(g0)
                        except StopIteration:
                            done0 = True
                    if not done1:
                        try:
                            next(g1)
                        except StopIteration:
                            done1 = True

            run2(scan_step(0, 32, False, 0), scan_step(1, 33, False, 0))
            run2(scan_step(0, 33, False, 1), scan_step(1, 32, False, 1))
            if stop == "pB1d":
                dd = {}
                for nm_, t_, shp in (("xs_tok", xs_tok, [128, NCK, 256]), ("dt_t", dt_t, [128, NCK, 8]), ("cs_t", cs_t, [128, NCK, 8]),
                                     ("wst_t", wst_t, [128, NCK, 8]), ("etot_t", etot_t, [128, NCK, 8]), ("S0", S[0], [128, 256]),
                                     ("S1", S[1], [128, 256]), ("scor", scor, [128, 32, 128])):
                    dd[nm_] = dbg_out(nm_, shp)
                    P.dma("sp", dd[nm_], t_[:], r=[nm_], w=["dbgo"])
                P.final_wait("sp", ["dbgo"])
            if stop == "pB1d":
                nmain = 0
            for ck_ in range(nmain):
                run2(scan_step(0, ck_, True, ck_ % 2), scan_step(1, 31 - ck_, True, ck_ % 2))
            P.dma("sp", YS.rearrange("(c p) n -> p c n", p=128), Yacc[:], r=["Yacc"], w=["YS"])
            if stop in ("pB1", "pB1s"):
                d1 = dbg_out("YS", [L, 256]); d2 = dbg_out("S", [2, 128, 256])
                P.dma("sp", d1[:, :], YS[:, :], r=["YS"], w=["dbgo"])
                P.dma("sp", d2[0], S[0][:], r=["S0"], w=["dbgo"])
                P.dma("sp", d2[1], S[1][:], r=["S1"], w=["dbgo"])
                P.final_wait("sp", ["dbgo"])
        if stop in ("pB1", "pB1s", "pB1d"):
            return nc, dbg, in_names

        with Scope():
            Ybuf = sb("Ybuf", [128, KT, 2, 256], BF16)
            wkc = sb("wkc", [128, KT]); hbq = sb("hbq", [128, 2])
            P.dma("sp", wkc[:], c_wk[:, :], w=["wkc"]); P.dma("sp", hbq[:], hy_biasq[:, :], w=["hbq"])
            with Scope():
                DC = sb("DC", [128, 32, 512], BF16); DS = sb("DS", [128, 32, 512], BF16)
                ptr = [ps("ptr0", [128, 8, 128], BF16), ps("ptr1", [128, 8, 128], BF16)]
                pf = [ps("pf0", [128, 512]), ps("pf1", [128, 512])]
                tm_state["ptr"] = ptr

                with Scope():
                    hcw = sb("hcw", [128, 6, 3]); hcb = sb("hcb", [128, 6])
                    P.dma("sp", hcw[:], hy_cw[:, :, :], w=["hcw"]); P.dma("sp", hcb[:], hy_cb[:, :], w=["hcb"])
                    cin = [sb("hcin0", [128, L + 2]), sb("hcin1", [128, L + 2])]
                    cacc = sb("hcacc", [128, L])
                    x1c = sb("x1c", [128, L]); cvo = sb("cvo", [128, L]); vxb = sb("vxb", [128, L], BF16)
                    k_ = 0
                    for ct in (0, 1, 2, 4, 3, 5):
                        s_ = k_ % 2
                        k_ += 1
                        ck = "hcin%d" % s_
                        P.op("pool", lambda e: e.memset(cin[s_][:, 0:1], 0.0), w=[ck])
                        P.op("pool", lambda e: e.memset(cin[s_][:, L + 1:L + 2], 0.0), w=[ck])
                        P.dma("sp", cin[s_][:, 1:L + 1], U[512 + 128 * ct:512 + 128 * (ct + 1), :], r=["U"], w=[ck])
                        P.op("act", lambda e: e.activation(out=cacc[:], in_=cin[s_][:, 0:L], func=AF.Identity,
                                                           scale=hcw[:, ct, 0:1]), r=[ck, "hcw"], w=["hcacc"])
                        for tap in (1, 2):
                            P.op("dve", lambda e: e.scalar_tensor_tensor(out=cacc[:], in0=cin[s_][:, tap:tap + L],
                                                                         scalar=hcw[:, ct, tap:tap + 1], in1=cacc[:],
                                                                         op0=ALU.mult, op1=ALU.add), r=[ck, "hcw"], w=["hcacc"])
                        j = ct % 2
                        if ct < 2:
                            P.op("act", lambda e: e.activation(out=cvo[:], in_=cacc[:], func=AF.Identity, bias=hcb[:, ct:ct + 1]),
                                 r=["hcacc", "hcb"], w=["cvo"])
                            P.dma("sp", X0S[128 * j:128 * (j + 1), :], cvo[:], r=["cvo"], w=["X0S"])
                        elif ct < 4:
                            P.op("act", lambda e: e.activation(out=x1c[:], in_=cacc[:], func=AF.Identity, bias=hcb[:, ct:ct + 1]),
                                 r=["hcacc", "hcb"], w=["x1c"])
                        else:
                            P.op("act", lambda e: e.activation(out=cvo[:], in_=cacc[:], func=AF.Identity, bias=hcb[:, ct:ct + 1]),
                                 r=["hcacc", "hcb"], w=["cvo"])
                            P.op("dve", lambda e: e.tensor_tensor(out=cvo[:], in0=cvo[:], in1=x1c[:], op=ALU.mult),
                                 r=["cvo", "x1c"], w=["cvo"])
                            P.op("act", lambda e: e.copy(out=vxb[:], in_=cvo[:]), r=["cvo"], w=["vxb"])
                            P.dma("sp", VXS[128 * j:128 * (j + 1), :], cvo[:], r=["cvo"], w=["VXS"])
                            to_token_major(vxb, "vxb", [(DC, "DC", 128 * j), (DS, "DS", 128 * j)])

                P.dma("act", DC[:, :, 256:512], FSD.rearrange("(t p) c -> p t c", p=128), r=["FSD"], w=["DC"])
                P.dma("act", DS[:, :, 256:512], FDD.rearrange("(t p) c -> p t c", p=128), r=["FDD"], w=["DS"])

                right_stack = contextlib.ExitStack()
                wo = right_stack.enter_context(nc.sbuf_tensor("wo", [128, NCH, D], BF16, side="right"))
                w_out_v = w_out.rearrange("(c p) n -> p c n", p=128)
                with Scope():
                    tcs = [sb("tcs0", [128, KT, 128], BF16), sb("tcs1", [128, KT, 128], BF16)]
                    tsn = [sb("tsn0", [128, KT, 128], BF16), sb("tsn1", [128, KT, 128], BF16)]
                    pcs = pf
                    psn = [ps("psn0", [128, 512]), ps("psn1", [128, 512])]
                    ev = [sb("evA", [128, 256]), sb("evKr", [128, 256]), sb("evB", [128, 256]), sb("evKi", [128, 256])]
                    tq = [sb("tq1", [128, 256]), sb("tq2", [128, 256]), sb("tq3", [128, 256]), sb("tq4", [128, 256])]
                    nkt = KT if stop != "pB2s" else 2
                    for kt in range(KT):
                        k = kt % 2
                        if 3 <= kt < 11:
                            for c in (2 * (kt - 3), 2 * (kt - 3) + 1):
                                P.dma("pool", wo[:, c, :], w_out_v[:, c, :], w=["wo"])
                        if 12 <= kt < 16 and stop not in ("pB2", "pB2f"):
                            tq_ = kt - 12
                            P.coll("AllGather", GRP, YS[1024 * tq_:1024 * (tq_ + 1), :], GS[4096 * tq_:4096 * (tq_ + 1), :],
                                   r=["YS"], w=["GS"], sk="gs")
                        P.dma("sp", tcs[k][:], t2cos[kt], w=["tcs%d" % k])
                        P.dma("sp", tsn[k][:], t2sin[kt], w=["tsn%d" % k])
                        for tt in range(32):
                            P.op("pe", lambda e: e.matmul(pcs[k][:], lhsT=tcs[k][:, tt, :], rhs=DC[:, tt, :], start=(tt == 0), stop=(tt == 31)),
                                 r=["tcs%d" % k, "DC"], w=["pf%d" % k])
                        for tt in range(32):
                            P.op("pe", lambda e: e.matmul(psn[k][:], lhsT=tsn[k][:, tt, :], rhs=DS[:, tt, :], start=(tt == 0), stop=(tt == 31)),
                                 r=["tsn%d" % k, "DS"], w=["psn%d" % k])
                        P.op("act", lambda e: e.activation(out=ev[0][:], in_=pcs[k][:, 0:256], func=AF.Identity, scale=wkc[:, kt:kt + 1]),
                             r=["pf%d" % k, "wkc"], w=["evA"])
                        P.op("act", lambda e: e.copy(out=ev[1][:], in_=pcs[k][:, 256:512]), r=["pf%d" % k], w=["evKr"])
                        P.op("dve", lambda e: e.tensor_scalar(out=ev[2][:], in0=psn[k][:, 0:256], scalar1=wkc[:, kt:kt + 1], scalar2=None,
                                                              op0=ALU.mult), r=["psn%d" % k, "wkc"], w=["evB"])
                        P.op("dve", lambda e: e.tensor_copy(out=ev[3][:], in_=psn[k][:, 256:512]), r=["psn%d" % k], w=["evKi"])
                        P.op("pool", lambda e: e.tensor_tensor(out=tq[0][:], in0=ev[0][:], in1=ev[1][:], op=ALU.mult), r=["evA", "evKr"], w=["tq1"])
                        P.op("pool", lambda e: e.tensor_tensor(out=tq[1][:], in0=ev[2][:], in1=ev[3][:], op=ALU.mult), r=["evB", "evKi"], w=["tq2"])
                        P.op("pool", lambda e: e.tensor_tensor(out=Ybuf[:, kt, 0, :], in0=tq[0][:], in1=tq[1][:], op=ALU.add),
                             r=["tq1", "tq2"], w=["Ybuf"])
                        P.op("dve", lambda e: e.tensor_tensor(out=tq[2][:], in0=ev[0][:], in1=ev[3][:], op=ALU.mult), r=["evA", "evKi"], w=["tq3"])
                        P.op("dve", lambda e: e.tensor_tensor(out=tq[3][:], in0=ev[2][:], in1=ev[1][:], op=ALU.mult), r=["evB", "evKr"], w=["tq4"])
                        P.op("dve", lambda e: e.tensor_tensor(out=Ybuf[:, kt, 1, :], in0=tq[3][:], in1=tq[2][:], op=ALU.subtract),
                             r=["tq3", "tq4"], w=["Ybuf"])
            with Scope():
                tic = [sb("tic0", [128, 2, KT, 128], BF16), sb("tic1", [128, 2, KT, 128], BF16)]
                tis = [sb("tis0", [128, 2, KT, 128], BF16), sb("tis1", [128, 2, KT, 128], BF16)]
                x0b = [sb("x0b0", [128, 2, 256]), sb("x0b1", [128, 2, 256])]
                vxs = [sb("vxs0", [128, 2, 256]), sb("vxs1", [128, 2, 256])]
                otmp = sb("otmp", [128, 256])
                YHo = sb("YHo", [128, 2, L], BF16)
                po = [ps("po0", [128, 512]), ps("po1", [128, 512]), ps("po2", [128, 512]), ps("po3", [128, 512])]
                for g in range(16):
                    k = g % 2
                    for i2 in range(2):
                        P.dma("sp", tic[k][:, i2], t2cos[2 * g + i2], w=["tic%d" % k])
                        P.dma("sp", tis[k][:, i2], t2sin[2 * g + i2], w=["tis%d" % k])
                    P.dma("sp", x0b[k][:], X0S[:, 256 * g:256 * (g + 1)].rearrange("(j p) t -> p j t", p=128), r=["X0S"], w=["x0b%d" % k])
                    P.dma("sp", vxs[k][:], VXS[:, 256 * g:256 * (g + 1)].rearrange("(j p) t -> p j t", p=128), r=["VXS"], w=["vxs%d" % k])
                    for j in range(2):
                        pq = po[2 * k + j]
                        pk = "po%d" % (2 * k + j)
                        for kt in range(KT):
                            P.op("pe", lambda e: e.matmul(pq[:, 0:256], lhsT=Ybuf[:, kt, 0, 128 * j:128 * (j + 1)], rhs=tic[k][:, :, kt, :],
                                                          start=(kt == 0), stop=False), r=["Ybuf", "tic%d" % k], w=[pk])
                            P.op("pe", lambda e: e.matmul(pq[:, 0:256], lhsT=Ybuf[:, kt, 1, 128 * j:128 * (j + 1)], rhs=tis[k][:, :, kt, :],
                                                          start=False, stop=(kt == KT - 1)), r=["Ybuf", "tis%d" % k], w=[pk])
                        P.op("dve", lambda e: e.scalar_tensor_tensor(out=otmp[:], in0=vxs[k][:, j, :], scalar=hbq[:, j:j + 1],
                                                                     in1=pq[:, 0:256], op0=ALU.mult, op1=ALU.add),
                             r=["vxs%d" % k, "hbq", pk], w=["otmp"])
                        P.op("pool", lambda e: e.tensor_tensor(out=YHo[:, j, 256 * g:256 * (g + 1)], in0=otmp[:], in1=x0b[k][:, j, :],
                                                               op=ALU.mult), r=["otmp", "x0b%d" % k], w=["YHo"])
                    if g in (7, 15) and stop != "pB2":
                        c2 = g // 8
                        for tq_ in (2 * c2, 2 * c2 + 1):
                            P.dma("sp", YH[256 * tq_:256 * (tq_ + 1), :].rearrange("(j p) t -> p j t", p=128),
                                  YHo[:, :, 1024 * tq_:1024 * (tq_ + 1)], r=["YHo"], w=["YH%d" % c2])
                        P.coll("AllGather", GRP, YH[512 * c2:512 * (c2 + 1), :], GH[2048 * c2:2048 * (c2 + 1), :],
                               r=["YH%d" % c2], w=["GH"], sk="gh")
                if stop == "pB2":
                    yf = sb("yf", [128, 2, L])
                    P.op("dve", lambda e: e.tensor_copy(out=yf[:], in_=YHo[:]), r=["YHo"], w=["yf"])
                    d1 = dbg_out("yhy", [256, L])
                    P.dma("sp", d1.rearrange("(j p) t -> p j t", p=128), yf[:], r=["yf"], w=["dbgo"])
                    P.final_wait("sp", ["dbgo"])
        if stop in ("pB2", "pB2f"):
            return nc, dbg, in_names

        gidx = sb("gidx_sb", [128, 56], I32)
        P.dma("sp", gidx[:], gidx_in[:, :], w=["gidx"])

        def make_norm(tag, nslot=1):
            xts_ = [sb(tag + "xt%d" % j_, [128, D]) for j_ in range(nslot)]
            posr_ = sb(tag + "posr", [128, 1024]) if nslot > 1 else None
            xt_ = xts_[0]; junk_ = sb(tag + "junk", [128, D], BF16); ss_ = sb(tag + "ss", [128, 1])
            rs_ = sb(tag + "rs", [128, 1]); xn_ = sb(tag + "xn", [128, D], BF16); mt_ = sb(tag + "mt", [128, NCH, 128])
            tp_ = ps(tag + "tp", [128, NCH, 128], BF16)

            def fn(src_dram, Gm, Sm, hT_dst, hkey, gcol=None, slot=0):
                xt_ = xts_[slot]
                xkey = tag + "xt" + ("%d" % slot if nslot > 1 else "")
                if gcol is not None:
                    P.gather(xt_[:], src_dram, gidx[:, gcol:gcol + 1], r=["XP", "gidx"], w=[xkey])
                    P.gather(posr_[:], POS[:, :], gidx[:, 48 + gcol:49 + gcol], r=["POS", "gidx"], w=[tag + "posr"])
                    P.op("pool", lambda e: e.tensor_tensor(out=xt_[:, 0:1024], in0=xt_[:, 0:1024], in1=posr_[:], op=ALU.add),
                         r=[tag + "posr"], w=[xkey])
                    P.op("dve", lambda e: e.tensor_tensor(out=xt_[:, 1024:D], in0=xt_[:, 1024:D], in1=poscol[:], op=ALU.add),
                         r=["poscol"], w=[xkey])
                else:
                    P.dma("sp", xt_[:], src_dram, r=["XP", "X1S"], w=[xkey])
                P.op("dve", lambda e: e.memset(ss_[:], 0.0), w=[tag + "ss"])
                P.op("act", lambda e: e.activation(out=junk_[:], in_=xt_[:], func=AF.Square, accum_out=ss_[:]),
                     r=[xkey], w=[tag + "ss", tag + "junk"])
                P.op("act", lambda e: e.activation(out=rs_[:], in_=ss_[:], func=AF.Sqrt, scale=1.0 / D, bias=epsc[:, 0:1]),
                     r=[tag + "ss", "epsc"], w=[tag + "rs"])
                P.op("dve", lambda e: e.reciprocal(out=rs_[:], in_=rs_[:]), r=[tag + "rs"], w=[tag + "rs"])
                P.op("dve", lambda e: e.tensor_scalar(out=xn_[:], in0=xt_[:], scalar1=rs_[:, 0:1], scalar2=None, op0=ALU.mult),
                     r=[xkey, tag + "rs"], w=[tag + "xn"])
                xnv = xn_[:].rearrange("p (j c) -> p c j", c=NCH)
                for c in range(NCH):
                    P.op("pe", lambda e: e.transpose(out=tp_[:, c, :], in_=xnv[:, c, :], identity=identb[:]),
                         r=[tag + "xn", "identb"], w=[tag + "tp"])
                P.op("dve", lambda e: e.tensor_tensor(out=mt_[:], in0=tp_[:], in1=bc_last(Gm[:], 128), op=ALU.mult),
                     r=[tag + "tp", Gm.name], w=[tag + "mt"])
                P.op("pool", lambda e: e.tensor_tensor(out=hT_dst, in0=mt_[:], in1=bc_last(Sm[:], 128), op=ALU.add),
                     r=[tag + "mt", Sm.name], w=[hkey])
            return fn, (xts_ if nslot > 1 else xt_), junk_

        with Scope():
            gp1 = sb("gp1", [128, D]); gtmp = sb("gtmp", [128, D])
            mod_bc(gp1, 0, 2)
            P.dma("sp", gtmp[:], nrm_post1.partition_broadcast(128), w=["gtmp"])
            P.op("pool", lambda e: e.tensor_tensor(out=gp1[:], in0=gp1[:], in1=gtmp[:], op=ALU.mult), r=["gp1", "gtmp"], w=["gp1"])
            snrm = sb("snrm", [128, 1024])
            P.dma("sp", snrm[:], ssd_nrm.partition_broadcast(128), w=["snrm"])
            wz = sb("wz", [128, NCH, 1024], BF16)
            w_z_v = w_z.rearrange("(p c) n -> p c n", c=NCH)
            for j in range(4):
                P.dma("pool", wz[:, :, 256 * j:256 * (j + 1)], w_z_v[:, :, 256 * j:256 * (j + 1)], w=["wz"])
            yhT = sb("yhT", [128, 8, 1024], BF16)
            for src_ in range(4):
                for jj in range(2):
                    P.gather(yhT[:, 2 * src_ + jj, :], GH[:, :], gidx[:, 40 + 2 * src_ + jj:41 + 2 * src_ + jj],
                             r=["GH", "gidx"], w=["yhT"])
            normC, xtC, junkC = make_norm("c1", nslot=2)
            hTc = sb("hTc", [128, NCH, 128], BF16)
            zs = sb("zs", [128, 1024]); ysb = [sb("ysb%d" % j_, [128, 256]) for j_ in range(4)]; gg = sb("gg", [128, 1024]); gnb = sb("gnb", [128, 1024], BF16)
            ymT = [sb("ymT0", [128, 8, 128], BF16), sb("ymT1", [128, 8, 128], BF16)]
            ss2 = sb("ss2", [128, 4]); rs2 = sb("rs2", [128, 1]); rsA = sb("rsA", [128, 1])
            x1t = sb("x1t", [128, D]); junk2 = sb("junk2", [128, 1024], BF16)
            pz = [ps("pz0", [128, 512]), ps("pz1", [128, 512])]
            pym = ps("pym", [128, 8, 128], BF16)
            pwo = [ps("pwo0", [128, 512]), ps("pwo1", [128, 512])]

            def stageA(i):
                sl_ = i % 2
                normC(x_b[:, :], G1, sh1, hTc[:], "hTc", gcol=i, slot=sl_)
                yield
                for nb in range(2):
                    for c in range(NCH):
                        P.op("pe", lambda e: e.matmul(pz[nb][:], lhsT=hTc[:, c, :], rhs=wz[:, c, 512 * nb:512 * (nb + 1)],
                                                      start=(c == 0), stop=(c == NCH - 1)), r=["hTc", "wz"], w=["pz%d" % nb])
                    P.op("act", lambda e: e.activation(out=zs[:, 512 * nb:512 * (nb + 1)], in_=pz[nb][:], func=AF.Silu),
                         r=["pz%d" % nb], w=["zs"])
                for src_ in range(4):
                    P.gather(ysb[src_][:], GS[:, :], gidx[:, 8 + 4 * i + src_:9 + 4 * i + src_],
                             r=["GS", "gidx"], w=["ysb%d" % src_])
                    P.op("dve", lambda e: e.tensor_tensor(out=gg[:, 256 * src_:256 * (src_ + 1)], in0=ysb[src_][:],
                                                          in1=zs[:, 256 * src_:256 * (src_ + 1)], op=ALU.mult),
                         r=["ysb%d" % src_, "zs"], w=["gg"])
                P.op("dve", lambda e: e.memset(rsA[:], 0.0), w=["rsA"])
                P.op("act", lambda e: e.activation(out=junkC[:, 0:1024], in_=gg[:], func=AF.Square, accum_out=rsA[:]),
                     r=["gg"], w=["rsA", "c1junk"])
                P.op("act", lambda e: e.activation(out=rsA[:], in_=rsA[:], func=AF.Sqrt, scale=1.0 / 1024, bias=epsc[:, 0:1]),
                     r=["rsA", "epsc"], w=["rsA"])
                P.op("dve", lambda e: e.reciprocal(out=rsA[:], in_=rsA[:]), r=["rsA"], w=["rsA"])
                P.op("dve", lambda e: e.scalar_tensor_tensor(out=gnb[:], in0=gg[:], scalar=rsA[:, 0:1], in1=snrm[:],
                                                             op0=ALU.mult, op1=ALU.mult), r=["gg", "rsA", "snrm"], w=["gnb"])
                yield
                for c in range(8):
                    P.op("pe", lambda e: e.transpose(out=pym[:, c, :], in_=gnb[:, 128 * c:128 * (c + 1)], identity=identb[:]),
                         r=["gnb", "identb"], w=["pym"])
                P.op("act", lambda e: e.copy(out=ymT[sl_][:], in_=pym[:]), r=["pym"], w=["ymT%d" % sl_])
                yield

            def stageB(i):
                sl_ = i % 2
                P.op("dve", lambda e: e.memset(ss2[:], 0.0), w=["ss2"])
                for nb in range(4):
                    k = nb % 2
                    for c in range(NCH):
                        lhs = ymT[sl_][:, c, :] if c < 8 else yhT[:, c - 8, 128 * i:128 * (i + 1)]
                        P.op("pe", lambda e: e.matmul(pwo[k][:], lhsT=lhs, rhs=wo[:, c, 512 * nb:512 * (nb + 1)],
                                                      start=(c == 0), stop=(c == NCH - 1)), r=["ymT%d" % sl_, "yhT", "wo"], w=["pwo%d" % k])
                    P.op("act", lambda e: e.activation(out=junk2[:, 0:512], in_=pwo[k][:], func=AF.Square, accum_out=ss2[:, nb:nb + 1]),
                         r=["pwo%d" % k], w=["ss2", "junk2"])
                    P.op("dve", lambda e: e.tensor_tensor(out=x1t[:, 512 * nb:512 * (nb + 1)], in0=pwo[k][:],
                                                          in1=gp1[:, 512 * nb:512 * (nb + 1)], op=ALU.mult),
                         r=["pwo%d" % k, "gp1", "junk2"], w=["x1t"])
                    yield
                P.op("dve", lambda e: e.tensor_reduce(out=rs2[:], in_=ss2[:], axis=AX.X, op=ALU.add), r=["ss2"], w=["rs2"])
                P.op("act", lambda e: e.activation(out=rs2[:], in_=rs2[:], func=AF.Sqrt, scale=1.0 / D, bias=epsc[:, 0:1]),
                     r=["rs2", "epsc"], w=["rs2"])
                P.op("dve", lambda e: e.reciprocal(out=rs2[:], in_=rs2[:]), r=["rs2"], w=["rs2"])
                P.op("dve", lambda e: e.scalar_tensor_tensor(out=x1t[:], in0=x1t[:], scalar=rs2[:, 0:1], in1=xtC[sl_][:],
                                                             op0=ALU.mult, op1=ALU.add), r=["x1t", "rs2", "c1xt%d" % sl_], w=["x1t"])
                P.dma("sp", X1S[128 * i:128 * (i + 1), :], x1t[:], r=["x1t"], w=["X1S"])

            for _ in stageA(0):
                pass
            for i in range(8):
                ga = stageA(i + 1) if i + 1 < 8 else iter(())
                gb = stageB(i)
                next(ga, None)
                next(gb, None)
                next(ga, None)
                next(gb, None)
                next(gb, None)
                next(ga, None)
                for _ in gb:
                    pass
                for _ in ga:
                    pass
            if stop == "pC1":
                d1 = dbg_out("x1", [1024, D])
                P.dma("sp", d1, X1S[:, :], r=["X1S"], w=["dbgo"])
                P.final_wait("sp", ["dbgo"])
        if stop == "pC1":
            return nc, dbg, in_names
        right_stack.close()

        with Scope():
            gp2 = sb("gp2", [128, D])
            with Scope():
                gtmp2 = sb("gtmp2", [128, D])
                mod_bc(gp2, 0, 5)
                P.dma("sp", gtmp2[:], nrm_post2.partition_broadcast(128), w=["gtmp2"])
                P.op("pool", lambda e: e.tensor_tensor(out=gp2[:], in0=gp2[:], in1=gtmp2[:], op=ALU.mult), r=["gp2", "gtmp2"], w=["gp2"])
            normF, xtF, junkF = make_norm("c2")
            h2T = sb("h2T", [128, NCH, 512], BF16)
            AT = sb("AT", [128, NFF, 512], BF16)
            wg = [sb("wg0", [128, NCH, 256], BF16), sb("wg1", [128, NCH, 256], BF16)]
            wu = [sb("wu0", [128, NCH, 256], BF16), sb("wu1", [128, NCH, 256], BF16)]
            wd = [sb("wd0", [128, NFF, 256], BF16), sb("wd1", [128, NFF, 256], BF16)]
            fo = sb("fo", [128, 4, D]); act_ = sb("act_", [128, 512])
            ss3 = sb("ss3", [128, 1]); rs3 = sb("rs3", [128, 1])
            pg = [ps("pg0", [128, 512]), ps("pg1", [128, 512])]
            pu = [ps("pu0", [128, 512]), ps("pu1", [128, 512])]
            pdn = pg
            w_gate_v = w_gate.rearrange("(p c) n -> p c n", c=NCH)
            w_up_v = w_up.rearrange("(p c) n -> p c n", c=NCH)
            w_down_v = w_down.rearrange("(c p) n -> p c n", p=128)
            for half in range(2):
                if half == 0:
                    for t in range(4):
                        normF(X1S[128 * t:128 * (t + 1), :], G2, sh2, h2T[:, :, 128 * t:128 * (t + 1)], "h2T")
                for gi in range(NFF // 2):
                    k = gi % 2
                    P.dma("pool", wg[k][:], w_gate_v[:, :, 256 * gi:256 * (gi + 1)], w=["wg%d" % k])
                    P.dma("pool", wu[k][:], w_up_v[:, :, 256 * gi:256 * (gi + 1)], w=["wu%d" % k])
                    for sub in range(2):
                        fc = 2 * gi + sub
                        pk = fc % 2
                        for c in range(NCH):
                            P.op("pe", lambda e: e.matmul(pg[pk][:], lhsT=wg[k][:, c, 128 * sub:128 * (sub + 1)], rhs=h2T[:, c, :],
                                                          start=(c == 0), stop=(c == NCH - 1)), r=["wg%d" % k, "h2T"], w=["pg%d" % pk])
                        for c in range(NCH):
                            P.op("pe", lambda e: e.matmul(pu[pk][:], lhsT=wu[k][:, c, 128 * sub:128 * (sub + 1)], rhs=h2T[:, c, :],
                                                          start=(c == 0), stop=(c == NCH - 1)), r=["wu%d" % k, "h2T"], w=["pu%d" % pk])
                        P.op("act", lambda e: e.activation(out=act_[:], in_=pg[pk][:], func=AF.Silu), r=["pg%d" % pk], w=["act_"])
                        P.op("dve", lambda e: e.tensor_tensor(out=AT[:, fc, :], in0=act_[:], in1=pu[pk][:], op=ALU.mult),
                             r=["act_", "pu%d" % pk], w=["AT"])
                for nbh in range(8):
                    k = nbh % 2
                    P.dma("pool", wd[k][:], w_down_v[:, :, 256 * nbh:256 * (nbh + 1)], w=["wd%d" % k])
                    if half == 0 and nbh % 2 == 1:
                        t_ = nbh // 2
                        normF(X1S[128 * (4 + t_):128 * (5 + t_), :], G2, sh2, h2T[:, :, 128 * t_:128 * (t_ + 1)], "h2T")
                    for t in range(4):
                        pk = (nbh * 4 + t) % 2
                        for fc in range(NFF):
                            P.op("pe", lambda e: e.matmul(pdn[pk][:, 0:256], lhsT=AT[:, fc, 128 * t:128 * (t + 1)], rhs=wd[k][:, fc, :],
                                                          start=(fc == 0), stop=(fc == NFF - 1)), r=["AT", "wd%d" % k], w=["pg%d" % pk])
                        P.op("act", lambda e: e.copy(out=fo[:, t, 256 * nbh:256 * (nbh + 1)], in_=pdn[pk][:, 0:256]),
                             r=["pg%d" % pk], w=["fo"])
                for t in range(4):
                    i = 4 * half + t
                    P.op("dve", lambda e: e.memset(ss3[:], 0.0), w=["ss3"])
                    P.op("act", lambda e: e.activation(out=junkF[:], in_=fo[:, t, :], func=AF.Square, accum_out=ss3[:]),
                         r=["fo"], w=["ss3", "c2junk"])
                    P.op("act", lambda e: e.activation(out=rs3[:], in_=ss3[:], func=AF.Sqrt, scale=1.0 / D, bias=epsc[:, 0:1]),
                         r=["ss3", "epsc"], w=["rs3"])
                    P.op("dve", lambda e: e.reciprocal(out=rs3[:], in_=rs3[:]), r=["rs3"], w=["rs3"])
                    P.dma("sp", xtF[:], X1S[128 * i:128 * (i + 1), :], r=["X1S"], w=["c2xt"])
                    P.op("pool", lambda e: e.tensor_tensor(out=fo[:, t, :], in0=fo[:, t, :], in1=gp2[:], op=ALU.mult),
                         r=["fo", "gp2"], w=["fo"])
                    P.op("dve", lambda e: e.scalar_tensor_tensor(out=xtF[:], in0=fo[:, t, :], scalar=rs3[:, 0:1], in1=xtF[:],
                                                                 op0=ALU.mult, op1=ALU.add), r=["fo", "rs3", "c2xt"], w=["c2xt"])
                    P.dma("sp", out_own[128 * i:128 * (i + 1), :], xtF[:], r=["c2xt"], w=["OUT"])
            P.final_wait("sp", ["OUT"])

        P.emit()
    return nc, dbg, in_names


CONST = {}


def make_consts():
    if CONST:
        return
    f = np.float32
    a = np.arange(KT * 128, dtype=np.int64)
    prod = (a[:, None] * a[None, :]) % NDFT
    ang = prod.astype(np.float64) * (2.0 * np.pi / NDFT)
    def lay(mat):
        m4 = mat.reshape(KT, 128, KT, 128)
        return np.ascontiguousarray(m4.transpose(2, 1, 0, 3)).astype(ml_dtypes.bfloat16)
    CONST["t2cos"] = lay(np.cos(ang)); CONST["t2sin"] = lay(np.sin(ang))
    CONST["irow"] = np.arange(L, dtype=f)[None, :]
    fb = np.linspace(1e-4, 15.0, 16, dtype=f)
    fbph = np.zeros((33, 2), f)
    fbph[1:17, 0] = fb; fbph[17:33, 0] = fb
    fbph[1:17, 1] = np.pi / 2.0; fbph[17:33, 1] = np.pi
    CONST["fbph"] = fbph
    k = np.arange(KT * 128)
    wk = np.where(k <= 4096, 2.0, 0.0); wk[0] = 1.0; wk[4096] = 1.0
    CONST["wk"] = np.ascontiguousarray((wk / NDFT).astype(f).reshape(KT, 128).T)
    max_decay = math.log(1e-2) / 0.3; min_decay = math.log(1e-2) / 1.5
    CONST["delta"] = np.abs(np.linspace(min_decay, max_decay, 1024, dtype=f))


def host_inputs(inp):
    make_consts()
    f = np.float32
    x = np.asarray(inp["x"], f); c = np.asarray(inp["c"], f); ctx = np.asarray(inp["ctx"], f)
    c_ctx = np.asarray(inp["c_ctx"], f)
    w_ada = np.asarray(inp["w_ada"], f)[0]; b_ada = np.asarray(inp["b_ada"], f)[0]
    maps = []
    ident = np.eye(128, dtype=f)
    jrow = np.arange(512, dtype=f)[None, :]
    kcol = np.arange(128, dtype=f)[:, None]
    for core in range(8):
        b, q = core // 4, core % 4
        cvv = np.stack([c[b], c_ctx], axis=-1).reshape(128, NCH, 2)
        m = {
            "x_b": x[b], "ctx_b": ctx[b], "cv": np.ascontiguousarray(cvv),
            "w_ada_q": np.ascontiguousarray(w_ada[:, 3072 * q:3072 * (q + 1)]),
            "b_ada_q": np.ascontiguousarray(b_ada[None, 3072 * q:3072 * (q + 1)]),
            "nrm_pre1": np.asarray(inp["norm_mix_pre"], f)[0].reshape(128, NCH),
            "nrm_pre2": np.asarray(inp["norm_ffn_pre"], f)[0].reshape(128, NCH),
            "nrm_post1": np.asarray(inp["norm_mix_post"], f)[0][None, :],
            "nrm_post2": np.asarray(inp["norm_ffn_post"], f)[0][None, :],
            "c_ident": ident, "c_jrow": jrow, "c_kcol": kcol,
        }
        g = q // 2
        w_in = np.asarray(inp["w_in"], f)[0]
        cols = np.concatenate([
            1024 + 256 * q + np.arange(256), 2048 + 128 * g + np.arange(128), 2304 + 128 * g + np.arange(128),
            2592 + 256 * q + np.arange(256), 3616 + 256 * q + np.arange(256), 4640 + 256 * q + np.arange(256),
            2560 + 4 * q + np.arange(4), 2576 + 4 * q + np.arange(4)])
        m["w_in_q"] = np.ascontiguousarray(w_in[:, cols])
        scw = np.asarray(inp["ssd_conv_w"], f)[0]; scb = np.asarray(inp["ssd_conv_b"], f)[0]
        ccols = np.concatenate([256 * q + np.arange(256), 1024 + 128 * g + np.arange(128), 1280 + 128 * g + np.arange(128)])
        m["ssd_cw"] = np.ascontiguousarray(scw[:, ccols].T.reshape(4, 128, 3).transpose(1, 0, 2))
        m["ssd_cb"] = np.ascontiguousarray(scb[ccols].reshape(4, 128).T)
        hs = 4 * q + np.arange(4)
        m["ssd_alog"] = np.asarray(inp["ssd_a_log"], f)[0][:, hs].reshape(1, 8)
        m["ssd_dtb"] = np.asarray(inp["ssd_dt_bias"], f)[0][:, hs].reshape(1, 8)
        m["ssd_Dq"] = np.asarray(inp["ssd_d"], f)[0][hs].reshape(1, 4)
        gi = np.zeros((128, 56), np.int32)
        pp = np.arange(128)
        for i_ in range(8):
            gi[:, i_] = q * 1024 + 128 * i_ + pp
            for s_ in range(4):
                gi[:, 8 + 4 * i_ + s_] = q * 4096 + s_ * 1024 + 128 * i_ + pp
        for s_ in range(4):
            for j_ in range(2):
                gi[:, 40 + 2 * s_ + j_] = (q // 2) * 2048 + (q % 2) * 256 + s_ * 512 + j_ * 128 + pp
        for i_ in range(8):
            gi[:, 48 + i_] = 16 * q + 2 * i_ + pp // 64
        m["gidx"] = gi
        m["w_z"] = np.ascontiguousarray(w_in[:, 0:1024])
        m["ssd_nrm"] = np.asarray(inp["ssd_norm"], f)[0][None, :]
        m["w_out"] = np.asarray(inp["w_out"], f)[0]
        m["w_gate"] = np.asarray(inp["w_gate"], f)[0]; m["w_up"] = np.asarray(inp["w_up"], f)[0]
        m["w_down"] = np.asarray(inp["w_down"], f)[0]
        hcw = np.asarray(inp["hy_conv_w"], f)[0]; hcb = np.asarray(inp["hy_conv_b"], f)[0]
        hcols = np.concatenate([1024 * a + 256 * q + np.arange(256) for a in range(3)])
        m["hy_cw"] = np.ascontiguousarray(hcw[:, hcols].T.reshape(6, 128, 3).transpose(1, 0, 2))
        m["hy_cb"] = np.ascontiguousarray(hcb[hcols].reshape(6, 128).T)
        m["hy_w1"] = np.asarray(inp["hy_w1"], f)[0]; m["hy_w2"] = np.asarray(inp["hy_w2"], f)[0]; m["hy_w3"] = np.asarray(inp["hy_w3"], f)[0]
        m["hy_b123"] = np.stack([np.asarray(inp["hy_b1"], f)[0], np.asarray(inp["hy_b2"], f)[0], np.asarray(inp["hy_b3"], f)[0]], axis=1)
        m["hy_fr"] = np.asarray(inp["hy_freq"], f)[0][:, None]
        w4 = np.asarray(inp["hy_w4"], f)[0]
        m["hy_w4q"] = np.ascontiguousarray(np.concatenate([w4[:, 256 * q:256 * (q + 1)], w4[:, 1024 + 256 * q:1024 + 256 * (q + 1)]], axis=1))
        m["hy_biasq"] = np.ascontiguousarray(np.asarray(inp["hy_bias"], f)[0][256 * q:256 * (q + 1)].reshape(2, 128).T)
        m["c_delta"] = np.ascontiguousarray(CONST["delta"][256 * q:256 * (q + 1)].reshape(2, 128).T)
        m["c_irow"] = CONST["irow"]; m["c_fbph"] = CONST["fbph"]; m["c_wk"] = CONST["wk"]
        m["t2cos"] = CONST["t2cos"]; m["t2sin"] = CONST["t2sin"]
        ii = np.arange(128)
        m["c_trif"] = (ii[:, None] <= ii[None, :]).astype(f)
        m["c_trib"] = (ii[:, None] >= ii[None, :]).astype(f)
        m["c_negf"] = np.where(ii[None, :] >= ii[:, None], 0.0, NEG).astype(f)
        m["c_negb"] = np.where(ii[None, :] <= ii[:, None], 0.0, NEG).astype(f)
        maps.append(m)
    return maps


def run(inp, stop="all"):
    nc, dbg, in_names = build(stop)
    maps = [{k: np.ascontiguousarray(m[k]) for k in in_names} for m in host_inputs(inp)]
    res = run_bass_kernel_spmd(nc, maps, core_ids=list(range(8)))
    return res


def kernel(**inputs):
    res = run(inputs, "all")
    out = np.zeros((2, L, D), np.float32)
    for core in range(8):
        b, q = core // 4, core % 4
        out[b, 1024 * q:1024 * (q + 1)] = res.results[core]["out_own"]
    return out
```
